# Optimizing a Trainium2 kernel written in Bass

```python
import math
import jax, jax.numpy as jnp
from jax import lax
import numpy as np

D_MODEL = 1024
BATCH = 16
SEQ = 2048
DEPTH = 2

HEAD_DIM = 64
N_HEADS_DIFF = 4
N_HEADS_FOX = 4
N_HEADS_MOBA = 4
N_BRANCH = 3
DIFF_QK_WIDTH = N_HEADS_DIFF * 2 * HEAD_DIM
DIFF_V_WIDTH = N_HEADS_DIFF * 2 * HEAD_DIM
FOX_WIDTH = N_HEADS_FOX * HEAD_DIM
MOBA_WIDTH = N_HEADS_MOBA * HEAD_DIM
D_IN = 2 * DIFF_QK_WIDTH + DIFF_V_WIDTH + 3 * FOX_WIDTH + 3 * MOBA_WIDTH + N_BRANCH * D_MODEL + N_HEADS_FOX
D_FF = ((8 * D_MODEL + 3 * 256 - 1) // (3 * 256)) * 256
ROPE_THETA = 10000.0
Q_BLOCK = 128
MOBA_BLOCK = 256
MOBA_TOPK = 3
MOBA_Q_CHUNK = 32
RMS_EPS = 1e-6
NEG_INF = -1e30

kernel_name = 'hybrid_diff_fox_moba_block'


def rmsnorm(x, g):
    xf = x.astype(jnp.float32)
    y = xf * lax.rsqrt(jnp.mean(xf * xf, axis=-1, keepdims=True) + RMS_EPS)
    return (y * g.astype(jnp.float32)).astype(x.dtype)


def rope_tables(positions):
    inv_freq = 1.0 / (ROPE_THETA ** (jnp.arange(0, HEAD_DIM, 2, dtype=jnp.float32) / HEAD_DIM))
    ang = positions.astype(jnp.float32)[..., None] * inv_freq
    return jnp.cos(ang)[:, :, None, :], jnp.sin(ang)[:, :, None, :]


def apply_rope(x, cos, sin):
    xf = x.astype(jnp.float32)
    x1, x2 = jnp.split(xf, 2, axis=-1)
    out = jnp.concatenate([x1 * cos - x2 * sin, x2 * cos + x1 * sin], axis=-1)
    return out.astype(x.dtype)


def causal_mask(q_start, n_q, n_k):
    qpos = q_start + jnp.arange(n_q)
    return jnp.arange(n_k)[None, :] <= qpos[:, None]


def diff_attention(q, k, v, lam, lam_init, g_subln):
    B, H, _, S, dh = q.shape
    kf = k.astype(jnp.float32)
    scale = dh ** -0.5

    def one_block(i):
        start = i * Q_BLOCK
        qb = lax.dynamic_slice_in_dim(q, start, Q_BLOCK, axis=3).astype(jnp.float32)
        logits = jnp.einsum('bhmqd,bhmkd->bhmqk', qb, kf) * scale
        logits = jnp.where(causal_mask(start, Q_BLOCK, S), logits, NEG_INF)
        p = jax.nn.softmax(logits, axis=-1)
        w = p[:, :, 0] - lam * p[:, :, 1]
        return jnp.einsum('bhqk,bhkd->bhqd', w.astype(v.dtype), v)

    o = lax.map(one_block, jnp.arange(S // Q_BLOCK))
    o = o.transpose(1, 2, 0, 3, 4).reshape(B, H, S, v.shape[-1])
    return rmsnorm(o, g_subln) * (1.0 - lam_init)


def forgetting_attention(q, k, v, log_f):
    B, H, S, dh = q.shape
    kf = k.astype(jnp.float32)
    cum = jnp.cumsum(log_f, axis=-1)
    scale = dh ** -0.5

    def one_block(i):
        start = i * Q_BLOCK
        qb = lax.dynamic_slice_in_dim(q, start, Q_BLOCK, axis=2).astype(jnp.float32)
        cq = lax.dynamic_slice_in_dim(cum, start, Q_BLOCK, axis=2)
        logits = jnp.einsum('bhqd,bhkd->bhqk', qb, kf) * scale
        logits = logits + cq[..., None] - cum[:, :, None, :]
        logits = jnp.where(causal_mask(start, Q_BLOCK, S), logits, NEG_INF)
        p = jax.nn.softmax(logits, axis=-1)
        return jnp.einsum('bhqk,bhkd->bhqd', p.astype(v.dtype), v)

    o = lax.map(one_block, jnp.arange(S // Q_BLOCK))
    return o.transpose(1, 2, 0, 3, 4).reshape(B, H, S, dh)


def moba_attention(q, k, v):
    B, H, S, dh = q.shape
    nb = -(-S // MOBA_BLOCK)
    pad = nb * MOBA_BLOCK - S
    kb = jnp.pad(k, ((0, 0), (0, 0), (0, pad), (0, 0))).reshape(B, H, nb, MOBA_BLOCK, dh)
    vb = jnp.pad(v, ((0, 0), (0, 0), (0, pad), (0, 0))).reshape(B, H, nb, MOBA_BLOCK, dh)
    kmean = jnp.mean(kb.astype(jnp.float32), axis=3)
    ksel = min(MOBA_TOPK, nb - 1)
    scale = dh ** -0.5
    b_idx = jnp.arange(B)[:, None, None, None]
    h_idx = jnp.arange(H)[None, :, None, None]

    def one_chunk(i):
        start = i * MOBA_Q_CHUNK
        qf = lax.dynamic_slice_in_dim(q, start, MOBA_Q_CHUNK, axis=2).astype(jnp.float32)
        qpos = start + jnp.arange(MOBA_Q_CHUNK)
        own = start // MOBA_BLOCK
        k_own = lax.dynamic_index_in_dim(kb, own, axis=2, keepdims=False)
        v_own = lax.dynamic_index_in_dim(vb, own, axis=2, keepdims=False)
        kpos_own = own * MOBA_BLOCK + jnp.arange(MOBA_BLOCK)
        lo = jnp.einsum('bhqd,bhkd->bhqk', qf, k_own.astype(jnp.float32)) * scale
        lo = jnp.where(kpos_own[None, :] <= qpos[:, None], lo, NEG_INF)
        if ksel == 0:
            p = jax.nn.softmax(lo, axis=-1)
            return jnp.einsum('bhqk,bhkd->bhqd', p.astype(v.dtype), v_own)
        gate = jnp.einsum('bhqd,bhnd->bhqn', qf, kmean)
        gate = jnp.where(jnp.arange(nb) < own, gate, NEG_INF)
        _, top_i = lax.top_k(gate, ksel)
        valid = top_i < own
        kg = kb[b_idx, h_idx, top_i]
        vg = vb[b_idx, h_idx, top_i]
        ls = jnp.einsum('bhqd,bhqnkd->bhqnk', qf, kg.astype(jnp.float32)) * scale
        ls = jnp.where(valid[..., None], ls, NEG_INF).reshape(B, H, MOBA_Q_CHUNK, ksel * MOBA_BLOCK)
        p = jax.nn.softmax(jnp.concatenate([ls, lo], axis=-1), axis=-1)
        ps = p[..., :ksel * MOBA_BLOCK].reshape(B, H, MOBA_Q_CHUNK, ksel, MOBA_BLOCK)
        po = p[..., ksel * MOBA_BLOCK:]
        return (jnp.einsum('bhqnk,bhqnkd->bhqd', ps.astype(v.dtype), vg)
                + jnp.einsum('bhqk,bhkd->bhqd', po.astype(v.dtype), v_own))

    o = lax.map(one_chunk, jnp.arange(S // MOBA_Q_CHUNK))
    return o.transpose(1, 2, 0, 3, 4).reshape(B, H, S, dh)


def hybrid_mixer(h, cos, sin, w_in, b_fgt, lam_q1, lam_k1, lam_q2, lam_k2, g_subln,
                 w_br_a, w_br_b, w_br_c, w_out, lam_init):
    B, S, _ = h.shape
    proj = jnp.einsum('bsd,de->bse', h, w_in)
    sizes = [DIFF_QK_WIDTH, DIFF_QK_WIDTH, DIFF_V_WIDTH,
             FOX_WIDTH, FOX_WIDTH, FOX_WIDTH,
             MOBA_WIDTH, MOBA_WIDTH, MOBA_WIDTH,
             N_BRANCH * D_MODEL, N_HEADS_FOX]
    idx = np.cumsum(sizes)[:-1].tolist()
    qa, ka, va, qb, kb, vb, qc, kc, vc, gate_logits, fgt_logits = jnp.split(proj, idx, axis=-1)

    def diff_qk(t):
        t = apply_rope(t.reshape(B, S, 2 * N_HEADS_DIFF, HEAD_DIM), cos, sin)
        return t.reshape(B, S, N_HEADS_DIFF, 2, HEAD_DIM).transpose(0, 2, 3, 1, 4)
    lam = (jnp.exp(jnp.sum(lam_q1.astype(jnp.float32) * lam_k1.astype(jnp.float32)))
           - jnp.exp(jnp.sum(lam_q2.astype(jnp.float32) * lam_k2.astype(jnp.float32))) + lam_init)
    va_h = va.reshape(B, S, N_HEADS_DIFF, 2 * HEAD_DIM).transpose(0, 2, 1, 3)
    oa = diff_attention(diff_qk(qa), diff_qk(ka), va_h, lam, lam_init, g_subln)
    ya = jnp.einsum('bse,ed->bsd', oa.transpose(0, 2, 1, 3).reshape(B, S, DIFF_V_WIDTH), w_br_a)

    def heads(t, n):
        return t.reshape(B, S, n, HEAD_DIM).transpose(0, 2, 1, 3)
    log_f = jax.nn.log_sigmoid(fgt_logits.astype(jnp.float32) + b_fgt.astype(jnp.float32))
    ob = forgetting_attention(heads(qb, N_HEADS_FOX), heads(kb, N_HEADS_FOX),
                              heads(vb, N_HEADS_FOX), log_f.transpose(0, 2, 1))
    yb = jnp.einsum('bse,ed->bsd', ob.transpose(0, 2, 1, 3).reshape(B, S, FOX_WIDTH), w_br_b)

    qc_r = apply_rope(qc.reshape(B, S, N_HEADS_MOBA, HEAD_DIM), cos, sin).transpose(0, 2, 1, 3)
    kc_r = apply_rope(kc.reshape(B, S, N_HEADS_MOBA, HEAD_DIM), cos, sin).transpose(0, 2, 1, 3)
    oc = moba_attention(qc_r, kc_r, heads(vc, N_HEADS_MOBA))
    yc = jnp.einsum('bse,ed->bsd', oc.transpose(0, 2, 1, 3).reshape(B, S, MOBA_WIDTH), w_br_c)

    g = jax.nn.sigmoid(gate_logits.astype(jnp.float32)).astype(h.dtype).reshape(B, S, N_BRANCH, D_MODEL)
    merged = g[:, :, 0] * ya + g[:, :, 1] * yb + g[:, :, 2] * yc
    return jnp.einsum('bsd,de->bse', merged, w_out)


def swiglu(h, w_gate_up, w_down):
    u = jnp.einsum('bsd,df->bsf', h, w_gate_up)
    a, b = jnp.split(u, 2, axis=-1)
    return jnp.einsum('bsf,fd->bsd', jax.nn.silu(a) * b, w_down)


def setup_inputs(seed: int = 0) -> dict:
    key = jax.random.key(seed)
    ks = jax.random.split(key, 24)

    def nrm(k, shape, scale):
        return jax.random.normal(k, shape, jnp.float32) * scale

    def gain(k, n):
        return 1.0 + nrm(k, (DEPTH, n), 0.05)

    return {
        'x': nrm(ks[0], (BATCH, SEQ, D_MODEL), 1.0),
        'c': nrm(ks[1], (BATCH, D_MODEL), 1.0),
        'positions': jnp.broadcast_to(jnp.arange(SEQ, dtype=jnp.int32)[None, :], (BATCH, SEQ)),
        'w_ada': nrm(ks[2], (DEPTH, D_MODEL, 6 * D_MODEL), 0.5 * D_MODEL ** -0.5),
        'b_ada': nrm(ks[3], (DEPTH, 6 * D_MODEL), 0.02),
        'g_pre_mix': gain(ks[4], D_MODEL),
        'g_post_mix': gain(ks[5], D_MODEL),
        'w_in': nrm(ks[6], (DEPTH, D_MODEL, D_IN), D_MODEL ** -0.5),
        'b_fgt': nrm(ks[7], (DEPTH, N_HEADS_FOX), 0.1),
        'lam_q1': nrm(ks[8], (DEPTH, HEAD_DIM), 0.1),
        'lam_k1': nrm(ks[9], (DEPTH, HEAD_DIM), 0.1),
        'lam_q2': nrm(ks[10], (DEPTH, HEAD_DIM), 0.1),
        'lam_k2': nrm(ks[11], (DEPTH, HEAD_DIM), 0.1),
        'g_subln': gain(ks[12], 2 * HEAD_DIM),
        'w_br_a': nrm(ks[13], (DEPTH, DIFF_V_WIDTH, D_MODEL), DIFF_V_WIDTH ** -0.5),
        'w_br_b': nrm(ks[14], (DEPTH, FOX_WIDTH, D_MODEL), FOX_WIDTH ** -0.5),
        'w_br_c': nrm(ks[15], (DEPTH, MOBA_WIDTH, D_MODEL), MOBA_WIDTH ** -0.5),
        'w_out': nrm(ks[16], (DEPTH, D_MODEL, D_MODEL), D_MODEL ** -0.5),
        'g_pre_ffn': gain(ks[17], D_MODEL),
        'g_post_ffn': gain(ks[18], D_MODEL),
        'w_gate_up': nrm(ks[19], (DEPTH, D_MODEL, 2 * D_FF), D_MODEL ** -0.5),
        'w_down': nrm(ks[20], (DEPTH, D_FF, D_MODEL), D_FF ** -0.5),
    }


def reference(x, c, positions, w_ada, b_ada, g_pre_mix, g_post_mix, w_in, b_fgt,
              lam_q1, lam_k1, lam_q2, lam_k2, g_subln, w_br_a, w_br_b, w_br_c, w_out,
              g_pre_ffn, g_post_ffn, w_gate_up, w_down):
    cos, sin = rope_tables(positions)
    c_act = jax.nn.silu(c)
    for l in range(DEPTH):
        lam_init = 0.8 - 0.6 * math.exp(-0.3 * l)
        mod = jnp.einsum('bd,de->be', c_act, w_ada[l]) + b_ada[l]
        sh_m, sc_m, gt_m, sh_f, sc_f, gt_f = [m[:, None, :] for m in jnp.split(mod, 6, axis=-1)]

        h = rmsnorm(x, g_pre_mix[l]) * (1.0 + sc_m) + sh_m
        y = hybrid_mixer(h, cos, sin, w_in[l], b_fgt[l], lam_q1[l], lam_k1[l], lam_q2[l], lam_k2[l],
                         g_subln[l], w_br_a[l], w_br_b[l], w_br_c[l], w_out[l], lam_init)
        x = x + gt_m * rmsnorm(y, g_post_mix[l])

        h = rmsnorm(x, g_pre_ffn[l]) * (1.0 + sc_f) + sh_f
        x = x + gt_f * rmsnorm(swiglu(h, w_gate_up[l], w_down[l]), g_post_ffn[l])
    return x
```

```python
import math
from contextlib import ExitStack
import numpy as np
import concourse.bass as bass
import concourse.mybir as mybir
from concourse.bass_utils import run_bass_kernel_spmd

F32 = mybir.dt.float32
BF16 = mybir.dt.bfloat16
I32 = mybir.dt.int32
AF = mybir.ActivationFunctionType
ALU = mybir.AluOpType
AX = mybir.AxisListType

ENGS = ("pe", "act", "dve", "pool", "sp")
N_DMA_SEMS = 24

D = 1024
S = 2048
NT = 16
DFF = 2816
NJ = 22
EPS = 1e-6
NEGBIG = -1920.0
PI = math.pi


class Plan:
    def __init__(self, nc):
        self.nc = nc
        self.streams = {e: [] for e in ENGS}
        self.count = {e: 0 for e in ENGS}
        self.wm = {e: {} for e in ENGS}
        self.res = {}
        self.dma_cnt = [0] * N_DMA_SEMS
        self.dma_rr = {"pool": 0, "sp": 0, "act": 0}
        self.sems = {}

    def _need(self, eng, reads, writes):
        need = {}

        def add(tok):
            if tok is None:
                return
            k, v = tok
            if need.get(k, 0) < v:
                need[k] = v

        for r in reads:
            ent = self.res.get(r)
            if ent is not None:
                add(ent[0])
        for w in writes:
            ent = self.res.get(w)
            if ent is not None:
                add(ent[0])
                for k, v in ent[1].items():
                    add((k, v))
        out = []
        for k, v in need.items():
            if k == "pe" and eng == "pe":
                continue
            if self.wm[eng].get(k, 0) >= v:
                continue
            self.wm[eng][k] = v
            out.append((k, v))
        return out

    def _record(self, tok, reads, writes):
        k, v = tok
        for r in reads:
            ent = self.res.setdefault(r, [None, {}])
            if ent[1].get(k, 0) < v:
                ent[1][k] = v
        for w in writes:
            self.res[w] = [tok, {}]

    def op(self, eng, fn, reads=(), writes=(), signal=True):
        writes = list(writes) + [r for r in reads if r.startswith("ps")]
        reads = [r for r in reads if not r.startswith("ps")]
        waits = self._need(eng, reads, writes)
        if signal:
            self.count[eng] += 1
            tok = (eng, self.count[eng])
            inc = (eng, 1)
        else:
            tok = (eng, self.count[eng] + 1)
            inc = None
        self._record(tok, reads, writes)
        self.streams[eng].append((waits, fn, inc))
        return tok

    def dma(self, eng, fn, reads=(), writes=()):
        half = N_DMA_SEMS // 2
        base = 0 if eng == "pool" else half
        s = base + self.dma_rr[eng]
        self.dma_rr[eng] = (self.dma_rr[eng] + 1) % half
        key = "dma%d" % s
        waits = self._need(eng, reads, writes)
        prev = 16 * self.dma_cnt[s]
        if prev and self.wm[eng].get(key, 0) < prev:
            self.wm[eng][key] = prev
            waits.append((key, prev))
        self.dma_cnt[s] += 1
        tok = (key, 16 * self.dma_cnt[s])
        self._record(tok, reads, writes)
        self.streams[eng].append((waits, fn, (key, 16)))
        return tok

    def fence(self, new, old):
        merged = {}
        for o in old:
            ent = self.res.get(o)
            if ent is None:
                continue
            if ent[0] is not None:
                k, v = ent[0]
                merged[k] = max(merged.get(k, 0), v)
            for k, v in ent[1].items():
                merged[k] = max(merged.get(k, 0), v)
        for n in new:
            ent = self.res.get(n)
            m2 = dict(merged)
            if ent is not None:
                if ent[0] is not None:
                    k, v = ent[0]
                    m2[k] = max(m2.get(k, 0), v)
                for k, v in ent[1].items():
                    m2[k] = max(m2.get(k, 0), v)
            self.res[n] = [None, m2]

    def wait_tokens(self, eng, toks):
        waits = []
        for k, v in toks:
            if self.wm[eng].get(k, 0) < v:
                self.wm[eng][k] = v
                waits.append((k, v))
        self.streams[eng].append((waits, None, None))

    def all_tokens(self):
        toks = {}
        for ent in self.res.values():
            if ent[0] is not None:
                k, v = ent[0]
                toks[k] = max(toks.get(k, 0), v)
            for k, v in ent[1].items():
                toks[k] = max(toks.get(k, 0), v)
        return list(toks.items())

    def replay(self):
        nc = self.nc
        with ExitStack() as es:
            for e in ENGS:
                self.sems[e] = es.enter_context(nc.semaphore("s_" + e))
            for i in range(N_DMA_SEMS):
                self.sems["dma%d" % i] = es.enter_context(nc.semaphore("s_dma%d" % i))
            block = es.enter_context(nc.Block())
            sems = self.sems

            def run(engname):
                def body(eng):
                    for waits, fn, inc in self.streams[engname]:
                        for k, v in waits:
                            eng.wait_ge(sems[k], v)
                        if fn is None:
                            continue
                        ins = fn(eng)
                        if inc is not None:
                            ins.then_inc(sems[inc[0]], inc[1])
                return body

            block.tensor(run("pe"))
            block.scalar(run("act"))
            block.vector(run("dve"))
            block.gpsimd(run("pool"))
            block.sync(run("sp"))


class Ring:
    def __init__(self, items):
        self.items = list(items)
        self.i = 0

    def next(self):
        v = self.items[self.i]
        self.i = (self.i + 1) % len(self.items)
        return v


def build(n_seq=2, layers=(0, 1), dbg=(), stop=None):
    nc = bass.Bass("TRN2", target_bir_lowering=False)

    def din(name, shape, dt=F32):
        return nc.dram_tensor(name, list(shape), dt, kind="ExternalInput").ap()

    x_d = din("x", [n_seq, S, D])
    cT_d = din("cT", [128, 16])
    pos_d = din("pos", [n_seq, S], I32)
    w_adaT_d = din("w_adaT", [2, 48, 128, 8, 128])
    b_ada_d = din("b_ada", [2, 6 * D])
    badaT_d = din("badaT", [2, 128, 48])
    gcols_d = din("gcols", [2, 128, 17])
    g_post_mix_d = din("g_post_mix", [2, D])
    g_post_ffn_d = din("g_post_ffn", [2, D])
    w_inT_d = din("w_inT", [2, 48, 128, 8, 128])
    w_fgt_d = din("w_fgt", [2, D, 4])
    b_fgt_d = din("b_fgt", [2, 4])
    lam_d = [din(n, [2, 64]) for n in ("lam_q1", "lam_k1", "lam_q2", "lam_k2")]
    w_brT_d = [din("w_braT", [2, 8, 128, 4, 128]), din("w_brbT", [2, 8, 128, 2, 128]),
               din("w_brcT", [2, 8, 128, 2, 128])]
    w_outT_d = din("w_outT", [2, 4, 128, 8, 256])
    w_guT_d = din("w_guT", [2, 44, 128, 8, 128])
    w_dnT_d = din("w_dnT", [2, 8, 128, NJ, 128])
    invf_d = din("invf", [128, 1])
    rmat_d = din("rmat", [128, 128])
    ident_d = din("ident", [128, 128])
    mask_d = din("mask01", [128, 128])
    negmask_d = din("negmask", [128, 128])
    onehot_d = din("onehot", [8, S])
    out_d = nc.dram_tensor("out", [n_seq, S, D], F32, kind="ExternalOutput").ap()
    dbg_d = {}

    es = ExitStack()
    with es:
        def sb(name, shape, dt):
            return es.enter_context(nc.sbuf_tensor(name, list(shape), dt))

        X = sb("X", [128, NT, D], F32)
        hT = sb("hT", [128, 8, S], BF16)
        R12 = sb("R12", [128, 24576], BF16)
        R3 = sb("R3", [128, 8192], BF16)
        tabC = sb("tabC", [128, S], BF16)
        tabS = sb("tabS", [128, S], BF16)
        NSLOT = 3
        wslot = [sb("wslot%d" % i, [128, 2816], BF16) for i in range(NSLOT)]
        G = sb("G", [128, D], F32)
        SCR = sb("SCR", [128, 2048], F32)
        ident_b = sb("ident_b", [128, 128], BF16)
        negmask_b = sb("negmask_b", [128, 128], BF16)
        rmat_b = sb("rmat_b", [128, 128], BF16)
        ones_b = sb("ones_b", [128, 128], BF16)
        U_f = sb("U_f", [128, 128], F32)
        ones_f = sb("ones_f", [128, 128], F32)
        invf = sb("invf_s", [128, 1], F32)
        halfpi = sb("halfpi", [128, 1], F32)
        modc1 = sb("modc1", [128, 2, 48], F32)
        cTs = sb("cTs", [128, 16], F32)
        ca = sb("ca", [128, 16], BF16)
        cbc = sb("cbc", [128, 8, 128], BF16)
        badaT = sb("badaT_s", [128, 48], F32)
        gcols = sb("gcols_s", [128, 17], F32)
        modc = sb("modc", [128, 48], F32)
        scsh = sb("scsh", [128, 32], F32)
        ss = sb("ss", [128, 16], F32)
        lnv = sb("lnv", [128, 16], F32)
        rstd = sb("rstd", [128, 16], F32)
        lamt = sb("lamt", [128, 4, 64], F32)
        lamp = sb("lamp", [128, 64], F32)
        lams = sb("lams", [128, 4], F32)
        neglam = sb("neglam", [128, 1], F32)
        gcol = sb("gcol", [128, 1], F32)
        bfrep = sb("bfrep", [128, 4], F32)
        zb = sb("zb", [128, 4, NT], F32)
        lf = sb("lf", [128, 4, NT], F32)
        tots = sb("tots", [128, 4, NT], F32)
        offs = sb("offs", [128, 4, NT], F32)
        Lcum = sb("Lcum", [128, 4, NT], F32)
        r8 = sb("r8", [128, 4, NT], BF16)
        kmT = sb("kmT", [128, 2, 8], BF16)
        kms = sb("kms", [128, 2, 8], F32)
        gm = sb("gm", [128, 2, 8, 8], F32)
        mx8 = sb("mx8", [128, 8], F32)
        selb = sb("selb", [128, 2, 8, 8], BF16)
        ssq = sb("ssq", [128, 8, 8], F32)
        ssu = sb("ssu", [128, 8], F32)
        lnvu = sb("lnvu", [128, 8], F32)
        rstdu = sb("rstdu", [128, 8], F32)
        junk2 = sb("junk2", [128, 4, 256], BF16)
        junk_ring = Ring([0, 1, 2, 3])

        ps = [es.enter_context(nc.psum_tensor("ps%d" % i, [128, 512], F32)) for i in range(8)]

        oT = R12[:, 0:16384].rearrange("p (c t) -> p c t", t=S)
        qk = [R12[:, 16384 + i * S: 16384 + (i + 1) * S] for i in range(4)]
        mg = R12[:, 16384:24576].rearrange("p (c t) -> p c t", t=1024)
        actT = R12[:, 0:22528].rearrange("p (c t) -> p c t", t=1024)

        Vd1 = R3[:, 0:2048].rearrange("p (t c) -> p t c", c=128)
        Vs = R3[:, 0:4096].rearrange("p (t j c) -> p t j c", j=2, c=128)
        PT = [R3[:, 4096 + i * 512: 4096 + (i + 1) * 512] for i in range(8)]
        ybuf = R3[:, 0:8192].rearrange("p (t c) -> p t c", c=1024)
        posi = R3[:, 0:4096].bitcast(I32)
        T = [SCR[:, i * 512:(i + 1) * 512] for i in range(4)]
        Ti = [T[i].bitcast(I32) for i in range(4)]
        xn = [T[2].bitcast(BF16), T[3].bitcast(BF16)]
        XN_NAMES = ["T2", "T3"]
        junkA = junk2[:].rearrange("p a b -> p (a b)")
        deferred = []

        def run_deferred():
            while deferred:
                deferred.pop(0)()

        N_O = ["o%d_%d" % (c, t) for c in range(8) for t in range(4)]
        N_AC = ["ac0", "ac1"]
        N_QK = ["qk0", "qk1", "qk2", "qk3"]
        N_MG = ["mg0", "mg1"]
        N_XN = ["xn0", "xn1"]
        N_V = ["V"]
        N_PT = ["pt%d" % i for i in range(8)]
        N_YB = ["yb%d" % i for i in range(8)]
        N_POSI = ["posi"]
        REG_A = N_O + N_AC
        REG_B = N_QK + N_MG + N_AC
        REG_C = N_V + N_PT + N_YB + N_POSI
        N_T = ["T0", "T1", "T2", "T3"]
        WNAMES = [["w%d_%d" % (s_, j) for j in range(4)] for s_ in range(NSLOT)]

        P = Plan(nc)

        def claim(names):
            for reg in (REG_A, REG_B, REG_C):
                mine = [n for n in names if n in reg]
                if mine:
                    P.fence(mine, [n for n in reg if n not in mine])

        def mm(out, lhsT, rhs, start, stop, reads, writes, signal):
            P.op("pe", lambda e: e.matmul(out, lhsT=lhsT, rhs=rhs, start=start, stop=stop),
                 reads, writes, signal)

        def tr(out, in_, reads, writes, signal=True):
            P.op("pe", lambda e: e.transpose(out=out, in_=in_, identity=ident_b[:]), reads, writes, signal)

        def act(out, in_, func, reads, writes, bias=None, scale=None, accum=None):
            kw = {}
            if bias is not None:
                kw["bias"] = bias
            if scale is not None:
                kw["scale"] = scale
            if accum is not None:
                kw["accum_out"] = accum
            P.op("act", lambda e: e.activation(out=out, in_=in_, func=func, **kw), reads, writes)

        def tt(out, in0, in1, op, reads, writes, eng="dve"):
            P.op(eng, lambda e: e.tensor_tensor(out=out, in0=in0, in1=in1, op=op), reads, writes)

        def ts(out, in0, s1, s2, op0, op1, reads, writes, eng="dve"):
            if op1 is None:
                P.op(eng, lambda e: e.tensor_scalar(out=out, in0=in0, scalar1=s1, scalar2=None, op0=op0), reads, writes)
            else:
                P.op(eng, lambda e: e.tensor_scalar(out=out, in0=in0, scalar1=s1, scalar2=s2, op0=op0, op1=op1), reads, writes)

        def stt(out, in0, scalar, in1, op0, op1, reads, writes, eng="dve"):
            P.op(eng, lambda e: e.scalar_tensor_tensor(out=out, in0=in0, scalar=scalar, in1=in1, op0=op0, op1=op1),
                 reads, writes)

        def cp(out, in_, reads, writes, eng="dve"):
            P.op(eng, lambda e: e.tensor_copy(out=out, in_=in_), reads, writes)

        def recip(out, in_, reads, writes):
            act(out, in_, AF.Ln, reads, writes)
            act(out, out, AF.Exp, list(writes), writes, scale=-1.0)

        def memset(ap, val, writes, eng="dve"):
            P.op(eng, lambda e: e.memset(ap, val), (), writes)

        def dma(eng, out, in_, reads, writes):
            P.dma(eng, lambda e: e.dma_start(out=out, in_=in_), reads, writes)

        def dump(name, ap, reads):
            if name not in dbg:
                return
            d = nc.dram_tensor("dbg_" + name, list(ap.shape), ap.dtype, kind="ExternalOutput").ap()
            dbg_d[name] = d
            dma("sp", d, ap, reads, ())

        bank_main = Ring([0, 1, 2, 3])
        bank_aux = Ring([4, 5, 6, 7])
        pt_ring = Ring(list(range(8)))
        slot_ring = Ring(list(range(NSLOT)))

        def wload(pieces):
            s_ = slot_ring.next()
            views = []
            off = 0
            for j, src in enumerate(pieces):
                kc_, n_ = src.shape[1], src.shape[2]
                v = wslot[s_][:, off:off + kc_ * n_].rearrange("p (k n) -> p k n", n=n_)
                off += kc_ * n_
                dma("pool", v, src, (), [WNAMES[s_][j]])
                views.append(v)
            assert off <= 2816
            return views, WNAMES[s_]

        dma("pool", ident_b[:], ident_d, (), ["ident_b"])
        dma("pool", negmask_b[:], negmask_d, (), ["negmask_b"])
        dma("pool", rmat_b[:], rmat_d, (), ["rmat_b"])
        dma("sp", U_f[:], mask_d, (), ["U_f"])
        dma("sp", invf[:], invf_d, (), ["invf"])
        dma("sp", cTs[:], cT_d, (), ["cTs"])
        memset(ones_b[:], 1.0, ["ones_b"])
        memset(halfpi[:], PI / 2, ["halfpi"])
        memset(ones_f[:], 1.0, ["ones_f"])
        memset(selb[:], 0.0, ["selb"])
        act(ca[:], cTs[:], AF.Silu, ["cTs"], ["ca"])

        def rmsn(ss_ap, lnv_ap, rstd_ap, in_names, ln_name, out_name, scale):
            act(lnv_ap, ss_ap, AF.Ln, list(in_names), [ln_name], bias=EPS, scale=scale)
            act(rstd_ap, lnv_ap, AF.Exp, [ln_name], [out_name], scale=-0.5)

        def rope_tables(s):
            claim(N_POSI)
            dma("sp", posi, pos_d[s:s + 1, :].to_broadcast([128, S]), (), ["posi"])
            for j in range(4):
                cs = slice(j * 512, (j + 1) * 512)
                ts(T[0], posi[:, cs], invf[:, 0:1], None, ALU.mult, None, ["posi", "invf"], ["T0"])
                ts(Ti[1], T[0], 1.0 / (2 * PI), None, ALU.mult, None, ["T0"], ["T1"])
                stt(T[2], Ti[1], -2 * PI, T[0], ALU.mult, ALU.add, ["T1", "T0"], ["T2"])
                ts(T[3], T[2], PI, -2 * PI, ALU.is_gt, ALU.mult, ["T2"], ["T3"])
                tt(T[2], T[2], T[3], ALU.add, ["T2", "T3"], ["T2"])
                ts(T[2], T[2], -PI, PI, ALU.max, ALU.min, ["T2"], ["T2"])
                act(tabS[:, cs], T[2], AF.Sin, ["T2"], ["tab0_%d" % j])
                stt(T[3], T[2], -1.0, T[2], ALU.mult, ALU.max, ["T2"], ["T3"])
                act(tabC[:, cs], T[3], AF.Sin, ["T3", "halfpi"], ["tab1_%d" % j], bias=halfpi[:, 0:1], scale=-1.0)
            dump("tabC", tabC[:], ["tab1_%d" % j for j in range(4)])
            dump("tabS", tabS[:], ["tab0_%d" % j for j in range(4)])

        def adaln_cols(s, l):
            dma("sp", badaT[:], badaT_d[l], (), ["badaT"])
            dma("sp", gcols[:], gcols_d[l], (), ["gcols"])
            if s == 0:
                b = bank_aux.next()
                chunks = list(range(0, 16)) + list(range(24, 40))
                for i in range(0, len(chunks), 2):
                    j0 = chunks[i]
                    wv, wn = wload([w_adaT_d[l, j0], w_adaT_d[l, j0 + 1]])
                    for jj in range(2):
                        j = j0 + jj
                        for kc in range(8):
                            mm(ps[b][:, 2 * j:2 * j + n_seq], wv[jj][:, kc, :],
                               ca[:, kc * 2: kc * 2 + n_seq], kc == 0, kc == 7, wn + ["ca"], ["ps%d" % b], kc == 7)
                psv = ps[b][:, 0:96].rearrange("p (j b) -> p j b", b=2)
                for (a0, a1) in ((0, 16), (24, 40)):
                    tt(modc[:, a0:a1], psv[:, a0:a1, 0], badaT[:, a0:a1], ALU.add, ["ps%d" % b, "badaT", "modc"], ["modc"])
                    if n_seq > 1:
                        tt(modc1[:, l, a0:a1], psv[:, a0:a1, 1], badaT[:, a0:a1], ALU.add,
                           ["ps%d" % b, "badaT", "modc1_%d" % l], ["modc1_%d" % l])
            else:
                for (a0, a1) in ((0, 16), (24, 40)):
                    cp(modc[:, a0:a1], modc1[:, l, a0:a1], ["modc1_%d" % l, "modc"], ["modc"])
            stt(scsh[:, 0:8], modc[:, 8:16], 1.0, gcols[:, 0:8], ALU.add, ALU.mult, ["modc", "gcols"], ["scsh"])
            cp(scsh[:, 8:16], modc[:, 0:8], ["modc"], ["scsh"])
            stt(scsh[:, 16:24], modc[:, 32:40], 1.0, gcols[:, 8:16], ALU.add, ALU.mult, ["modc", "gcols"], ["scsh"])
            cp(scsh[:, 24:32], modc[:, 24:32], ["modc"], ["scsh"])
            ts(gcol[:], gcols[:, 16:17], 1.0 - lam_init(l), None, ALU.mult, None, ["gcols"], ["gcol"])
            for i in range(4):
                dma("sp", lamt[:, i, :], lam_d[i][l:l + 1, :].to_broadcast([128, 64]), (), ["lamt%d" % i])
            for i in range(2):
                tt(lamp[:], lamt[:, 2 * i, :], lamt[:, 2 * i + 1, :], ALU.mult,
                   ["lamt%d" % (2 * i), "lamt%d" % (2 * i + 1)], ["lamp"])
                P.op("dve", (lambda i_: lambda e: e.reduce_sum(out=lams[:, i_:i_ + 1], in_=lamp[:], axis=AX.X))(i),
                     ["lamp"], ["lams%d" % i])
            act(lams[:, 2:4], lams[:, 0:2], AF.Exp, ["lams0", "lams1"], ["lamse"])
            tt(neglam[:], lams[:, 3:4], lams[:, 2:3], ALU.subtract, ["lamse"], ["neglam"])
            ts(neglam[:], neglam[:], -lam_init(l), None, ALU.add, None, ["neglam"], ["neglam"])
            dma("sp", bfrep[:], b_fgt_d[l:l + 1, :].to_broadcast([128, 4]), (), ["bfrep"])

        def adaln_G(s, l, which):
            c0 = 2048 if which == 0 else 5120
            gp = g_post_mix_d if which == 0 else g_post_ffn_d
            brep = SCR[:, 0:1024]
            grep = SCR[:, 1024:2048]
            dma("sp", brep, b_ada_d[l:l + 1, c0:c0 + 1024].to_broadcast([128, 1024]), (), ["T0", "T1"])
            dma("sp", grep, gp[l:l + 1, :].to_broadcast([128, 1024]), (), ["T2", "T3"])
            for q in range(4):
                jq = c0 // 128 + 2 * q
                wv, wn = wload([w_adaT_d[l, jq], w_adaT_d[l, jq + 1]])
                b = bank_main.next()
                for jj in range(2):
                    for kc in range(8):
                        mm(ps[b][:, jj * 128:(jj + 1) * 128], cbc[:, kc, :], wv[jj][:, kc, :], kc == 0, kc == 7,
                           wn + ["cbc"], ["ps%d" % b], kc == 7)
                cs = slice(q * 256, (q + 1) * 256)
                tt(G[:, cs], ps[b][:, 0:256], brep[:, cs], ALU.add, ["ps%d" % b, "T0", "T1"], ["G%d" % q])
                tt(G[:, cs], G[:, cs], grep[:, cs], ALU.mult, ["G%d" % q, "T2", "T3"], ["G%d" % q])

        N_G = ["G0", "G1", "G2", "G3"]

        def stage_a(tiles, off):
            for _ in stage_a_gen(tiles, off):
                pass

        def stage_a_gen(tiles, off):
            ngrp = len(tiles) // 4

            def sq(g):
                grp = tiles[4 * g:4 * g + 4]
                c0 = grp[0]
                names = ["ss%d" % t for t in grp]
                memset(ss[:, c0:c0 + 4], 0.0, names)
                for t in grp:
                    act(junkA, X[:, t, :], AF.Square, ["X%d" % t], ["junk0", "junk1", "junk2", "junk3", "ss%d" % t],
                        accum=ss[:, t:t + 1])
                rmsn(ss[:, c0:c0 + 4], lnv[:, c0:c0 + 4], rstd[:, c0:c0 + 4], names, "lnv%d" % (c0 // 4),
                     "rstd%d" % (c0 // 4), 1.0 / D)

            sq(0)
            for g in range(ngrp):
                if g + 1 < ngrp:
                    sq(g + 1)
                grp = tiles[4 * g:4 * g + 4]
                tc = grp[0] // 4
                banks = [bank_main.next(), bank_main.next(), bank_aux.next(), bank_aux.next()]
                for i, t in enumerate(grp):
                    ts(xn[i % 2], X[:, t, :], rstd[:, t:t + 1], None, ALU.mult, None,
                       ["X%d" % t, "rstd%d" % tc], [XN_NAMES[i % 2]])
                    for kc in range(8):
                        b = banks[kc // 2]
                        pv = ps[b][:].bitcast(BF16)
                        o0 = (kc % 2) * 512 + i * 128
                        tr(pv[:, o0:o0 + 128], xn[i % 2][:, kc * 128:(kc + 1) * 128],
                           [XN_NAMES[i % 2], "ident_b"], ["ps%d" % b], signal=(kc == 7))
                for kc in range(8):
                    b = banks[kc // 2]
                    pv = ps[b][:].bitcast(BF16)
                    o0 = (kc % 2) * 512
                    if kc % 2 == 0:
                        act(hT[:, kc, tc * 512:(tc + 1) * 512], pv[:, o0:o0 + 512], AF.Identity,
                            ["ps%d" % b, "scsh"], ["hT%d" % tc],
                            bias=scsh[:, off + 8 + kc: off + 9 + kc], scale=scsh[:, off + kc: off + kc + 1])
                    else:
                        ts(hT[:, kc, tc * 512:(tc + 1) * 512], pv[:, o0:o0 + 512],
                           scsh[:, off + kc: off + kc + 1], scsh[:, off + 8 + kc: off + 9 + kc], ALU.mult, ALU.add,
                           ["ps%d" % b, "scsh"], ["hT%d" % tc])
                yield g

        def proj_F(wp, wn, rope, dests):
            step, fin = proj_F_steps(wp, wn, rope, dests)
            for tc in range(4):
                step(tc)
            fin()

        def proj_F_steps(wp, wn, rope, dests):
            tail = [None]

            def step(tc):
                cs = slice(tc * 512, (tc + 1) * 512)
                if tc == 0:
                    run_deferred()
                b = bank_main.next()
                for kc in range(8):
                    mm(ps[b][:], wp[:, kc, :], hT[:, kc, cs], kc == 0, kc == 7,
                       wn + ["hT%d" % tc], ["ps%d" % b], kc == 7)
                if not rope:
                    for (r0, nr, tile, d0, nm) in dests:
                        cp(tile[d0:d0 + nr, cs], ps[b][r0:r0 + nr, :], ["ps%d" % b], [nm])
                else:
                    pi = pt_ring.next()
                    act(PT[pi], ps[b][:], AF.Identity, ["ps%d" % b], ["pt%d" % pi])
                    if tail[0] is not None:
                        tail[0]()

                    def mk(b=b, pi=pi, cs=cs, tc=tc):
                        b2 = bank_aux.next()
                        mm(ps[b2][:], rmat_b[:], PT[pi], True, True, ["rmat_b", "pt%d" % pi], ["ps%d" % b2], True)
                        tt(T[0], ps[b][:], tabC[:, cs], ALU.mult, ["ps%d" % b, "tab1_%d" % tc], ["T0"])
                        tt(T[1], ps[b2][:], tabS[:, cs], ALU.mult, ["ps%d" % b2, "tab0_%d" % tc], ["T1"])
                        for (r0, nr, tile, d0, nm) in dests:
                            tt(tile[d0:d0 + nr, cs], T[0][r0:r0 + nr, :], T[1][r0:r0 + nr, :], ALU.add,
                               ["T0", "T1"], [nm])
                    tail[0] = mk

            def fin():
                if tail[0] is not None:
                    tail[0]()
                    tail[0] = None
            return step, fin

        def proj_V(wp, wn, N, evac, extra=None, tiles=None, ring=None):
            for t in (range(NT) if tiles is None else tiles):
                b = (bank_main if ring is None else ring).next()
                for kc in range(8):
                    mm(ps[b][:, 0:N], hT[:, kc, t * 128:(t + 1) * 128], wp[:, kc, 0:N], kc == 0, kc == 7,
                       wn + ["hT%d" % (t // 4)], ["ps%d" % b], kc == 7)
                if extra is not None:
                    wp2, n2 = extra
                    for kc in range(8):
                        mm(ps[b][:, N:N + n2], hT[:, kc, t * 128:(t + 1) * 128], wp2[:, kc, :], kc == 0, kc == 7,
                           wn + ["hT%d" % (t // 4)], ["ps%d" % b], kc == 7)
                evac(t, b)

        def attend_chunk(qc, maps, st_ring, hook=None):
            steps = []
            nk = 4 * qc + 4
            for kt in range(nk):
                for mi, m in enumerate(maps):
                    steps.append((mi, kt, nk))
            LAG = 2 * len(maps)
            pend = []
            nstep = [0]

            def emit_pv(st):
                mi, kt, nk, pi, c0 = st
                m = maps[mi]
                for (ab, lfn, rds) in m["pv"]:
                    mm(ps[ab][:, c0:512], lfn(kt), PT[pi][:, c0:512], kt == 0, kt == nk - 1,
                       ["pt%d" % pi] + rds, ["ps%d" % ab], True)

            for (mi, kt, nk) in steps:
                m = maps[mi]
                j = kt - 4 * qc
                c0 = max(j, 0) * 128
                b = st_ring.next()
                diag = j >= 0
                mm(ps[b][:, c0:512], m["k"][:, kt * 128:(kt + 1) * 128], m["q"][:, qc * 512 + c0:(qc + 1) * 512],
                   True, not diag, m["rn"], ["ps%d" % b], not diag)
                if diag:
                    mm(ps[b][:, c0:c0 + 128], ident_b[:], negmask_b[:], False, True,
                       ["ident_b", "negmask_b"], ["ps%d" % b], True)
                pi = pt_ring.next()
                bias_ap, bias_names = m["bias"](kt)
                act(PT[pi][:, c0:512], ps[b][:, c0:512], AF.Exp, ["ps%d" % b] + bias_names, ["pt%d" % pi],
                    bias=bias_ap, scale=0.125)
                pend.append((mi, kt, nk, pi, c0))
                if len(pend) > LAG:
                    emit_pv(pend.pop(0))
                nstep[0] += 1
                if nstep[0] == 4:
                    run_deferred()
            while pend:
                emit_pv(pend.pop(0))

        ACC6 = [2, 3, 4, 5, 6, 7]
        diff_chunk_ctr = [0]
        st2 = Ring([0, 1])
        st4 = Ring([0, 1, 2, 3])

        diff_pending = [None]

        def attend_diff(h):
            for qc in range(4):
                c = diff_chunk_ctr[0]
                diff_chunk_ctr[0] += 1
                A = [ACC6[(4 * c) % 6], ACC6[(4 * c + 2) % 6]]
                Sm = [ACC6[(4 * c + 1) % 6], ACC6[(4 * c + 3) % 6]]
                for m_ in range(2):
                    mp = dict(q=qk[m_][:, :], k=qk[2][:, :], rn=["qk%d" % m_, "qk2"],
                              bias=lambda kt: (None, []),
                              pv=[(A[m_], (lambda kt: Vd1[:, kt, :]), ["V"]),
                                  (Sm[m_], (lambda kt: ones_b[:]), ["ones_b"])])
                    attend_chunk(qc, [mp], st2)
                cs = slice(qc * 512, (qc + 1) * 512)
                a0, s0, a1, s1 = A[0], Sm[0], A[1], Sm[1]
                recip(T[0], ps[s0][:], ["ps%d" % s0], ["T0"])
                tt(T[1], ps[a0][:], T[0], ALU.mult, ["ps%d" % a0, "T0"], ["T1"])
                recip(T[0], ps[s1][:], ["ps%d" % s1], ["T0"])
                tt(T[2], ps[a1][:], T[0], ALU.mult, ["ps%d" % a1, "T0"], ["T2"])
                stt(T[3], T[2], neglam[:, 0:1], T[1], ALU.mult, ALU.add, ["T2", "T1", "neglam"], ["T3"])
                sqv = T[2].bitcast(BF16)[:, 0:512]
                tt(sqv, T[3], T[3], ALU.mult, ["T3", "T2"], ["T2"])

                def part2(sqv=sqv, h=h, qc=qc, cs=cs):
                    b = st2.next()
                    mm(ps[b][:], ones_b[:], sqv, True, True, ["ones_b", "T2"], ["ps%d" % b], True)
                    act(T[0], ps[b][:], AF.Ln, ["ps%d" % b], ["T0"], bias=EPS, scale=1.0 / 128)
                    act(T[1], T[0], AF.Exp, ["T0"], ["T1"], scale=-0.5)
                    stt(oT[:, h, cs], T[3], gcol[:, 0:1], T[1], ALU.mult, ALU.mult, ["T3", "T1", "gcol"],
                        ["o%d_%d" % (h, qc)])
                deferred.append(part2)

        def attend_single(pair, chunk, R, bias_fn):
            for hl in range(2):
                h = 2 * pair + hl
                for qc in range(4):
                    ab = 4 + (hl * 4 + qc) % 4
                    maps = [dict(q=qk[hl][0:R, :], k=qk[2 + hl][0:R, :], rn=["qk%d" % hl, "qk%d" % (2 + hl)],
                                 bias=(lambda kt, h_=h: bias_fn(h_, kt)),
                                 pv=[(ab, (lambda kt, hl_=hl: Vs[:, kt, hl_, :]), ["V"])])]
                    attend_chunk(qc, maps, st4)
                    cs = slice(qc * 512, (qc + 1) * 512)
                    P.op("dve", (lambda ab_: lambda e: e.reciprocal(out=T[0][64:128, :], in_=ps[ab_][64:128, :]))(ab),
                         (), ["ps%d" % ab, "T0"])
                    d0 = hl * 64
                    tt(oT[d0:d0 + 64, chunk, cs], ps[ab][0:64, :], T[0][64:128, :], ALU.mult,
                       ["ps%d" % ab, "T0"], ["o%d_%d" % (chunk, qc)])

        def branch_diff(l, h, sa_gen=None):
            claim(N_QK + N_V + N_PT)
            wqk, nqk = wload([w_inT_d[l, h], w_inT_d[l, 4 + h]])
            wvv, nv = wload([w_inT_d[l, 8 + h]])
            if h == 0:
                memset(qk[0][64:128, :], 0.0, ["qk0"])
                memset(qk[1][0:64, :], 0.0, ["qk1"])
            def evac(t, b):
                cp(Vd1[:, t, :], ps[b][:, 0:128], ["ps%d" % b], ["V"])
            dq = [(0, 64, qk[0], 0, "qk0"), (64, 64, qk[1], 64, "qk1")]
            dk = [(0, 128, qk[2], 0, "qk2")]
            if sa_gen is None:
                proj_F(wqk[0], nqk, True, dq)
                proj_F(wqk[1], nqk, True, dk)
                proj_V(wvv[0], nv, 128, evac)
            else:
                sq_, fq_ = proj_F_steps(wqk[0], nqk, True, dq)
                sk_, fk_ = proj_F_steps(wqk[1], nqk, True, dk)
                for tc in range(4):
                    next(sa_gen)
                    sq_(tc)
                    sk_(tc)
                    proj_V(wvv[0], nv, 128, evac, tiles=range(4 * tc, 4 * tc + 4), ring=bank_aux)
                    fq_()
                    fk_()
                for _ in sa_gen:
                    pass
            if h == 0:
                dump("qd0", qk[0], ["qk0"])
                dump("kd0", qk[2], ["qk2"])
            attend_diff(h)

        def branch_fox(l, pair):
            claim(N_QK + N_V + N_PT)
            wqk, nqk = wload([w_inT_d[l, 12 + pair], w_inT_d[l, 14 + pair]])
            pieces = [w_inT_d[l, 16 + pair]]
            if pair == 0:
                pieces.append(w_fgt_d[l].rearrange("(k p) n -> p k n", p=128))
            wvv, nv = wload(pieces)
            proj_F(wqk[0], nqk, False, [(0, 64, qk[0], 0, "qk0"), (64, 64, qk[1], 0, "qk1")])
            proj_F(wqk[1], nqk, False, [(0, 64, qk[2], 0, "qk2"), (64, 64, qk[3], 0, "qk3")])
            memset(Vs[:, :, :, 64:128], 1.0, ["V"], eng="pool")
            for i in range(4):
                memset(qk[i][64:128, :], 0.0, ["qk%d" % i], eng="pool")
            memset(qk[2][64:65, :], 1.0, ["qk2"], eng="pool")
            memset(qk[3][64:65, :], 1.0, ["qk3"], eng="pool")

            def evac(t, b):
                cp(Vs[:, t, :, 0:64], ps[b][:, 0:128].rearrange("p (j c) -> p j c", c=64), ["ps%d" % b], ["V"])
                if pair == 0:
                    tt(zb[:, :, t], ps[b][:, 128:132], bfrep[:], ALU.add, ["ps%d" % b, "bfrep"], ["zb"])
            proj_V(wvv[0], nv, 128, evac, extra=((wvv[1], 4) if pair == 0 else None))
            if pair == 0:
                act(lf[:], zb[:], AF.Exp, ["zb"], ["lf"], scale=-1.0)
                act(lf[:], lf[:], AF.Ln, ["lf"], ["lf"], bias=1.0)
                b1 = bank_aux.next()
                lf2 = lf[:].rearrange("p h t -> p (h t)")
                mm(ps[b1][:, 0:64], U_f[:], lf2, True, True, ["U_f", "lf"], ["ps%d" % b1], True)
                b2 = bank_aux.next()
                mm(ps[b2][:, 0:64], ones_f[:], lf2, True, True, ["ones_f", "lf"], ["ps%d" % b2], True)
                cp(tots[:].rearrange("p h t -> p (h t)"), ps[b2][:, 0:64], ["ps%d" % b2], ["tots"])
                memset(offs[:, :, 0:1], 0.0, ["offs"])
                for i in range(1, NT):
                    tt(offs[:, :, i:i + 1], offs[:, :, i - 1:i], tots[:, :, i - 1:i], ALU.add, ["offs", "tots"], ["offs"])
                tt(Lcum[:].rearrange("p h t -> p (h t)"), ps[b1][:, 0:64], offs[:].rearrange("p h t -> p (h t)"),
                   ALU.add, ["ps%d" % b1, "offs"], ["Lcum"])
                ts(r8[:], Lcum[:], -8.0, None, ALU.mult, None, ["Lcum"], ["r8"])
                dump("Lcum", Lcum[:], ["Lcum"])
            for hl in range(2):
                h = 2 * pair + hl
                for g in range(4):
                    b = bank_aux.next()
                    for i in range(4):
                        t = 4 * g + i
                        mm(ps[b][0:1, i * 128:(i + 1) * 128], r8[:, h, t:t + 1], ident_b[:], True, True,
                           ["r8", "ident_b"], ["ps%d" % b], i == 3)
                    act(qk[hl][64:65, g * 512:(g + 1) * 512], ps[b][0:1, :], AF.Identity, ["ps%d" % b], ["qk%d" % hl])
            if pair == 0:
                dump("qf0", qk[0], ["qk0"])
                dump("kf0", qk[2], ["qk2"])

            def bias_fn(h, kt):
                return Lcum[:, h, kt:kt + 1], ["Lcum"]
            attend_single(pair, 4 + pair, 128, bias_fn)

        def branch_moba(l, pair):
            claim(N_QK + N_V + N_PT)
            wqk, nqk = wload([w_inT_d[l, 18 + pair], w_inT_d[l, 20 + pair]])
            wvv, nv = wload([w_inT_d[l, 22 + pair]])
            proj_F(wqk[0], nqk, True, [(0, 64, qk[0], 0, "qk0"), (64, 64, qk[1], 0, "qk1")])
            proj_F(wqk[1], nqk, True, [(0, 64, qk[2], 0, "qk2"), (64, 64, qk[3], 0, "qk3")])
            memset(Vs[:, :, :, 64:128], 1.0, ["V"], eng="pool")
            for i in range(4):
                memset(qk[i][64:128, :], 0.0, ["qk%d" % i], eng="pool")
            for hl in range(2):
                dma("pool", qk[2 + hl][64:72, :], onehot_d, (), ["qk%d" % (2 + hl)])

            def evac(t, b):
                cp(Vs[:, t, :, 0:64], ps[b][:, 0:128].rearrange("p (j c) -> p j c", c=64), ["ps%d" % b], ["V"])
            proj_V(wvv[0], nv, 128, evac)
            bg = bank_aux.next()
            for hl in range(2):
                P.op("dve", (lambda hl_: lambda e: e.tensor_reduce(
                    out=kms[0:64, hl_, :], in_=qk[2 + hl_][0:64, :].rearrange("p (n k) -> p n k", k=256),
                    axis=AX.X, op=ALU.add))(hl), ["qk%d" % (2 + hl)], ["kms%d" % hl])
                ts(kmT[0:64, hl, :], kms[0:64, hl, :], 1.0 / 256, None, ALU.mult, None, ["kms%d" % hl], ["kmT%d" % hl])
                for i in range(8):
                    qt = 8 + i
                    c = (hl * 8 + i) * 8
                    mm(ps[bg][:, c:c + 8], qk[hl][0:64, qt * 128:(qt + 1) * 128], kmT[0:64, hl, :], True, True,
                       ["qk%d" % hl, "kmT%d" % hl], ["ps%d" % bg], (hl == 1 and i == 7))
            memset(gm[:], -1e30, ["gm"])
            for hl in range(2):
                for i in range(8):
                    own = (8 + i) // 2
                    c = (hl * 8 + i) * 8
                    cp(gm[:, hl, i, 0:own], ps[bg][:, c:c + own], ["ps%d" % bg, "gm"], ["gm"])
            for hl in range(2):
                for i in range(8):
                    own = (8 + i) // 2
                    P.op("dve", (lambda hl_, i_: lambda e: e.max(out=mx8[:], in_=gm[:, hl_, i_, :]))(hl, i),
                         ["gm"], ["mx8"])
                    ts(selb[:, hl, i, 0:own], gm[:, hl, i, 0:own], mx8[:, 2:3], NEGBIG, ALU.is_lt, ALU.mult,
                       ["gm", "mx8"], ["selb"])
            for i in range(8):
                qt = 8 + i
                b = bank_aux.next()
                for hl in range(2):
                    mm(ps[b][0:8, hl * 128:(hl + 1) * 128], selb[:, hl, i, :], ident_b[:], True, True,
                       ["selb", "ident_b"], ["ps%d" % b], hl == 1)
                for hl in range(2):
                    act(qk[hl][64:72, qt * 128:(qt + 1) * 128], ps[b][0:8, hl * 128:(hl + 1) * 128], AF.Identity,
                        ["ps%d" % b], ["qk%d" % hl])
            if pair == 0:
                dump("qm0", qk[0], ["qk0"])
                dump("km0", qk[2], ["qk2"])

            def bias_fn(h, kt):
                return None, []
            attend_single(pair, 6 + pair, 128, bias_fn)

        def update_x(half, nslab):
            unames = ["ssu%d" % i for i in range(8)]
            P.op("dve", lambda e: e.reduce_sum(out=ssu[:, 0:8], in_=ssq[:, :, 0:nslab], axis=AX.X),
                 ["ssq%d" % t for t in range(8)], unames)
            rmsn(ssu[:], lnvu[:], rstdu[:], unames, "lnvu", "rstdu", 1.0 / D)
            for t in range(8):
                tg = half * 8 + t
                stt(X[:, tg, :], ybuf[:, t, :], rstdu[:, t:t + 1], X[:, tg, :], ALU.mult, ALU.add,
                    ["yb%d" % t, "rstdu", "X%d" % tg], ["X%d" % tg])

        def evac_y(b, t, sl, ncol):
            cs = slice(sl * ncol, (sl + 1) * ncol)
            ji = junk_ring.next()
            act(junk2[:, ji, 0:ncol], ps[b][:, 0:ncol], AF.Square, ["ps%d" % b], ["ssq%d" % t, "junk%d" % ji],
                accum=ssq[:, t, sl:sl + 1])
            tt(ybuf[:, t, cs], ps[b][:, 0:ncol], G[:, cs], ALU.mult, ["ps%d" % b] + N_G, ["yb%d" % t])

        def merge_half(l, half):
            claim(N_MG)
            for fc in range(8):
                wa, na = wload([w_inT_d[l, 24 + fc], w_inT_d[l, 32 + fc]])
                wb, nb = wload([w_inT_d[l, 40 + fc], w_brT_d[0][l, fc], w_brT_d[1][l, fc], w_brT_d[2][l, fc]])
                for tcl in range(2):
                    tc = half * 2 + tcl
                    cs = slice(tc * 512, (tc + 1) * 512)
                    csl = slice(tcl * 512, (tcl + 1) * 512)
                    for br in range(3):
                        bgk = bank_main.next()
                        wsrc, wnm = (wa[0], na) if br == 0 else ((wa[1], na) if br == 1 else (wb[0], nb))
                        for kc in range(8):
                            mm(ps[bgk][:], wsrc[:, kc, :], hT[:, kc, cs], kc == 0, kc == 7,
                               wnm + ["hT%d" % tc], ["ps%d" % bgk], kc == 7)
                        by = bank_aux.next()
                        rng_ = (range(0, 4), range(4, 6), range(6, 8))[br]
                        kcs = [(wb[1 + br][:, kc - rng_[0], :], nb, kc) for kc in rng_]
                        for i, (wap, wnm2, oc) in enumerate(kcs):
                            mm(ps[by][:], wap, oT[:, oc, cs], i == 0, i == len(kcs) - 1,
                               wnm2 + ["o%d_%d" % (oc, tc)], ["ps%d" % by], i == len(kcs) - 1)
                        sg = T[br % 2]
                        sgn = "T%d" % (br % 2)
                        act(sg, ps[bgk][:], AF.Sigmoid, ["ps%d" % bgk], [sgn])
                        if br == 0:
                            tt(T[2], ps[by][:], sg, ALU.mult, ["ps%d" % by, sgn], ["T2"])
                        elif br == 1:
                            tt(T[3], ps[by][:], sg, ALU.mult, ["ps%d" % by, sgn], ["T3"])
                            tt(T[2], T[2], T[3], ALU.add, ["T2", "T3"], ["T2"])
                        else:
                            tt(T[3], ps[by][:], sg, ALU.mult, ["ps%d" % by, sgn], ["T3"])
                            tt(mg[:, fc, csl], T[2], T[3], ALU.add, ["T2", "T3"], ["mg%d" % tcl])
            if half == 0:
                dump("mg", R12[:, 16384:24576], N_MG)

        def wout_half(l, half):
            claim(N_YB)
            memset(ssq[:], 0.0, ["ssq%d" % t for t in range(8)])
            for sl in range(4):
                wvl, wn = wload([w_outT_d[l, sl]])
                wv = wvl[0]
                for t in range(8):
                    b = bank_main.next()
                    for kc in range(8):
                        mm(ps[b][:, 0:256], mg[:, kc, t * 128:(t + 1) * 128], wv[:, kc, :], kc == 0, kc == 7,
                           wn + ["mg%d" % (t // 4)], ["ps%d" % b], kc == 7)
                    evac_y(b, t, sl, 256)

        def gate_up(l, half):
            claim(N_AC)
            for j in range(NJ):
                wv, wn = wload([w_guT_d[l, j], w_guT_d[l, NJ + j]])
                for tcl in range(2):
                    tc = half * 2 + tcl
                    cs = slice(tc * 512, (tc + 1) * 512)
                    ba = bank_main.next()
                    for kc in range(8):
                        mm(ps[ba][:], wv[0][:, kc, :], hT[:, kc, cs], kc == 0, kc == 7, wn + ["hT%d" % tc],
                           ["ps%d" % ba], kc == 7)
                    bb = bank_aux.next()
                    for kc in range(8):
                        mm(ps[bb][:], wv[1][:, kc, :], hT[:, kc, cs], kc == 0, kc == 7, wn + ["hT%d" % tc],
                           ["ps%d" % bb], kc == 7)
                    sg = T[(j * 2 + tcl) % 2]
                    sgn = "T%d" % ((j * 2 + tcl) % 2)
                    act(sg, ps[ba][:], AF.Silu, ["ps%d" % ba], [sgn])
                    tt(actT[:, j, tcl * 512:(tcl + 1) * 512], ps[bb][:], sg, ALU.mult, ["ps%d" % bb, sgn], ["ac%d" % tcl])

        def down(l, half):
            claim(N_YB)
            memset(ssq[:], 0.0, ["ssq%d" % t for t in range(8)])
            for fcol in range(8):
                wvl, wn = wload([w_dnT_d[l, fcol]])
                wv = wvl[0]
                for t in range(8):
                    b = bank_main.next()
                    for kc in range(NJ):
                        mm(ps[b][:, 0:128], actT[:, kc, t * 128:(t + 1) * 128], wv[:, kc, :], kc == 0, kc == NJ - 1,
                           wn + ["ac%d" % (t // 4)], ["ps%d" % b], kc == NJ - 1)
                    evac_y(b, t, fcol, 128)
            update_x(half, 8)

        def ffn(l, s):
            stage_a(list(range(0, 8)), 16)
            gate_up(l, 0)
            stage_a(list(range(8, 16)), 16)
            down(l, 0)
            if l == layers[-1]:
                finish_tiles(s, range(0, 8))
            gate_up(l, 1)
            down(l, 1)
            if l == layers[-1]:
                finish_tiles(s, range(8, 16))

        def finish_tiles(s, tiles):
            for t in tiles:
                dma("sp", out_d[s, t * 128:(t + 1) * 128, :], X[:, t, :], ["X%d" % t], ())
            if s + 1 < n_seq:
                for t in tiles:
                    dma("sp", X[:, t, :], x_d[s + 1, t * 128:(t + 1) * 128, :], (), ["X%d" % t])

        def program():
            for s in range(n_seq):
                if s == 0:
                    for t in range(NT):
                        dma("sp", X[:, t, :], x_d[s, t * 128:(t + 1) * 128, :], (), ["X%d" % t])
                for kc in range(8):
                    cp(cbc[:, kc, :], ca[:, kc * 2 + s: kc * 2 + s + 1].to_broadcast([128, 128]), ["ca"], ["cbc"])
                rope_tables(s)
                for l in layers:
                    adaln_cols(s, l)
                    adaln_G(s, l, 0)
                    sa = stage_a_gen(list(range(NT)), 0)
                    if stop == "a":
                        for _ in sa:
                            pass
                        dump("hT", hT[:], ["hT%d" % i for i in range(4)])
                        return
                    claim(N_O)
                    for h in range(4):
                        branch_diff(l, h, sa if h == 0 else None)
                        if h == 0:
                            dump("hT", hT[:], ["hT%d" % i for i in range(4)])
                        if stop == "d0":
                            dump("oT2", R12[:, 0:4096], N_O)
                            return
                    for pair in range(2):
                        branch_fox(l, pair)
                    for pair in range(2):
                        branch_moba(l, pair)
                    dump("oT", R12[:, 0:16384], N_O)
                    if stop == "attn":
                        return
                    for half in range(2):
                        merge_half(l, half)
                        wout_half(l, half)
                        update_x(half, 4)
                    dump("xmix", X[:], ["X%d" % t for t in range(NT)])
                    if stop == "mix":
                        return
                    adaln_G(s, l, 1)
                    ffn(l, s)

        def lam_init(l):
            return 0.8 - 0.6 * math.exp(-0.3 * l)

        program()
        P.wait_tokens("sp", P.all_tokens())
        P.replay()
    nc._dbg_names = list(dbg_d.keys())
    nc._n_instr = {e: len(P.streams[e]) for e in ENGS}
    return nc


def host_consts():
    inv_freq = 1.0 / (10000.0 ** (np.arange(0, 64, 2, dtype=np.float32) / 64.0))
    invf = np.tile(inv_freq.astype(np.float32), 4).reshape(128, 1)
    rmat = np.zeros((128, 128), np.float32)
    for m in range(128):
        if m % 64 < 32:
            rmat[m + 32, m] = -1.0
        else:
            rmat[m - 32, m] = 1.0
    ident = np.eye(128, dtype=np.float32)
    mask01 = (np.arange(128)[:, None] <= np.arange(128)[None, :]).astype(np.float32)
    onehot = (np.arange(S)[None, :] // 256 == np.arange(8)[:, None]).astype(np.float32)
    negmask = np.where(np.arange(128)[:, None] > np.arange(128)[None, :], -30000.0, 0.0).astype(np.float32)
    return dict(invf=invf, rmat=rmat, ident=ident, mask01=mask01, onehot=onehot, negmask=negmask)


def make_in_maps(inputs, n_cores=8, n_seq=2):
    f = lambda a: np.ascontiguousarray(np.asarray(a))
    consts = host_consts()
    b_ada = f(inputs["b_ada"])
    badaT = np.ascontiguousarray(b_ada.reshape(2, 48, 128).transpose(0, 2, 1))
    gpm = f(inputs["g_pre_mix"]).reshape(2, 8, 128).transpose(0, 2, 1)
    gpf = f(inputs["g_pre_ffn"]).reshape(2, 8, 128).transpose(0, 2, 1)
    gsl = f(inputs["g_subln"]).reshape(2, 128, 1)
    gcols = np.ascontiguousarray(np.concatenate([gpm, gpf, gsl], axis=2))
    def tile_w(w, ncol):
        w = f(w)
        L, K, N = w.shape
        return np.ascontiguousarray(w.reshape(L, K // 128, 128, N // ncol, ncol).transpose(0, 3, 2, 1, 4))
    w_in = f(inputs["w_in"])
    shared = dict(
        w_adaT=tile_w(inputs["w_ada"], 128), b_ada=b_ada, badaT=badaT, gcols=gcols,
        g_post_mix=f(inputs["g_post_mix"]), g_post_ffn=f(inputs["g_post_ffn"]),
        w_inT=tile_w(w_in[:, :, 0:6144], 128), w_fgt=np.ascontiguousarray(w_in[:, :, 6144:6148]),
        b_fgt=f(inputs["b_fgt"]),
        lam_q1=f(inputs["lam_q1"]), lam_k1=f(inputs["lam_k1"]), lam_q2=f(inputs["lam_q2"]), lam_k2=f(inputs["lam_k2"]),
        w_braT=tile_w(inputs["w_br_a"], 128), w_brbT=tile_w(inputs["w_br_b"], 128), w_brcT=tile_w(inputs["w_br_c"], 128),
        w_outT=tile_w(inputs["w_out"], 256), w_guT=tile_w(inputs["w_gate_up"], 128),
        w_dnT=tile_w(inputs["w_down"], 128), **consts)
    x = f(inputs["x"])
    c = f(inputs["c"])
    pos = f(inputs["positions"]).astype(np.int32)
    maps = []
    for i in range(n_cores):
        bs = slice(i * n_seq, (i + 1) * n_seq)
        cc = c[bs]
        cT = np.zeros((128, 16), np.float32)
        cT[:, : 8 * 2] = 0
        for b in range(n_seq):
            cT[:, b::2][:, :8] = cc[b].reshape(8, 128).T
        m = dict(shared)
        m.update(x=np.ascontiguousarray(x[bs]), cT=cT, pos=np.ascontiguousarray(pos[bs]))
        maps.append(m)
    return maps


def kernel(**inputs):
    nc = build(n_seq=2, layers=(0, 1))
    maps = make_in_maps(inputs, 8, 2)
    res = run_bass_kernel_spmd(nc, maps, core_ids=list(range(8)))
    return np.concatenate([r["out"] for r in res.results], axis=0).astype(np.float32)
```

```python
import math
from contextlib import ExitStack
import numpy as np
import concourse.bass as bass
import concourse.mybir as mybir
from concourse.bass_utils import run_bass_kernel_spmd

F32 = mybir.dt.float32
BF16 = mybir.dt.bfloat16
I32 = mybir.dt.int32
AF = mybir.ActivationFunctionType
ALU = mybir.AluOpType
AX = mybir.AxisListType

ENGS = ("pe", "act", "dve", "pool", "sp")
N_DMA_SEMS = 24

D = 1024
S = 2048
NT = 16
DFF = 2816
NJ = 22
EPS = 1e-6
NEGBIG = -1920.0
PI = math.pi


class Plan:
    def __init__(self, nc):
        self.nc = nc
        self.streams = {e: [] for e in ENGS}
        self.count = {e: 0 for e in ENGS}
        self.wm = {e: {} for e in ENGS}
        self.res = {}
        self.dma_cnt = [0] * N_DMA_SEMS
        self.dma_rr = {"pool": 0, "sp": 0, "act": 0}
        self.sems = {}

    def _need(self, eng, reads, writes):
        need = {}

        def add(tok):
            if tok is None:
                return
            k, v = tok
            if need.get(k, 0) < v:
                need[k] = v

        for r in reads:
            ent = self.res.get(r)
            if ent is not None:
                add(ent[0])
        for w in writes:
            ent = self.res.get(w)
            if ent is not None:
                add(ent[0])
                for k, v in ent[1].items():
                    add((k, v))
        out = []
        for k, v in need.items():
            if k == "pe" and eng == "pe":
                continue
            if self.wm[eng].get(k, 0) >= v:
                continue
            self.wm[eng][k] = v
            out.append((k, v))
        return out

    def _record(self, tok, reads, writes):
        k, v = tok
        for r in reads:
            ent = self.res.setdefault(r, [None, {}])
            if ent[1].get(k, 0) < v:
                ent[1][k] = v
        for w in writes:
            self.res[w] = [tok, {}]

    def op(self, eng, fn, reads=(), writes=(), signal=True):
        writes = list(writes) + [r for r in reads if r.startswith("ps")]
        reads = [r for r in reads if not r.startswith("ps")]
        waits = self._need(eng, reads, writes)
        if signal:
            self.count[eng] += 1
            tok = (eng, self.count[eng])
            inc = (eng, 1)
        else:
            tok = (eng, self.count[eng] + 1)
            inc = None
        self._record(tok, reads, writes)
        self.streams[eng].append((waits, fn, inc))
        return tok

    def dma(self, eng, fn, reads=(), writes=()):
        half = N_DMA_SEMS // 2
        base = 0 if eng == "pool" else half
        s = base + self.dma_rr[eng]
        self.dma_rr[eng] = (self.dma_rr[eng] + 1) % half
        key = "dma%d" % s
        waits = self._need(eng, reads, writes)
        prev = 16 * self.dma_cnt[s]
        if prev and self.wm[eng].get(key, 0) < prev:
            self.wm[eng][key] = prev
            waits.append((key, prev))
        self.dma_cnt[s] += 1
        tok = (key, 16 * self.dma_cnt[s])
        self._record(tok, reads, writes)
        self.streams[eng].append((waits, fn, (key, 16)))
        return tok

    def fence(self, new, old):
        merged = {}
        for o in old:
            ent = self.res.get(o)
            if ent is None:
                continue
            if ent[0] is not None:
                k, v = ent[0]
                merged[k] = max(merged.get(k, 0), v)
            for k, v in ent[1].items():
                merged[k] = max(merged.get(k, 0), v)
        for n in new:
            ent = self.res.get(n)
            m2 = dict(merged)
            if ent is not None:
                if ent[0] is not None:
                    k, v = ent[0]
                    m2[k] = max(m2.get(k, 0), v)
                for k, v in ent[1].items():
                    m2[k] = max(m2.get(k, 0), v)
            self.res[n] = [None, m2]

    def wait_tokens(self, eng, toks):
        waits = []
        for k, v in toks:
            if self.wm[eng].get(k, 0) < v:
                self.wm[eng][k] = v
                waits.append((k, v))
        self.streams[eng].append((waits, None, None))

    def all_tokens(self):
        toks = {}
        for ent in self.res.values():
            if ent[0] is not None:
                k, v = ent[0]
                toks[k] = max(toks.get(k, 0), v)
            for k, v in ent[1].items():
                toks[k] = max(toks.get(k, 0), v)
        return list(toks.items())

    def replay(self):
        nc = self.nc
        with ExitStack() as es:
            for e in ENGS:
                self.sems[e] = es.enter_context(nc.semaphore("s_" + e))
            for i in range(N_DMA_SEMS):
                self.sems["dma%d" % i] = es.enter_context(nc.semaphore("s_dma%d" % i))
            block = es.enter_context(nc.Block())
            sems = self.sems

            def run(engname):
                def body(eng):
                    for waits, fn, inc in self.streams[engname]:
                        for k, v in waits:
                            eng.wait_ge(sems[k], v)
                        if fn is None:
                            continue
                        ins = fn(eng)
                        if inc is not None:
                            ins.then_inc(sems[inc[0]], inc[1])
                return body

            block.tensor(run("pe"))
            block.scalar(run("act"))
            block.vector(run("dve"))
            block.gpsimd(run("pool"))
            block.sync(run("sp"))


class Ring:
    def __init__(self, items):
        self.items = list(items)
        self.i = 0

    def next(self):
        v = self.items[self.i]
        self.i = (self.i + 1) % len(self.items)
        return v


def build(n_seq=2, layers=(0, 1), dbg=(), stop=None):
    nc = bass.Bass("TRN2", target_bir_lowering=False)

    def din(name, shape, dt=F32):
        return nc.dram_tensor(name, list(shape), dt, kind="ExternalInput").ap()

    x_d = din("x", [n_seq, S, D])
    cT_d = din("cT", [128, 16])
    pos_d = din("pos", [n_seq, S], I32)
    w_adaT_d = din("w_adaT", [2, 48, 128, 8, 128])
    b_ada_d = din("b_ada", [2, 6 * D])
    badaT_d = din("badaT", [2, 128, 48])
    gcols_d = din("gcols", [2, 128, 17])
    g_post_mix_d = din("g_post_mix", [2, D])
    g_post_ffn_d = din("g_post_ffn", [2, D])
    w_inT_d = din("w_inT", [2, 48, 128, 8, 128])
    w_fgt_d = din("w_fgt", [2, D, 4])
    b_fgt_d = din("b_fgt", [2, 4])
    lam_d = [din(n, [2, 64]) for n in ("lam_q1", "lam_k1", "lam_q2", "lam_k2")]
    w_brT_d = [din("w_braT", [2, 8, 128, 4, 128]), din("w_brbT", [2, 8, 128, 2, 128]),
               din("w_brcT", [2, 8, 128, 2, 128])]
    w_outT_d = din("w_outT", [2, 4, 128, 8, 256])
    w_guT_d = din("w_guT", [2, 44, 128, 8, 128])
    w_dnT_d = din("w_dnT", [2, 8, 128, NJ, 128])
    invf_d = din("invf", [128, 1])
    rmat_d = din("rmat", [128, 128])
    ident_d = din("ident", [128, 128])
    mask_d = din("mask01", [128, 128])
    negmask_d = din("negmask", [128, 128])
    onehot_d = din("onehot", [8, S])
    out_d = nc.dram_tensor("out", [n_seq, S, D], F32, kind="ExternalOutput").ap()
    dbg_d = {}

    es = ExitStack()
    with es:
        def sb(name, shape, dt):
            return es.enter_context(nc.sbuf_tensor(name, list(shape), dt))

        X = sb("X", [128, NT, D], F32)
        hT = sb("hT", [128, 8, S], BF16)
        R12 = sb("R12", [128, 24576], BF16)
        R3 = sb("R3", [128, 8192], BF16)
        tabC = sb("tabC", [128, S], BF16)
        tabS = sb("tabS", [128, S], BF16)
        NSLOT = 3
        wslot = [sb("wslot%d" % i, [128, 2816], BF16) for i in range(NSLOT)]
        G = sb("G", [128, D], F32)
        SCR = sb("SCR", [128, 2048], F32)
        ident_b = sb("ident_b", [128, 128], BF16)
        negmask_b = sb("negmask_b", [128, 128], BF16)
        rmat_b = sb("rmat_b", [128, 128], BF16)
        ones_b = sb("ones_b", [128, 128], BF16)
        U_f = sb("U_f", [128, 128], F32)
        ones_f = sb("ones_f", [128, 128], F32)
        invf = sb("invf_s", [128, 1], F32)
        halfpi = sb("halfpi", [128, 1], F32)
        modc1 = sb("modc1", [128, 2, 48], F32)
        cTs = sb("cTs", [128, 16], F32)
        ca = sb("ca", [128, 16], BF16)
        cbc = sb("cbc", [128, 8, 128], BF16)
        badaT = sb("badaT_s", [128, 48], F32)
        gcols = sb("gcols_s", [128, 17], F32)
        modc = sb("modc", [128, 48], F32)
        scsh = sb("scsh", [128, 32], F32)
        ss = sb("ss", [128, 16], F32)
        lnv = sb("lnv", [128, 16], F32)
        rstd = sb("rstd", [128, 16], F32)
        lamt = sb("lamt", [128, 4, 64], F32)
        lamp = sb("lamp", [128, 64], F32)
        lams = sb("lams", [128, 4], F32)
        neglam = sb("neglam", [128, 1], F32)
        gcol = sb("gcol", [128, 1], F32)
        bfrep = sb("bfrep", [128, 4], F32)
        zb = sb("zb", [128, 4, NT], F32)
        lf = sb("lf", [128, 4, NT], F32)
        tots = sb("tots", [128, 4, NT], F32)
        offs = sb("offs", [128, 4, NT], F32)
        Lcum = sb("Lcum", [128, 4, NT], F32)
        r8 = sb("r8", [128, 4, NT], BF16)
        kmT = sb("kmT", [128, 2, 8], BF16)
        kms = sb("kms", [128, 2, 8], F32)
        gm = sb("gm", [128, 2, 8, 8], F32)
        mx8 = sb("mx8", [128, 8], F32)
        selb = sb("selb", [128, 2, 8, 8], BF16)
        ssq = sb("ssq", [128, 8, 8], F32)
        ssu = sb("ssu", [128, 8], F32)
        lnvu = sb("lnvu", [128, 8], F32)
        rstdu = sb("rstdu", [128, 8], F32)
        junk2 = sb("junk2", [128, 4, 256], BF16)
        junk_ring = Ring([0, 1, 2, 3])

        ps = [es.enter_context(nc.psum_tensor("ps%d" % i, [128, 512], F32)) for i in range(8)]

        oT = R12[:, 0:16384].rearrange("p (c t) -> p c t", t=S)
        qk = [R12[:, 16384 + i * S: 16384 + (i + 1) * S] for i in range(4)]
        mg = R12[:, 16384:24576].rearrange("p (c t) -> p c t", t=1024)
        actT = R12[:, 0:22528].rearrange("p (c t) -> p c t", t=1024)

        Vd1 = R3[:, 0:2048].rearrange("p (t c) -> p t c", c=128)
        Vs = R3[:, 0:4096].rearrange("p (t j c) -> p t j c", j=2, c=128)
        PT = [R3[:, 4096 + i * 512: 4096 + (i + 1) * 512] for i in range(8)]
        ybuf = R3[:, 0:8192].rearrange("p (t c) -> p t c", c=1024)
        posi = R3[:, 0:4096].bitcast(I32)
        T = [SCR[:, i * 512:(i + 1) * 512] for i in range(4)]
        Ti = [T[i].bitcast(I32) for i in range(4)]
        xn = [T[2].bitcast(BF16), T[3].bitcast(BF16)]
        XN_NAMES = ["T2", "T3"]
        junkA = junk2[:].rearrange("p a b -> p (a b)")
        deferred = []

        def run_deferred(ring):
            while deferred:
                deferred.pop(0)(ring)

        N_O = ["o%d_%d" % (c, t) for c in range(8) for t in range(4)]
        N_AC = ["ac0", "ac1"]
        N_QK = ["qk0", "qk1", "qk2", "qk3"]
        N_MG = ["mg0", "mg1"]
        N_XN = ["xn0", "xn1"]
        N_V = ["V"]
        N_PT = ["pt%d" % i for i in range(8)]
        N_YB = ["yb%d" % i for i in range(8)]
        N_POSI = ["posi"]
        REG_A = N_O + N_AC
        REG_B = N_QK + N_MG + N_AC
        REG_C = N_V + N_PT + N_YB + N_POSI
        N_T = ["T0", "T1", "T2", "T3"]
        WNAMES = [["w%d_%d" % (s_, j) for j in range(4)] for s_ in range(NSLOT)]

        P = Plan(nc)

        def claim(names):
            for reg in (REG_A, REG_B, REG_C):
                mine = [n for n in names if n in reg]
                if mine:
                    P.fence(mine, [n for n in reg if n not in mine])

        def mm(out, lhsT, rhs, start, stop, reads, writes, signal):
            P.op("pe", lambda e: e.matmul(out, lhsT=lhsT, rhs=rhs, start=start, stop=stop),
                 reads, writes, signal)

        def tr(out, in_, reads, writes, signal=True):
            P.op("pe", lambda e: e.transpose(out=out, in_=in_, identity=ident_b[:]), reads, writes, signal)

        def act(out, in_, func, reads, writes, bias=None, scale=None, accum=None):
            kw = {}
            if bias is not None:
                kw["bias"] = bias
            if scale is not None:
                kw["scale"] = scale
            if accum is not None:
                kw["accum_out"] = accum
            P.op("act", lambda e: e.activation(out=out, in_=in_, func=func, **kw), reads, writes)

        def tt(out, in0, in1, op, reads, writes, eng="dve"):
            P.op(eng, lambda e: e.tensor_tensor(out=out, in0=in0, in1=in1, op=op), reads, writes)

        def ts(out, in0, s1, s2, op0, op1, reads, writes, eng="dve"):
            if op1 is None:
                P.op(eng, lambda e: e.tensor_scalar(out=out, in0=in0, scalar1=s1, scalar2=None, op0=op0), reads, writes)
            else:
                P.op(eng, lambda e: e.tensor_scalar(out=out, in0=in0, scalar1=s1, scalar2=s2, op0=op0, op1=op1), reads, writes)

        def stt(out, in0, scalar, in1, op0, op1, reads, writes, eng="dve"):
            P.op(eng, lambda e: e.scalar_tensor_tensor(out=out, in0=in0, scalar=scalar, in1=in1, op0=op0, op1=op1),
                 reads, writes)

        def cp(out, in_, reads, writes, eng="dve"):
            P.op(eng, lambda e: e.tensor_copy(out=out, in_=in_), reads, writes)

        def recip(out, in_, reads, writes):
            act(out, in_, AF.Ln, reads, writes)
            act(out, out, AF.Exp, list(writes), writes, scale=-1.0)

        def memset(ap, val, writes, eng="dve"):
            P.op(eng, lambda e: e.memset(ap, val), (), writes)

        def dma(eng, out, in_, reads, writes):
            P.dma(eng, lambda e: e.dma_start(out=out, in_=in_), reads, writes)

        def dump(name, ap, reads):
            if name not in dbg:
                return
            d = nc.dram_tensor("dbg_" + name, list(ap.shape), ap.dtype, kind="ExternalOutput").ap()
            dbg_d[name] = d
            dma("sp", d, ap, reads, ())

        bank_main = Ring([0, 1, 2, 3])
        bank_aux = Ring([4, 5, 6, 7])
        pt_ring = Ring(list(range(8)))
        slot_ring = Ring(list(range(NSLOT)))

        def wload(pieces):
            s_ = slot_ring.next()
            views = []
            off = 0
            for j, src in enumerate(pieces):
                kc_, n_ = src.shape[1], src.shape[2]
                v = wslot[s_][:, off:off + kc_ * n_].rearrange("p (k n) -> p k n", n=n_)
                off += kc_ * n_
                dma("pool", v, src, (), [WNAMES[s_][j]])
                views.append(v)
            assert off <= 2816
            return views, WNAMES[s_]

        dma("pool", ident_b[:], ident_d, (), ["ident_b"])
        dma("pool", negmask_b[:], negmask_d, (), ["negmask_b"])
        dma("pool", rmat_b[:], rmat_d, (), ["rmat_b"])
        dma("sp", U_f[:], mask_d, (), ["U_f"])
        dma("sp", invf[:], invf_d, (), ["invf"])
        dma("sp", cTs[:], cT_d, (), ["cTs"])
        memset(ones_b[:], 1.0, ["ones_b"])
        memset(halfpi[:], PI / 2, ["halfpi"])
        memset(ones_f[:], 1.0, ["ones_f"])
        memset(selb[:], 0.0, ["selb"])
        act(ca[:], cTs[:], AF.Silu, ["cTs"], ["ca"])

        def rmsn(ss_ap, lnv_ap, rstd_ap, in_names, ln_name, out_name, scale):
            act(lnv_ap, ss_ap, AF.Ln, list(in_names), [ln_name], bias=EPS, scale=scale)
            act(rstd_ap, lnv_ap, AF.Exp, [ln_name], [out_name], scale=-0.5)

        def rope_tables(s):
            claim(N_POSI)
            dma("sp", posi, pos_d[s:s + 1, :].to_broadcast([128, S]), (), ["posi"])
            for j in range(4):
                cs = slice(j * 512, (j + 1) * 512)
                ts(T[0], posi[:, cs], invf[:, 0:1], None, ALU.mult, None, ["posi", "invf"], ["T0"])
                ts(Ti[1], T[0], 1.0 / (2 * PI), None, ALU.mult, None, ["T0"], ["T1"])
                stt(T[2], Ti[1], -2 * PI, T[0], ALU.mult, ALU.add, ["T1", "T0"], ["T2"])
                ts(T[3], T[2], PI, -2 * PI, ALU.is_gt, ALU.mult, ["T2"], ["T3"])
                tt(T[2], T[2], T[3], ALU.add, ["T2", "T3"], ["T2"])
                ts(T[2], T[2], -PI, PI, ALU.max, ALU.min, ["T2"], ["T2"])
                act(tabS[:, cs], T[2], AF.Sin, ["T2"], ["tab0_%d" % j])
                stt(T[3], T[2], -1.0, T[2], ALU.mult, ALU.max, ["T2"], ["T3"])
                act(tabC[:, cs], T[3], AF.Sin, ["T3", "halfpi"], ["tab1_%d" % j], bias=halfpi[:, 0:1], scale=-1.0)
            dump("tabC", tabC[:], ["tab1_%d" % j for j in range(4)])
            dump("tabS", tabS[:], ["tab0_%d" % j for j in range(4)])

        def adaln_cols(s, l):
            dma("sp", badaT[:], badaT_d[l], (), ["badaT"])
            dma("sp", gcols[:], gcols_d[l], (), ["gcols"])
            if s == 0:
                b = bank_aux.next()
                chunks = list(range(0, 16)) + list(range(24, 40))
                for i in range(0, len(chunks), 2):
                    j0 = chunks[i]
                    wv, wn = wload([w_adaT_d[l, j0], w_adaT_d[l, j0 + 1]])
                    for jj in range(2):
                        j = j0 + jj
                        for kc in range(8):
                            mm(ps[b][:, 2 * j:2 * j + n_seq], wv[jj][:, kc, :],
                               ca[:, kc * 2: kc * 2 + n_seq], kc == 0, kc == 7, wn + ["ca"], ["ps%d" % b], kc == 7)
                psv = ps[b][:, 0:96].rearrange("p (j b) -> p j b", b=2)
                for (a0, a1) in ((0, 16), (24, 40)):
                    tt(modc[:, a0:a1], psv[:, a0:a1, 0], badaT[:, a0:a1], ALU.add, ["ps%d" % b, "badaT", "modc"], ["modc"])
                    if n_seq > 1:
                        tt(modc1[:, l, a0:a1], psv[:, a0:a1, 1], badaT[:, a0:a1], ALU.add,
                           ["ps%d" % b, "badaT", "modc1_%d" % l], ["modc1_%d" % l])
            else:
                for (a0, a1) in ((0, 16), (24, 40)):
                    cp(modc[:, a0:a1], modc1[:, l, a0:a1], ["modc1_%d" % l, "modc"], ["modc"])
            stt(scsh[:, 0:8], modc[:, 8:16], 1.0, gcols[:, 0:8], ALU.add, ALU.mult, ["modc", "gcols"], ["scsh"])
            cp(scsh[:, 8:16], modc[:, 0:8], ["modc"], ["scsh"])
            stt(scsh[:, 16:24], modc[:, 32:40], 1.0, gcols[:, 8:16], ALU.add, ALU.mult, ["modc", "gcols"], ["scsh"])
            cp(scsh[:, 24:32], modc[:, 24:32], ["modc"], ["scsh"])
            ts(gcol[:], gcols[:, 16:17], 1.0 - lam_init(l), None, ALU.mult, None, ["gcols"], ["gcol"])
            for i in range(4):
                dma("sp", lamt[:, i, :], lam_d[i][l:l + 1, :].to_broadcast([128, 64]), (), ["lamt%d" % i])
            for i in range(2):
                tt(lamp[:], lamt[:, 2 * i, :], lamt[:, 2 * i + 1, :], ALU.mult,
                   ["lamt%d" % (2 * i), "lamt%d" % (2 * i + 1)], ["lamp"])
                P.op("dve", (lambda i_: lambda e: e.reduce_sum(out=lams[:, i_:i_ + 1], in_=lamp[:], axis=AX.X))(i),
                     ["lamp"], ["lams%d" % i])
            act(lams[:, 2:4], lams[:, 0:2], AF.Exp, ["lams0", "lams1"], ["lamse"])
            tt(neglam[:], lams[:, 3:4], lams[:, 2:3], ALU.subtract, ["lamse"], ["neglam"])
            ts(neglam[:], neglam[:], -lam_init(l), None, ALU.add, None, ["neglam"], ["neglam"])
            dma("sp", bfrep[:], b_fgt_d[l:l + 1, :].to_broadcast([128, 4]), (), ["bfrep"])

        def adaln_G(s, l, which):
            c0 = 2048 if which == 0 else 5120
            gp = g_post_mix_d if which == 0 else g_post_ffn_d
            brep = SCR[:, 0:1024]
            grep = SCR[:, 1024:2048]
            dma("sp", brep, b_ada_d[l:l + 1, c0:c0 + 1024].to_broadcast([128, 1024]), (), ["T0", "T1"])
            dma("sp", grep, gp[l:l + 1, :].to_broadcast([128, 1024]), (), ["T2", "T3"])
            for q in range(4):
                jq = c0 // 128 + 2 * q
                wv, wn = wload([w_adaT_d[l, jq], w_adaT_d[l, jq + 1]])
                b = bank_main.next()
                for jj in range(2):
                    for kc in range(8):
                        mm(ps[b][:, jj * 128:(jj + 1) * 128], cbc[:, kc, :], wv[jj][:, kc, :], kc == 0, kc == 7,
                           wn + ["cbc"], ["ps%d" % b], kc == 7)
                cs = slice(q * 256, (q + 1) * 256)
                tt(G[:, cs], ps[b][:, 0:256], brep[:, cs], ALU.add, ["ps%d" % b, "T0", "T1"], ["G%d" % q])
                tt(G[:, cs], G[:, cs], grep[:, cs], ALU.mult, ["G%d" % q, "T2", "T3"], ["G%d" % q])

        N_G = ["G0", "G1", "G2", "G3"]

        def stage_a(tiles, off):
            for _ in stage_a_gen(tiles, off):
                pass

        def stage_a_gen(tiles, off):
            ngrp = len(tiles) // 4

            def sq(g):
                grp = tiles[4 * g:4 * g + 4]
                c0 = grp[0]
                names = ["ss%d" % t for t in grp]
                memset(ss[:, c0:c0 + 4], 0.0, names)
                for t in grp:
                    act(junkA, X[:, t, :], AF.Square, ["X%d" % t], ["junk0", "junk1", "junk2", "junk3", "ss%d" % t],
                        accum=ss[:, t:t + 1])
                rmsn(ss[:, c0:c0 + 4], lnv[:, c0:c0 + 4], rstd[:, c0:c0 + 4], names, "lnv%d" % (c0 // 4),
                     "rstd%d" % (c0 // 4), 1.0 / D)

            sq(0)
            for g in range(ngrp):
                if g + 1 < ngrp:
                    sq(g + 1)
                grp = tiles[4 * g:4 * g + 4]
                tc = grp[0] // 4
                banks = [bank_main.next(), bank_main.next(), bank_aux.next(), bank_aux.next()]
                for i, t in enumerate(grp):
                    ts(xn[i % 2], X[:, t, :], rstd[:, t:t + 1], None, ALU.mult, None,
                       ["X%d" % t, "rstd%d" % tc], [XN_NAMES[i % 2]])
                    for kc in range(8):
                        b = banks[kc // 2]
                        pv = ps[b][:].bitcast(BF16)
                        o0 = (kc % 2) * 512 + i * 128
                        tr(pv[:, o0:o0 + 128], xn[i % 2][:, kc * 128:(kc + 1) * 128],
                           [XN_NAMES[i % 2], "ident_b"], ["ps%d" % b], signal=(kc == 7))
                for kc in range(8):
                    b = banks[kc // 2]
                    pv = ps[b][:].bitcast(BF16)
                    o0 = (kc % 2) * 512
                    if kc % 2 == 0:
                        act(hT[:, kc, tc * 512:(tc + 1) * 512], pv[:, o0:o0 + 512], AF.Identity,
                            ["ps%d" % b, "scsh"], ["hT%d" % tc],
                            bias=scsh[:, off + 8 + kc: off + 9 + kc], scale=scsh[:, off + kc: off + kc + 1])
                    else:
                        ts(hT[:, kc, tc * 512:(tc + 1) * 512], pv[:, o0:o0 + 512],
                           scsh[:, off + kc: off + kc + 1], scsh[:, off + 8 + kc: off + 9 + kc], ALU.mult, ALU.add,
                           ["ps%d" % b, "scsh"], ["hT%d" % tc])
                yield g

        def proj_F(wp, wn, rope, dests):
            step, fin = proj_F_steps(wp, wn, rope, dests)
            for tc in range(4):
                step(tc)
            fin()

        def proj_F_steps(wp, wn, rope, dests):
            tail = [None]

            def step(tc):
                cs = slice(tc * 512, (tc + 1) * 512)
                b = bank_main.next()
                for kc in range(8):
                    mm(ps[b][:], wp[:, kc, :], hT[:, kc, cs], kc == 0, kc == 7,
                       wn + ["hT%d" % tc], ["ps%d" % b], kc == 7)
                if tc == 0:
                    run_deferred(bank_aux)
                if not rope:
                    for (r0, nr, tile, d0, nm) in dests:
                        cp(tile[d0:d0 + nr, cs], ps[b][r0:r0 + nr, :], ["ps%d" % b], [nm])
                else:
                    pi = pt_ring.next()
                    act(PT[pi], ps[b][:], AF.Identity, ["ps%d" % b], ["pt%d" % pi])
                    if tail[0] is not None:
                        tail[0]()

                    def mk(b=b, pi=pi, cs=cs, tc=tc):
                        b2 = bank_aux.next()
                        mm(ps[b2][:], rmat_b[:], PT[pi], True, True, ["rmat_b", "pt%d" % pi], ["ps%d" % b2], True)
                        tt(T[0], ps[b][:], tabC[:, cs], ALU.mult, ["ps%d" % b, "tab1_%d" % tc], ["T0"])
                        tt(T[1], ps[b2][:], tabS[:, cs], ALU.mult, ["ps%d" % b2, "tab0_%d" % tc], ["T1"])
                        for (r0, nr, tile, d0, nm) in dests:
                            tt(tile[d0:d0 + nr, cs], T[0][r0:r0 + nr, :], T[1][r0:r0 + nr, :], ALU.add,
                               ["T0", "T1"], [nm])
                    tail[0] = mk

            def fin():
                if tail[0] is not None:
                    tail[0]()
                    tail[0] = None
            return step, fin

        def proj_V(wp, wn, N, evac, extra=None, tiles=None, ring=None):
            for t in (range(NT) if tiles is None else tiles):
                b = (bank_main if ring is None else ring).next()
                for kc in range(8):
                    mm(ps[b][:, 0:N], hT[:, kc, t * 128:(t + 1) * 128], wp[:, kc, 0:N], kc == 0, kc == 7,
                       wn + ["hT%d" % (t // 4)], ["ps%d" % b], kc == 7)
                if extra is not None:
                    wp2, n2 = extra
                    for kc in range(8):
                        mm(ps[b][:, N:N + n2], hT[:, kc, t * 128:(t + 1) * 128], wp2[:, kc, :], kc == 0, kc == 7,
                           wn + ["hT%d" % (t // 4)], ["ps%d" % b], kc == 7)
                evac(t, b)

        def attend_chunk(qc, maps, st_ring, hook=None):
            steps = []
            nk = 4 * qc + 4
            for kt in range(nk):
                for mi, m in enumerate(maps):
                    steps.append((mi, kt, nk))
            LAG = 2 * len(maps)
            pend = []
            nstep = [0]

            def emit_pv(st):
                mi, kt, nk, pi, c0 = st
                m = maps[mi]
                for (ab, lfn, rds) in m["pv"]:
                    mm(ps[ab][:, c0:512], lfn(kt), PT[pi][:, c0:512], kt == 0, kt == nk - 1,
                       ["pt%d" % pi] + rds, ["ps%d" % ab], True)

            for (mi, kt, nk) in steps:
                m = maps[mi]
                j = kt - 4 * qc
                c0 = max(j, 0) * 128
                b = st_ring.next()
                diag = j >= 0
                mm(ps[b][:, c0:512], m["k"][:, kt * 128:(kt + 1) * 128], m["q"][:, qc * 512 + c0:(qc + 1) * 512],
                   True, not diag, m["rn"], ["ps%d" % b], not diag)
                if diag:
                    mm(ps[b][:, c0:c0 + 128], ident_b[:], negmask_b[:], False, True,
                       ["ident_b", "negmask_b"], ["ps%d" % b], True)
                pi = pt_ring.next()
                bias_ap, bias_names = m["bias"](kt)
                act(PT[pi][:, c0:512], ps[b][:, c0:512], AF.Exp, ["ps%d" % b] + bias_names, ["pt%d" % pi],
                    bias=bias_ap, scale=0.125)
                pend.append((mi, kt, nk, pi, c0))
                if len(pend) > LAG:
                    emit_pv(pend.pop(0))
                nstep[0] += 1
                if nstep[0] == 4:
                    run_deferred(st_ring)
            while pend:
                emit_pv(pend.pop(0))

        ACC6 = [2, 3, 4, 5, 6, 7]
        diff_chunk_ctr = [0]
        st2 = Ring([0, 1])
        st4 = Ring([0, 1, 2, 3])

        diff_pending = [None]

        def attend_diff(h):
            for qc in range(4):
                c = diff_chunk_ctr[0]
                diff_chunk_ctr[0] += 1
                A = [ACC6[(4 * c) % 6], ACC6[(4 * c + 2) % 6]]
                Sm = [ACC6[(4 * c + 1) % 6], ACC6[(4 * c + 3) % 6]]
                for m_ in range(2):
                    mp = dict(q=qk[m_][:, :], k=qk[2][:, :], rn=["qk%d" % m_, "qk2"],
                              bias=lambda kt: (None, []),
                              pv=[(A[m_], (lambda kt: Vd1[:, kt, :]), ["V"]),
                                  (Sm[m_], (lambda kt: ones_b[:]), ["ones_b"])])
                    attend_chunk(qc, [mp], st2)
                cs = slice(qc * 512, (qc + 1) * 512)
                a0, s0, a1, s1 = A[0], Sm[0], A[1], Sm[1]
                recip(T[0], ps[s0][:], ["ps%d" % s0], ["T0"])
                tt(T[1], ps[a0][:], T[0], ALU.mult, ["ps%d" % a0, "T0"], ["T1"])
                recip(T[0], ps[s1][:], ["ps%d" % s1], ["T0"])
                tt(T[2], ps[a1][:], T[0], ALU.mult, ["ps%d" % a1, "T0"], ["T2"])
                stt(T[3], T[2], neglam[:, 0:1], T[1], ALU.mult, ALU.add, ["T2", "T1", "neglam"], ["T3"])
                sqv = T[2].bitcast(BF16)[:, 0:512]
                tt(sqv, T[3], T[3], ALU.mult, ["T3", "T2"], ["T2"])

                def part2(ring, sqv=sqv, h=h, qc=qc, cs=cs):
                    b = ring.next()
                    mm(ps[b][:], ones_b[:], sqv, True, True, ["ones_b", "T2"], ["ps%d" % b], True)
                    act(T[0], ps[b][:], AF.Ln, ["ps%d" % b], ["T0"], bias=EPS, scale=1.0 / 128)
                    act(T[1], T[0], AF.Exp, ["T0"], ["T1"], scale=-0.5)
                    stt(oT[:, h, cs], T[3], gcol[:, 0:1], T[1], ALU.mult, ALU.mult, ["T3", "T1", "gcol"],
                        ["o%d_%d" % (h, qc)])
                deferred.append(part2)

        def attend_single(pair, chunk, R, bias_fn):
            for hl in range(2):
                h = 2 * pair + hl
                for qc in range(4):
                    ab = 4 + (hl * 4 + qc) % 4
                    maps = [dict(q=qk[hl][0:R, :], k=qk[2 + hl][0:R, :], rn=["qk%d" % hl, "qk%d" % (2 + hl)],
                                 bias=(lambda kt, h_=h: bias_fn(h_, kt)),
                                 pv=[(ab, (lambda kt, hl_=hl: Vs[:, kt, hl_, :]), ["V"])])]
                    attend_chunk(qc, maps, st4)
                    cs = slice(qc * 512, (qc + 1) * 512)
                    P.op("dve", (lambda ab_: lambda e: e.reciprocal(out=T[0][64:128, :], in_=ps[ab_][64:128, :]))(ab),
                         (), ["ps%d" % ab, "T0"])
                    d0 = hl * 64
                    tt(oT[d0:d0 + 64, chunk, cs], ps[ab][0:64, :], T[0][64:128, :], ALU.mult,
                       ["ps%d" % ab, "T0"], ["o%d_%d" % (chunk, qc)])

        def branch_diff(l, h, sa_gen=None):
            claim(N_QK + N_V + N_PT)
            wqk, nqk = wload([w_inT_d[l, h], w_inT_d[l, 4 + h]])
            wvv, nv = wload([w_inT_d[l, 8 + h]])
            if h == 0:
                memset(qk[0][64:128, :], 0.0, ["qk0"])
                memset(qk[1][0:64, :], 0.0, ["qk1"])
            def evac(t, b):
                cp(Vd1[:, t, :], ps[b][:, 0:128], ["ps%d" % b], ["V"])
            dq = [(0, 64, qk[0], 0, "qk0"), (64, 64, qk[1], 64, "qk1")]
            dk = [(0, 128, qk[2], 0, "qk2")]
            if sa_gen is None:
                proj_F(wqk[0], nqk, True, dq)
                proj_F(wqk[1], nqk, True, dk)
                proj_V(wvv[0], nv, 128, evac)
            else:
                sq_, fq_ = proj_F_steps(wqk[0], nqk, True, dq)
                sk_, fk_ = proj_F_steps(wqk[1], nqk, True, dk)
                for tc in range(4):
                    next(sa_gen)
                    sq_(tc)
                    sk_(tc)
                    proj_V(wvv[0], nv, 128, evac, tiles=range(4 * tc, 4 * tc + 4), ring=bank_aux)
                    fq_()
                    fk_()
                for _ in sa_gen:
                    pass
            if h == 0:
                dump("qd0", qk[0], ["qk0"])
                dump("kd0", qk[2], ["qk2"])
            attend_diff(h)

        def branch_fox(l, pair):
            claim(N_QK + N_V + N_PT)
            wqk, nqk = wload([w_inT_d[l, 12 + pair], w_inT_d[l, 14 + pair]])
            pieces = [w_inT_d[l, 16 + pair]]
            if pair == 0:
                pieces.append(w_fgt_d[l].rearrange("(k p) n -> p k n", p=128))
            wvv, nv = wload(pieces)
            proj_F(wqk[0], nqk, False, [(0, 64, qk[0], 0, "qk0"), (64, 64, qk[1], 0, "qk1")])
            proj_F(wqk[1], nqk, False, [(0, 64, qk[2], 0, "qk2"), (64, 64, qk[3], 0, "qk3")])
            memset(Vs[:, :, :, 64:128], 1.0, ["V"], eng="pool")
            for i in range(4):
                memset(qk[i][64:128, :], 0.0, ["qk%d" % i], eng="pool")
            memset(qk[2][64:65, :], 1.0, ["qk2"], eng="pool")
            memset(qk[3][64:65, :], 1.0, ["qk3"], eng="pool")

            def evac(t, b):
                cp(Vs[:, t, :, 0:64], ps[b][:, 0:128].rearrange("p (j c) -> p j c", c=64), ["ps%d" % b], ["V"])
                if pair == 0:
                    tt(zb[:, :, t], ps[b][:, 128:132], bfrep[:], ALU.add, ["ps%d" % b, "bfrep"], ["zb"])
            proj_V(wvv[0], nv, 128, evac, extra=((wvv[1], 4) if pair == 0 else None))
            if pair == 0:
                act(lf[:], zb[:], AF.Exp, ["zb"], ["lf"], scale=-1.0)
                act(lf[:], lf[:], AF.Ln, ["lf"], ["lf"], bias=1.0)
                b1 = bank_aux.next()
                lf2 = lf[:].rearrange("p h t -> p (h t)")
                mm(ps[b1][:, 0:64], U_f[:], lf2, True, True, ["U_f", "lf"], ["ps%d" % b1], True)
                b2 = bank_aux.next()
                mm(ps[b2][:, 0:64], ones_f[:], lf2, True, True, ["ones_f", "lf"], ["ps%d" % b2], True)
                cp(tots[:].rearrange("p h t -> p (h t)"), ps[b2][:, 0:64], ["ps%d" % b2], ["tots"])
                memset(offs[:, :, 0:1], 0.0, ["offs"])
                for i in range(1, NT):
                    tt(offs[:, :, i:i + 1], offs[:, :, i - 1:i], tots[:, :, i - 1:i], ALU.add, ["offs", "tots"], ["offs"])
                tt(Lcum[:].rearrange("p h t -> p (h t)"), ps[b1][:, 0:64], offs[:].rearrange("p h t -> p (h t)"),
                   ALU.add, ["ps%d" % b1, "offs"], ["Lcum"])
                ts(r8[:], Lcum[:], -8.0, None, ALU.mult, None, ["Lcum"], ["r8"])
                dump("Lcum", Lcum[:], ["Lcum"])
            for hl in range(2):
                h = 2 * pair + hl
                for g in range(4):
                    b = bank_aux.next()
                    for i in range(4):
                        t = 4 * g + i
                        mm(ps[b][0:1, i * 128:(i + 1) * 128], r8[:, h, t:t + 1], ident_b[:], True, True,
                           ["r8", "ident_b"], ["ps%d" % b], i == 3)
                    act(qk[hl][64:65, g * 512:(g + 1) * 512], ps[b][0:1, :], AF.Identity, ["ps%d" % b], ["qk%d" % hl])
            if pair == 0:
                dump("qf0", qk[0], ["qk0"])
                dump("kf0", qk[2], ["qk2"])

            def bias_fn(h, kt):
                return Lcum[:, h, kt:kt + 1], ["Lcum"]
            attend_single(pair, 4 + pair, 128, bias_fn)

        def branch_moba(l, pair):
            claim(N_QK + N_V + N_PT)
            wqk, nqk = wload([w_inT_d[l, 18 + pair], w_inT_d[l, 20 + pair]])
            wvv, nv = wload([w_inT_d[l, 22 + pair]])
            proj_F(wqk[0], nqk, True, [(0, 64, qk[0], 0, "qk0"), (64, 64, qk[1], 0, "qk1")])
            proj_F(wqk[1], nqk, True, [(0, 64, qk[2], 0, "qk2"), (64, 64, qk[3], 0, "qk3")])
            memset(Vs[:, :, :, 64:128], 1.0, ["V"], eng="pool")
            for i in range(4):
                memset(qk[i][64:128, :], 0.0, ["qk%d" % i], eng="pool")
            for hl in range(2):
                dma("pool", qk[2 + hl][64:72, :], onehot_d, (), ["qk%d" % (2 + hl)])

            def evac(t, b):
                cp(Vs[:, t, :, 0:64], ps[b][:, 0:128].rearrange("p (j c) -> p j c", c=64), ["ps%d" % b], ["V"])
            proj_V(wvv[0], nv, 128, evac)
            bg = bank_aux.next()
            for hl in range(2):
                P.op("dve", (lambda hl_: lambda e: e.tensor_reduce(
                    out=kms[0:64, hl_, :], in_=qk[2 + hl_][0:64, :].rearrange("p (n k) -> p n k", k=256),
                    axis=AX.X, op=ALU.add))(hl), ["qk%d" % (2 + hl)], ["kms%d" % hl])
                ts(kmT[0:64, hl, :], kms[0:64, hl, :], 1.0 / 256, None, ALU.mult, None, ["kms%d" % hl], ["kmT%d" % hl])
                for i in range(8):
                    qt = 8 + i
                    c = (hl * 8 + i) * 8
                    mm(ps[bg][:, c:c + 8], qk[hl][0:64, qt * 128:(qt + 1) * 128], kmT[0:64, hl, :], True, True,
                       ["qk%d" % hl, "kmT%d" % hl], ["ps%d" % bg], (hl == 1 and i == 7))
            memset(gm[:], -1e30, ["gm"])
            for hl in range(2):
                for i in range(8):
                    own = (8 + i) // 2
                    c = (hl * 8 + i) * 8
                    cp(gm[:, hl, i, 0:own], ps[bg][:, c:c + own], ["ps%d" % bg, "gm"], ["gm"])
            for hl in range(2):
                for i in range(8):
                    own = (8 + i) // 2
                    P.op("dve", (lambda hl_, i_: lambda e: e.max(out=mx8[:], in_=gm[:, hl_, i_, :]))(hl, i),
                         ["gm"], ["mx8"])
                    ts(selb[:, hl, i, 0:own], gm[:, hl, i, 0:own], mx8[:, 2:3], NEGBIG, ALU.is_lt, ALU.mult,
                       ["gm", "mx8"], ["selb"])
            for i in range(8):
                qt = 8 + i
                b = bank_aux.next()
                for hl in range(2):
                    mm(ps[b][0:8, hl * 128:(hl + 1) * 128], selb[:, hl, i, :], ident_b[:], True, True,
                       ["selb", "ident_b"], ["ps%d" % b], hl == 1)
                for hl in range(2):
                    act(qk[hl][64:72, qt * 128:(qt + 1) * 128], ps[b][0:8, hl * 128:(hl + 1) * 128], AF.Identity,
                        ["ps%d" % b], ["qk%d" % hl])
            if pair == 0:
                dump("qm0", qk[0], ["qk0"])
                dump("km0", qk[2], ["qk2"])

            def bias_fn(h, kt):
                return None, []
            attend_single(pair, 6 + pair, 128, bias_fn)

        def update_x(half, nslab):
            unames = ["ssu%d" % i for i in range(8)]
            P.op("dve", lambda e: e.reduce_sum(out=ssu[:, 0:8], in_=ssq[:, :, 0:nslab], axis=AX.X),
                 ["ssq%d" % t for t in range(8)], unames)
            rmsn(ssu[:], lnvu[:], rstdu[:], unames, "lnvu", "rstdu", 1.0 / D)
            for t in range(8):
                tg = half * 8 + t
                stt(X[:, tg, :], ybuf[:, t, :], rstdu[:, t:t + 1], X[:, tg, :], ALU.mult, ALU.add,
                    ["yb%d" % t, "rstdu", "X%d" % tg], ["X%d" % tg])

        def evac_y(b, t, sl, ncol):
            cs = slice(sl * ncol, (sl + 1) * ncol)
            ji = junk_ring.next()
            act(junk2[:, ji, 0:ncol], ps[b][:, 0:ncol], AF.Square, ["ps%d" % b], ["ssq%d" % t, "junk%d" % ji],
                accum=ssq[:, t, sl:sl + 1])
            tt(ybuf[:, t, cs], ps[b][:, 0:ncol], G[:, cs], ALU.mult, ["ps%d" % b] + N_G, ["yb%d" % t])

        def merge_half(l, half):
            claim(N_MG)
            for fc in range(8):
                wa, na = wload([w_inT_d[l, 24 + fc], w_inT_d[l, 32 + fc]])
                wb, nb = wload([w_inT_d[l, 40 + fc], w_brT_d[0][l, fc], w_brT_d[1][l, fc], w_brT_d[2][l, fc]])
                for tcl in range(2):
                    tc = half * 2 + tcl
                    cs = slice(tc * 512, (tc + 1) * 512)
                    csl = slice(tcl * 512, (tcl + 1) * 512)
                    for br in range(3):
                        bgk = bank_main.next()
                        wsrc, wnm = (wa[0], na) if br == 0 else ((wa[1], na) if br == 1 else (wb[0], nb))
                        for kc in range(8):
                            mm(ps[bgk][:], wsrc[:, kc, :], hT[:, kc, cs], kc == 0, kc == 7,
                               wnm + ["hT%d" % tc], ["ps%d" % bgk], kc == 7)
                        by = bank_aux.next()
                        rng_ = (range(0, 4), range(4, 6), range(6, 8))[br]
                        kcs = [(wb[1 + br][:, kc - rng_[0], :], nb, kc) for kc in rng_]
                        for i, (wap, wnm2, oc) in enumerate(kcs):
                            mm(ps[by][:], wap, oT[:, oc, cs], i == 0, i == len(kcs) - 1,
                               wnm2 + ["o%d_%d" % (oc, tc)], ["ps%d" % by], i == len(kcs) - 1)
                        sg = T[br % 2]
                        sgn = "T%d" % (br % 2)
                        act(sg, ps[bgk][:], AF.Sigmoid, ["ps%d" % bgk], [sgn])
                        if br == 0:
                            tt(T[2], ps[by][:], sg, ALU.mult, ["ps%d" % by, sgn], ["T2"])
                        elif br == 1:
                            tt(T[3], ps[by][:], sg, ALU.mult, ["ps%d" % by, sgn], ["T3"])
                            tt(T[2], T[2], T[3], ALU.add, ["T2", "T3"], ["T2"])
                        else:
                            tt(T[3], ps[by][:], sg, ALU.mult, ["ps%d" % by, sgn], ["T3"])
                            tt(mg[:, fc, csl], T[2], T[3], ALU.add, ["T2", "T3"], ["mg%d" % tcl])
            if half == 0:
                dump("mg", R12[:, 16384:24576], N_MG)

        def wout_half(l, half):
            claim(N_YB)
            memset(ssq[:], 0.0, ["ssq%d" % t for t in range(8)])
            for sl in range(4):
                wvl, wn = wload([w_outT_d[l, sl]])
                wv = wvl[0]
                for t in range(8):
                    b = bank_main.next()
                    for kc in range(8):
                        mm(ps[b][:, 0:256], mg[:, kc, t * 128:(t + 1) * 128], wv[:, kc, :], kc == 0, kc == 7,
                           wn + ["mg%d" % (t // 4)], ["ps%d" % b], kc == 7)
                    evac_y(b, t, sl, 256)

        def gate_up(l, half):
            claim(N_AC)
            for j in range(NJ):
                wv, wn = wload([w_guT_d[l, j], w_guT_d[l, NJ + j]])
                for tcl in range(2):
                    tc = half * 2 + tcl
                    cs = slice(tc * 512, (tc + 1) * 512)
                    ba = bank_main.next()
                    for kc in range(8):
                        mm(ps[ba][:], wv[0][:, kc, :], hT[:, kc, cs], kc == 0, kc == 7, wn + ["hT%d" % tc],
                           ["ps%d" % ba], kc == 7)
                    bb = bank_aux.next()
                    for kc in range(8):
                        mm(ps[bb][:], wv[1][:, kc, :], hT[:, kc, cs], kc == 0, kc == 7, wn + ["hT%d" % tc],
                           ["ps%d" % bb], kc == 7)
                    sg = T[(j * 2 + tcl) % 2]
                    sgn = "T%d" % ((j * 2 + tcl) % 2)
                    act(sg, ps[ba][:], AF.Silu, ["ps%d" % ba], [sgn])
                    tt(actT[:, j, tcl * 512:(tcl + 1) * 512], ps[bb][:], sg, ALU.mult, ["ps%d" % bb, sgn], ["ac%d" % tcl])

        def down(l, half):
            claim(N_YB)
            memset(ssq[:], 0.0, ["ssq%d" % t for t in range(8)])
            for fcol in range(8):
                wvl, wn = wload([w_dnT_d[l, fcol]])
                wv = wvl[0]
                for t in range(8):
                    b = bank_main.next()
                    for kc in range(NJ):
                        mm(ps[b][:, 0:128], actT[:, kc, t * 128:(t + 1) * 128], wv[:, kc, :], kc == 0, kc == NJ - 1,
                           wn + ["ac%d" % (t // 4)], ["ps%d" % b], kc == NJ - 1)
                    evac_y(b, t, fcol, 128)
            update_x(half, 8)

        def ffn(l, s):
            stage_a(list(range(0, 8)), 16)
            gate_up(l, 0)
            stage_a(list(range(8, 16)), 16)
            down(l, 0)
            if l == layers[-1]:
                finish_tiles(s, range(0, 8))
            gate_up(l, 1)
            down(l, 1)
            if l == layers[-1]:
                finish_tiles(s, range(8, 16))

        def finish_tiles(s, tiles):
            for t in tiles:
                dma("sp", out_d[s, t * 128:(t + 1) * 128, :], X[:, t, :], ["X%d" % t], ())
            if s + 1 < n_seq:
                for t in tiles:
                    dma("sp", X[:, t, :], x_d[s + 1, t * 128:(t + 1) * 128, :], (), ["X%d" % t])

        def program():
            for s in range(n_seq):
                if s == 0:
                    for t in range(NT):
                        dma("sp", X[:, t, :], x_d[s, t * 128:(t + 1) * 128, :], (), ["X%d" % t])
                for kc in range(8):
                    cp(cbc[:, kc, :], ca[:, kc * 2 + s: kc * 2 + s + 1].to_broadcast([128, 128]), ["ca"], ["cbc"])
                rope_tables(s)
                for l in layers:
                    adaln_cols(s, l)
                    adaln_G(s, l, 0)
                    sa = stage_a_gen(list(range(NT)), 0)
                    if stop == "a":
                        for _ in sa:
                            pass
                        dump("hT", hT[:], ["hT%d" % i for i in range(4)])
                        return
                    claim(N_O)
                    for h in range(4):
                        branch_diff(l, h, sa if h == 0 else None)
                        if h == 0:
                            dump("hT", hT[:], ["hT%d" % i for i in range(4)])
                        if stop == "d0":
                            dump("oT2", R12[:, 0:4096], N_O)
                            return
                    for pair in range(2):
                        branch_fox(l, pair)
                    for pair in range(2):
                        branch_moba(l, pair)
                    dump("oT", R12[:, 0:16384], N_O)
                    if stop == "attn":
                        return
                    for half in range(2):
                        merge_half(l, half)
                        wout_half(l, half)
                        update_x(half, 4)
                    dump("xmix", X[:], ["X%d" % t for t in range(NT)])
                    if stop == "mix":
                        return
                    adaln_G(s, l, 1)
                    ffn(l, s)

        def lam_init(l):
            return 0.8 - 0.6 * math.exp(-0.3 * l)

        program()
        P.wait_tokens("sp", P.all_tokens())
        P.replay()
    nc._dbg_names = list(dbg_d.keys())
    nc._n_instr = {e: len(P.streams[e]) for e in ENGS}
    return nc


def host_consts():
    inv_freq = 1.0 / (10000.0 ** (np.arange(0, 64, 2, dtype=np.float32) / 64.0))
    invf = np.tile(inv_freq.astype(np.float32), 4).reshape(128, 1)
    rmat = np.zeros((128, 128), np.float32)
    for m in range(128):
        if m % 64 < 32:
            rmat[m + 32, m] = -1.0
        else:
            rmat[m - 32, m] = 1.0
    ident = np.eye(128, dtype=np.float32)
    mask01 = (np.arange(128)[:, None] <= np.arange(128)[None, :]).astype(np.float32)
    onehot = (np.arange(S)[None, :] // 256 == np.arange(8)[:, None]).astype(np.float32)
    negmask = np.where(np.arange(128)[:, None] > np.arange(128)[None, :], -30000.0, 0.0).astype(np.float32)
    return dict(invf=invf, rmat=rmat, ident=ident, mask01=mask01, onehot=onehot, negmask=negmask)


def make_in_maps(inputs, n_cores=8, n_seq=2):
    f = lambda a: np.ascontiguousarray(np.asarray(a))
    consts = host_consts()
    b_ada = f(inputs["b_ada"])
    badaT = np.ascontiguousarray(b_ada.reshape(2, 48, 128).transpose(0, 2, 1))
    gpm = f(inputs["g_pre_mix"]).reshape(2, 8, 128).transpose(0, 2, 1)
    gpf = f(inputs["g_pre_ffn"]).reshape(2, 8, 128).transpose(0, 2, 1)
    gsl = f(inputs["g_subln"]).reshape(2, 128, 1)
    gcols = np.ascontiguousarray(np.concatenate([gpm, gpf, gsl], axis=2))
    def tile_w(w, ncol):
        w = f(w)
        L, K, N = w.shape
        return np.ascontiguousarray(w.reshape(L, K // 128, 128, N // ncol, ncol).transpose(0, 3, 2, 1, 4))
    w_in = f(inputs["w_in"])
    shared = dict(
        w_adaT=tile_w(inputs["w_ada"], 128), b_ada=b_ada, badaT=badaT, gcols=gcols,
        g_post_mix=f(inputs["g_post_mix"]), g_post_ffn=f(inputs["g_post_ffn"]),
        w_inT=tile_w(w_in[:, :, 0:6144], 128), w_fgt=np.ascontiguousarray(w_in[:, :, 6144:6148]),
        b_fgt=f(inputs["b_fgt"]),
        lam_q1=f(inputs["lam_q1"]), lam_k1=f(inputs["lam_k1"]), lam_q2=f(inputs["lam_q2"]), lam_k2=f(inputs["lam_k2"]),
        w_braT=tile_w(inputs["w_br_a"], 128), w_brbT=tile_w(inputs["w_br_b"], 128), w_brcT=tile_w(inputs["w_br_c"], 128),
        w_outT=tile_w(inputs["w_out"], 256), w_guT=tile_w(inputs["w_gate_up"], 128),
        w_dnT=tile_w(inputs["w_down"], 128), **consts)
    x = f(inputs["x"])
    c = f(inputs["c"])
    pos = f(inputs["positions"]).astype(np.int32)
    maps = []
    for i in range(n_cores):
        bs = slice(i * n_seq, (i + 1) * n_seq)
        cc = c[bs]
        cT = np.zeros((128, 16), np.float32)
        cT[:, : 8 * 2] = 0
        for b in range(n_seq):
            cT[:, b::2][:, :8] = cc[b].reshape(8, 128).T
        m = dict(shared)
        m.update(x=np.ascontiguousarray(x[bs]), cT=cT, pos=np.ascontiguousarray(pos[bs]))
        maps.append(m)
    return maps


def kernel(**inputs):
    nc = build(n_seq=2, layers=(0, 1))
    maps = make_in_maps(inputs, 8, 2)
    res = run_bass_kernel_spmd(nc, maps, core_ids=list(range(8)))
    return np.concatenate([r["out"] for r in res.results], axis=0).astype(np.float32)
```

```python
import math
from contextlib import ExitStack
import numpy as np
import concourse.bass as bass
import concourse.mybir as mybir
from concourse.bass_utils import run_bass_kernel_spmd

F32 = mybir.dt.float32
BF16 = mybir.dt.bfloat16
I32 = mybir.dt.int32
AF = mybir.ActivationFunctionType
ALU = mybir.AluOpType
AX = mybir.AxisListType

ENGS = ("pe", "act", "dve", "pool", "sp")
N_DMA_SEMS = 24

D = 1024
S = 2048
NT = 16
DFF = 2816
NJ = 22
EPS = 1e-6
NEGBIG = -1920.0
PI = math.pi


class Plan:
    def __init__(self, nc):
        self.nc = nc
        self.streams = {e: [] for e in ENGS}
        self.count = {e: 0 for e in ENGS}
        self.wm = {e: {} for e in ENGS}
        self.res = {}
        self.dma_cnt = [0] * N_DMA_SEMS
        self.dma_rr = {"pool": 0, "sp": 0, "act": 0}
        self.sems = {}

    def _need(self, eng, reads, writes):
        need = {}

        def add(tok):
            if tok is None:
                return
            k, v = tok
            if need.get(k, 0) < v:
                need[k] = v

        for r in reads:
            ent = self.res.get(r)
            if ent is not None:
                add(ent[0])
        for w in writes:
            ent = self.res.get(w)
            if ent is not None:
                add(ent[0])
                for k, v in ent[1].items():
                    add((k, v))
        out = []
        for k, v in need.items():
            if k == "pe" and eng == "pe":
                continue
            if self.wm[eng].get(k, 0) >= v:
                continue
            self.wm[eng][k] = v
            out.append((k, v))
        return out

    def _record(self, tok, reads, writes):
        k, v = tok
        for r in reads:
            ent = self.res.setdefault(r, [None, {}])
            if ent[1].get(k, 0) < v:
                ent[1][k] = v
        for w in writes:
            self.res[w] = [tok, {}]

    def op(self, eng, fn, reads=(), writes=(), signal=True):
        writes = list(writes) + [r for r in reads if r.startswith("ps")]
        reads = [r for r in reads if not r.startswith("ps")]
        waits = self._need(eng, reads, writes)
        if signal:
            self.count[eng] += 1
            tok = (eng, self.count[eng])
            inc = (eng, 1)
        else:
            tok = (eng, self.count[eng] + 1)
            inc = None
        self._record(tok, reads, writes)
        self.streams[eng].append((waits, fn, inc))
        return tok

    def dma(self, eng, fn, reads=(), writes=()):
        half = N_DMA_SEMS // 2
        base = 0 if eng == "pool" else half
        s = base + self.dma_rr[eng]
        self.dma_rr[eng] = (self.dma_rr[eng] + 1) % half
        key = "dma%d" % s
        waits = self._need(eng, reads, writes)
        prev = 16 * self.dma_cnt[s]
        if prev and self.wm[eng].get(key, 0) < prev:
            self.wm[eng][key] = prev
            waits.append((key, prev))
        self.dma_cnt[s] += 1
        tok = (key, 16 * self.dma_cnt[s])
        self._record(tok, reads, writes)
        self.streams[eng].append((waits, fn, (key, 16)))
        return tok

    def fence(self, new, old):
        merged = {}
        for o in old:
            ent = self.res.get(o)
            if ent is None:
                continue
            if ent[0] is not None:
                k, v = ent[0]
                merged[k] = max(merged.get(k, 0), v)
            for k, v in ent[1].items():
                merged[k] = max(merged.get(k, 0), v)
        for n in new:
            ent = self.res.get(n)
            m2 = dict(merged)
            if ent is not None:
                if ent[0] is not None:
                    k, v = ent[0]
                    m2[k] = max(m2.get(k, 0), v)
                for k, v in ent[1].items():
                    m2[k] = max(m2.get(k, 0), v)
            self.res[n] = [None, m2]

    def wait_tokens(self, eng, toks):
        waits = []
        for k, v in toks:
            if self.wm[eng].get(k, 0) < v:
                self.wm[eng][k] = v
                waits.append((k, v))
        self.streams[eng].append((waits, None, None))

    def all_tokens(self):
        toks = {}
        for ent in self.res.values():
            if ent[0] is not None:
                k, v = ent[0]
                toks[k] = max(toks.get(k, 0), v)
            for k, v in ent[1].items():
                toks[k] = max(toks.get(k, 0), v)
        return list(toks.items())

    def replay(self):
        nc = self.nc
        with ExitStack() as es:
            for e in ENGS:
                self.sems[e] = es.enter_context(nc.semaphore("s_" + e))
            for i in range(N_DMA_SEMS):
                self.sems["dma%d" % i] = es.enter_context(nc.semaphore("s_dma%d" % i))
            block = es.enter_context(nc.Block())
            sems = self.sems

            def run(engname):
                def body(eng):
                    for waits, fn, inc in self.streams[engname]:
                        for k, v in waits:
                            eng.wait_ge(sems[k], v)
                        if fn is None:
                            continue
                        ins = fn(eng)
                        if inc is not None:
                            ins.then_inc(sems[inc[0]], inc[1])
                return body

            block.tensor(run("pe"))
            block.scalar(run("act"))
            block.vector(run("dve"))
            block.gpsimd(run("pool"))
            block.sync(run("sp"))


class Ring:
    def __init__(self, items):
        self.items = list(items)
        self.i = 0

    def next(self):
        v = self.items[self.i]
        self.i = (self.i + 1) % len(self.items)
        return v


def build(n_seq=2, layers=(0, 1), dbg=(), stop=None):
    nc = bass.Bass("TRN2", target_bir_lowering=False)

    def din(name, shape, dt=F32):
        return nc.dram_tensor(name, list(shape), dt, kind="ExternalInput").ap()

    x_d = din("x", [n_seq, S, D])
    cT_d = din("cT", [128, 16])
    pos_d = din("pos", [n_seq, S], I32)
    w_adaT_d = din("w_adaT", [2, 48, 128, 8, 128])
    b_ada_d = din("b_ada", [2, 6 * D])
    badaT_d = din("badaT", [2, 128, 48])
    gcols_d = din("gcols", [2, 128, 17])
    g_post_mix_d = din("g_post_mix", [2, D])
    g_post_ffn_d = din("g_post_ffn", [2, D])
    w_inT_d = din("w_inT", [2, 48, 128, 8, 128])
    w_fgt_d = din("w_fgt", [2, D, 4])
    b_fgt_d = din("b_fgt", [2, 4])
    lam_d = [din(n, [2, 64]) for n in ("lam_q1", "lam_k1", "lam_q2", "lam_k2")]
    w_brT_d = [din("w_braT", [2, 8, 128, 4, 128]), din("w_brbT", [2, 8, 128, 2, 128]),
               din("w_brcT", [2, 8, 128, 2, 128])]
    w_outT_d = din("w_outT", [2, 4, 128, 8, 256])
    w_guT_d = din("w_guT", [2, 44, 128, 8, 128])
    w_dnT_d = din("w_dnT", [2, 8, 128, NJ, 128])
    invf_d = din("invf", [128, 1])
    rmat_d = din("rmat", [128, 128])
    ident_d = din("ident", [128, 128])
    mask_d = din("mask01", [128, 128])
    negmask_d = din("negmask", [128, 128])
    onehot_d = din("onehot", [8, S])
    out_d = nc.dram_tensor("out", [n_seq, S, D], F32, kind="ExternalOutput").ap()
    dbg_d = {}

    es = ExitStack()
    with es:
        def sb(name, shape, dt):
            return es.enter_context(nc.sbuf_tensor(name, list(shape), dt))

        X = sb("X", [128, NT, D], F32)
        hT = sb("hT", [128, 8, S], BF16)
        R12 = sb("R12", [128, 24576], BF16)
        R3 = sb("R3", [128, 8192], BF16)
        tabC = sb("tabC", [128, S], BF16)
        tabS = sb("tabS", [128, S], BF16)
        NSLOT = 3
        wslot = [sb("wslot%d" % i, [128, 2816], BF16) for i in range(NSLOT)]
        G = sb("G", [128, D], F32)
        SCR = sb("SCR", [128, 2048], F32)
        ident_b = sb("ident_b", [128, 128], BF16)
        negmask_b = sb("negmask_b", [128, 128], BF16)
        rmat_b = sb("rmat_b", [128, 128], BF16)
        ones_b = sb("ones_b", [128, 128], BF16)
        U_f = sb("U_f", [128, 128], F32)
        ones_f = sb("ones_f", [128, 128], F32)
        invf = sb("invf_s", [128, 1], F32)
        halfpi = sb("halfpi", [128, 1], F32)
        modc1 = sb("modc1", [128, 2, 48], F32)
        cTs = sb("cTs", [128, 16], F32)
        ca = sb("ca", [128, 16], BF16)
        cbc = sb("cbc", [128, 8, 128], BF16)
        badaT = sb("badaT_s", [128, 48], F32)
        gcols = sb("gcols_s", [128, 17], F32)
        modc = sb("modc", [128, 48], F32)
        scsh = sb("scsh", [128, 32], F32)
        ss = sb("ss", [128, 16], F32)
        lnv = sb("lnv", [128, 16], F32)
        rstd = sb("rstd", [128, 16], F32)
        lamt = sb("lamt", [128, 4, 64], F32)
        lamp = sb("lamp", [128, 64], F32)
        lams = sb("lams", [128, 4], F32)
        neglam = sb("neglam", [128, 1], F32)
        gcol = sb("gcol", [128, 1], F32)
        bfrep = sb("bfrep", [128, 4], F32)
        zb = sb("zb", [128, 4, NT], F32)
        lf = sb("lf", [128, 4, NT], F32)
        tots = sb("tots", [128, 4, NT], F32)
        offs = sb("offs", [128, 4, NT], F32)
        Lcum = sb("Lcum", [128, 4, NT], F32)
        r8 = sb("r8", [128, 4, NT], BF16)
        kmT = sb("kmT", [128, 2, 8], BF16)
        kms = sb("kms", [128, 2, 8], F32)
        gm = sb("gm", [128, 2, 8, 8], F32)
        mx8 = sb("mx8", [128, 8], F32)
        selb = sb("selb", [128, 2, 8, 8], BF16)
        ssq = sb("ssq", [128, 8, 8], F32)
        ssu = sb("ssu", [128, 8], F32)
        lnvu = sb("lnvu", [128, 8], F32)
        rstdu = sb("rstdu", [128, 8], F32)
        junk2 = sb("junk2", [128, 4, 256], BF16)
        junk_ring = Ring([0, 1, 2, 3])

        ps = [es.enter_context(nc.psum_tensor("ps%d" % i, [128, 512], F32)) for i in range(8)]

        oT = R12[:, 0:16384].rearrange("p (c t) -> p c t", t=S)
        qk = [R12[:, 16384 + i * S: 16384 + (i + 1) * S] for i in range(4)]
        mg = R12[:, 16384:24576].rearrange("p (c t) -> p c t", t=1024)
        actT = R12[:, 0:22528].rearrange("p (c t) -> p c t", t=1024)

        Vd1 = R3[:, 0:2048].rearrange("p (t c) -> p t c", c=128)
        Vs = R3[:, 0:4096].rearrange("p (t j c) -> p t j c", j=2, c=128)
        PT = [R3[:, 4096 + i * 512: 4096 + (i + 1) * 512] for i in range(8)]
        ybuf = R3[:, 0:8192].rearrange("p (t c) -> p t c", c=1024)
        posi = R3[:, 0:4096].bitcast(I32)
        T = [SCR[:, i * 512:(i + 1) * 512] for i in range(4)]
        Ti = [T[i].bitcast(I32) for i in range(4)]
        xn = [T[2].bitcast(BF16), T[3].bitcast(BF16)]
        XN_NAMES = ["T2", "T3"]
        junkA = junk2[:].rearrange("p a b -> p (a b)")
        deferred = []

        def run_deferred(ring):
            while deferred:
                deferred.pop(0)(ring)

        N_O = ["o%d_%d" % (c, t) for c in range(8) for t in range(4)]
        N_AC = ["ac0", "ac1"]
        N_QK = ["qk0", "qk1", "qk2", "qk3"]
        N_MG = ["mg0", "mg1"]
        N_XN = ["xn0", "xn1"]
        N_V = ["V"]
        N_PT = ["pt%d" % i for i in range(8)]
        N_YB = ["yb%d" % i for i in range(8)]
        N_POSI = ["posi"]
        REG_A = N_O + N_AC
        REG_B = N_QK + N_MG + N_AC
        REG_C = N_V + N_PT + N_YB + N_POSI
        N_T = ["T0", "T1", "T2", "T3"]
        WNAMES = [["w%d_%d" % (s_, j) for j in range(4)] for s_ in range(NSLOT)]

        P = Plan(nc)

        def claim(names):
            for reg in (REG_A, REG_B, REG_C):
                mine = [n for n in names if n in reg]
                if mine:
                    P.fence(mine, [n for n in reg if n not in mine])

        def mm(out, lhsT, rhs, start, stop, reads, writes, signal):
            P.op("pe", lambda e: e.matmul(out, lhsT=lhsT, rhs=rhs, start=start, stop=stop),
                 reads, writes, signal)

        def tr(out, in_, reads, writes, signal=True):
            P.op("pe", lambda e: e.transpose(out=out, in_=in_, identity=ident_b[:]), reads, writes, signal)

        def act(out, in_, func, reads, writes, bias=None, scale=None, accum=None):
            kw = {}
            if bias is not None:
                kw["bias"] = bias
            if scale is not None:
                kw["scale"] = scale
            if accum is not None:
                kw["accum_out"] = accum
            P.op("act", lambda e: e.activation(out=out, in_=in_, func=func, **kw), reads, writes)

        def tt(out, in0, in1, op, reads, writes, eng="dve"):
            P.op(eng, lambda e: e.tensor_tensor(out=out, in0=in0, in1=in1, op=op), reads, writes)

        def ts(out, in0, s1, s2, op0, op1, reads, writes, eng="dve"):
            if op1 is None:
                P.op(eng, lambda e: e.tensor_scalar(out=out, in0=in0, scalar1=s1, scalar2=None, op0=op0), reads, writes)
            else:
                P.op(eng, lambda e: e.tensor_scalar(out=out, in0=in0, scalar1=s1, scalar2=s2, op0=op0, op1=op1), reads, writes)

        def stt(out, in0, scalar, in1, op0, op1, reads, writes, eng="dve"):
            P.op(eng, lambda e: e.scalar_tensor_tensor(out=out, in0=in0, scalar=scalar, in1=in1, op0=op0, op1=op1),
                 reads, writes)

        def cp(out, in_, reads, writes, eng="dve"):
            P.op(eng, lambda e: e.tensor_copy(out=out, in_=in_), reads, writes)

        def recip(out, in_, reads, writes):
            act(out, in_, AF.Ln, reads, writes)
            act(out, out, AF.Exp, list(writes), writes, scale=-1.0)

        def memset(ap, val, writes, eng="dve"):
            P.op(eng, lambda e: e.memset(ap, val), (), writes)

        def dma(eng, out, in_, reads, writes):
            P.dma(eng, lambda e: e.dma_start(out=out, in_=in_), reads, writes)

        def dump(name, ap, reads):
            if name not in dbg:
                return
            d = nc.dram_tensor("dbg_" + name, list(ap.shape), ap.dtype, kind="ExternalOutput").ap()
            dbg_d[name] = d
            dma("sp", d, ap, reads, ())

        bank_main = Ring([0, 1, 2, 3])
        bank_aux = Ring([4, 5, 6, 7])
        pt_ring = Ring(list(range(8)))
        slot_ring = Ring(list(range(NSLOT)))

        def wload(pieces):
            s_ = slot_ring.next()
            views = []
            off = 0
            for j, src in enumerate(pieces):
                kc_, n_ = src.shape[1], src.shape[2]
                v = wslot[s_][:, off:off + kc_ * n_].rearrange("p (k n) -> p k n", n=n_)
                off += kc_ * n_
                dma("pool", v, src, (), [WNAMES[s_][j]])
                views.append(v)
            assert off <= 2816
            return views, WNAMES[s_]

        dma("pool", ident_b[:], ident_d, (), ["ident_b"])
        dma("pool", negmask_b[:], negmask_d, (), ["negmask_b"])
        dma("pool", rmat_b[:], rmat_d, (), ["rmat_b"])
        dma("sp", U_f[:], mask_d, (), ["U_f"])
        dma("sp", invf[:], invf_d, (), ["invf"])
        dma("sp", cTs[:], cT_d, (), ["cTs"])
        memset(ones_b[:], 1.0, ["ones_b"])
        memset(halfpi[:], PI / 2, ["halfpi"])
        memset(ones_f[:], 1.0, ["ones_f"])
        memset(selb[:], 0.0, ["selb"])
        act(ca[:], cTs[:], AF.Silu, ["cTs"], ["ca"])

        def rmsn(ss_ap, lnv_ap, rstd_ap, in_names, ln_name, out_name, scale):
            act(lnv_ap, ss_ap, AF.Ln, list(in_names), [ln_name], bias=EPS, scale=scale)
            act(rstd_ap, lnv_ap, AF.Exp, [ln_name], [out_name], scale=-0.5)

        def rope_tables(s):
            claim(N_POSI)
            dma("sp", posi, pos_d[s:s + 1, :].to_broadcast([128, S]), (), ["posi"])
            for j in range(4):
                cs = slice(j * 512, (j + 1) * 512)
                ts(T[0], posi[:, cs], invf[:, 0:1], None, ALU.mult, None, ["posi", "invf"], ["T0"])
                ts(Ti[1], T[0], 1.0 / (2 * PI), None, ALU.mult, None, ["T0"], ["T1"])
                stt(T[2], Ti[1], -2 * PI, T[0], ALU.mult, ALU.add, ["T1", "T0"], ["T2"])
                ts(T[3], T[2], PI, -2 * PI, ALU.is_gt, ALU.mult, ["T2"], ["T3"])
                tt(T[2], T[2], T[3], ALU.add, ["T2", "T3"], ["T2"])
                ts(T[2], T[2], -PI, PI, ALU.max, ALU.min, ["T2"], ["T2"])
                act(tabS[:, cs], T[2], AF.Sin, ["T2"], ["tab0_%d" % j])
                stt(T[3], T[2], -1.0, T[2], ALU.mult, ALU.max, ["T2"], ["T3"])
                act(tabC[:, cs], T[3], AF.Sin, ["T3", "halfpi"], ["tab1_%d" % j], bias=halfpi[:, 0:1], scale=-1.0)
            dump("tabC", tabC[:], ["tab1_%d" % j for j in range(4)])
            dump("tabS", tabS[:], ["tab0_%d" % j for j in range(4)])

        def adaln_cols(s, l):
            dma("sp", badaT[:], badaT_d[l], (), ["badaT"])
            dma("sp", gcols[:], gcols_d[l], (), ["gcols"])
            if s == 0:
                b = bank_aux.next()
                chunks = list(range(0, 16)) + list(range(24, 40))
                for i in range(0, len(chunks), 2):
                    j0 = chunks[i]
                    wv, wn = wload([w_adaT_d[l, j0], w_adaT_d[l, j0 + 1]])
                    for jj in range(2):
                        j = j0 + jj
                        for kc in range(8):
                            mm(ps[b][:, 2 * j:2 * j + n_seq], wv[jj][:, kc, :],
                               ca[:, kc * 2: kc * 2 + n_seq], kc == 0, kc == 7, wn + ["ca"], ["ps%d" % b], kc == 7)
                psv = ps[b][:, 0:96].rearrange("p (j b) -> p j b", b=2)
                for (a0, a1) in ((0, 16), (24, 40)):
                    tt(modc[:, a0:a1], psv[:, a0:a1, 0], badaT[:, a0:a1], ALU.add, ["ps%d" % b, "badaT", "modc"], ["modc"])
                    if n_seq > 1:
                        tt(modc1[:, l, a0:a1], psv[:, a0:a1, 1], badaT[:, a0:a1], ALU.add,
                           ["ps%d" % b, "badaT", "modc1_%d" % l], ["modc1_%d" % l])
            else:
                for (a0, a1) in ((0, 16), (24, 40)):
                    cp(modc[:, a0:a1], modc1[:, l, a0:a1], ["modc1_%d" % l, "modc"], ["modc"])
            stt(scsh[:, 0:8], modc[:, 8:16], 1.0, gcols[:, 0:8], ALU.add, ALU.mult, ["modc", "gcols"], ["scsh"])
            cp(scsh[:, 8:16], modc[:, 0:8], ["modc"], ["scsh"])
            stt(scsh[:, 16:24], modc[:, 32:40], 1.0, gcols[:, 8:16], ALU.add, ALU.mult, ["modc", "gcols"], ["scsh"])
            cp(scsh[:, 24:32], modc[:, 24:32], ["modc"], ["scsh"])
            ts(gcol[:], gcols[:, 16:17], 1.0 - lam_init(l), None, ALU.mult, None, ["gcols"], ["gcol"])
            for i in range(4):
                dma("sp", lamt[:, i, :], lam_d[i][l:l + 1, :].to_broadcast([128, 64]), (), ["lamt%d" % i])
            for i in range(2):
                tt(lamp[:], lamt[:, 2 * i, :], lamt[:, 2 * i + 1, :], ALU.mult,
                   ["lamt%d" % (2 * i), "lamt%d" % (2 * i + 1)], ["lamp"])
                P.op("dve", (lambda i_: lambda e: e.reduce_sum(out=lams[:, i_:i_ + 1], in_=lamp[:], axis=AX.X))(i),
                     ["lamp"], ["lams%d" % i])
            act(lams[:, 2:4], lams[:, 0:2], AF.Exp, ["lams0", "lams1"], ["lamse"])
            tt(neglam[:], lams[:, 3:4], lams[:, 2:3], ALU.subtract, ["lamse"], ["neglam"])
            ts(neglam[:], neglam[:], -lam_init(l), None, ALU.add, None, ["neglam"], ["neglam"])
            dma("sp", bfrep[:], b_fgt_d[l:l + 1, :].to_broadcast([128, 4]), (), ["bfrep"])

        def adaln_G(s, l, which):
            c0 = 2048 if which == 0 else 5120
            gp = g_post_mix_d if which == 0 else g_post_ffn_d
            brep = SCR[:, 0:1024]
            grep = SCR[:, 1024:2048]
            dma("sp", brep, b_ada_d[l:l + 1, c0:c0 + 1024].to_broadcast([128, 1024]), (), ["T0", "T1"])
            dma("sp", grep, gp[l:l + 1, :].to_broadcast([128, 1024]), (), ["T2", "T3"])
            for q in range(4):
                jq = c0 // 128 + 2 * q
                wv, wn = wload([w_adaT_d[l, jq], w_adaT_d[l, jq + 1]])
                b = bank_main.next()
                for jj in range(2):
                    for kc in range(8):
                        mm(ps[b][:, jj * 128:(jj + 1) * 128], cbc[:, kc, :], wv[jj][:, kc, :], kc == 0, kc == 7,
                           wn + ["cbc"], ["ps%d" % b], kc == 7)
                cs = slice(q * 256, (q + 1) * 256)
                tt(G[:, cs], ps[b][:, 0:256], brep[:, cs], ALU.add, ["ps%d" % b, "T0", "T1"], ["G%d" % q])
                tt(G[:, cs], G[:, cs], grep[:, cs], ALU.mult, ["G%d" % q, "T2", "T3"], ["G%d" % q])

        N_G = ["G0", "G1", "G2", "G3"]

        def stage_a(tiles, off):
            ngrp = len(tiles) // 4

            def sq(g):
                grp = tiles[4 * g:4 * g + 4]
                c0 = grp[0]
                names = ["ss%d" % t for t in grp]
                memset(ss[:, c0:c0 + 4], 0.0, names)
                for t in grp:
                    act(junkA, X[:, t, :], AF.Square, ["X%d" % t], ["junk0", "junk1", "junk2", "junk3", "ss%d" % t],
                        accum=ss[:, t:t + 1])
                rmsn(ss[:, c0:c0 + 4], lnv[:, c0:c0 + 4], rstd[:, c0:c0 + 4], names, "lnv%d" % (c0 // 4),
                     "rstd%d" % (c0 // 4), 1.0 / D)

            sq(0)
            for g in range(ngrp):
                if g + 1 < ngrp:
                    sq(g + 1)
                grp = tiles[4 * g:4 * g + 4]
                tc = grp[0] // 4
                banks = [bank_main.next(), bank_main.next(), bank_aux.next(), bank_aux.next()]
                for i, t in enumerate(grp):
                    ts(xn[i % 2], X[:, t, :], rstd[:, t:t + 1], None, ALU.mult, None,
                       ["X%d" % t, "rstd%d" % tc], [XN_NAMES[i % 2]])
                    for kc in range(8):
                        b = banks[kc // 2]
                        pv = ps[b][:].bitcast(BF16)
                        o0 = (kc % 2) * 512 + i * 128
                        tr(pv[:, o0:o0 + 128], xn[i % 2][:, kc * 128:(kc + 1) * 128],
                           [XN_NAMES[i % 2], "ident_b"], ["ps%d" % b], signal=(kc == 7))
                for kc in range(8):
                    b = banks[kc // 2]
                    pv = ps[b][:].bitcast(BF16)
                    o0 = (kc % 2) * 512
                    if kc % 2 == 0:
                        act(hT[:, kc, tc * 512:(tc + 1) * 512], pv[:, o0:o0 + 512], AF.Identity,
                            ["ps%d" % b, "scsh"], ["hT%d" % tc],
                            bias=scsh[:, off + 8 + kc: off + 9 + kc], scale=scsh[:, off + kc: off + kc + 1])
                    else:
                        ts(hT[:, kc, tc * 512:(tc + 1) * 512], pv[:, o0:o0 + 512],
                           scsh[:, off + kc: off + kc + 1], scsh[:, off + 8 + kc: off + 9 + kc], ALU.mult, ALU.add,
                           ["ps%d" % b, "scsh"], ["hT%d" % tc])

        def proj_F(wp, wn, rope, dests):
            tail = [None]
            for tc in range(4):
                cs = slice(tc * 512, (tc + 1) * 512)
                b = bank_main.next()
                for kc in range(8):
                    mm(ps[b][:], wp[:, kc, :], hT[:, kc, cs], kc == 0, kc == 7,
                       wn + ["hT%d" % tc], ["ps%d" % b], kc == 7)
                if tc == 0:
                    run_deferred(bank_aux)
                if not rope:
                    for (r0, nr, tile, d0, nm) in dests:
                        cp(tile[d0:d0 + nr, cs], ps[b][r0:r0 + nr, :], ["ps%d" % b], [nm])
                else:
                    pi = pt_ring.next()
                    act(PT[pi], ps[b][:], AF.Identity, ["ps%d" % b], ["pt%d" % pi])
                    if tail[0] is not None:
                        tail[0]()

                    def mk(b=b, pi=pi, cs=cs, tc=tc):
                        b2 = bank_aux.next()
                        mm(ps[b2][:], rmat_b[:], PT[pi], True, True, ["rmat_b", "pt%d" % pi], ["ps%d" % b2], True)
                        tt(T[0], ps[b][:], tabC[:, cs], ALU.mult, ["ps%d" % b, "tab1_%d" % tc], ["T0"])
                        tt(T[1], ps[b2][:], tabS[:, cs], ALU.mult, ["ps%d" % b2, "tab0_%d" % tc], ["T1"])
                        for (r0, nr, tile, d0, nm) in dests:
                            tt(tile[d0:d0 + nr, cs], T[0][r0:r0 + nr, :], T[1][r0:r0 + nr, :], ALU.add,
                               ["T0", "T1"], [nm])
                    tail[0] = mk
            if tail[0] is not None:
                tail[0]()

        def proj_V(wp, wn, N, evac, extra=None):
            for t in range(NT):
                b = bank_main.next()
                for kc in range(8):
                    mm(ps[b][:, 0:N], hT[:, kc, t * 128:(t + 1) * 128], wp[:, kc, 0:N], kc == 0, kc == 7,
                       wn + ["hT%d" % (t // 4)], ["ps%d" % b], kc == 7)
                if extra is not None:
                    wp2, n2 = extra
                    for kc in range(8):
                        mm(ps[b][:, N:N + n2], hT[:, kc, t * 128:(t + 1) * 128], wp2[:, kc, :], kc == 0, kc == 7,
                           wn + ["hT%d" % (t // 4)], ["ps%d" % b], kc == 7)
                evac(t, b)

        def attend_chunk(qc, maps, st_ring, hook=None):
            steps = []
            nk = 4 * qc + 4
            for kt in range(nk):
                for mi, m in enumerate(maps):
                    steps.append((mi, kt, nk))
            LAG = 3 if len(st_ring.items) >= 4 else 2
            pend = []
            nstep = [0]

            def emit_pv(st):
                mi, kt, nk, pi, c0 = st
                m = maps[mi]
                for (ab, lfn, rds) in m["pv"]:
                    mm(ps[ab][:, c0:512], lfn(kt), PT[pi][:, c0:512], kt == 0, kt == nk - 1,
                       ["pt%d" % pi] + rds, ["ps%d" % ab], True)

            for (mi, kt, nk) in steps:
                m = maps[mi]
                j = kt - 4 * qc
                c0 = max(j, 0) * 128
                b = st_ring.next()
                diag = j >= 0
                mm(ps[b][:, c0:512], m["k"][:, kt * 128:(kt + 1) * 128], m["q"][:, qc * 512 + c0:(qc + 1) * 512],
                   True, not diag, m["rn"], ["ps%d" % b], not diag)
                if diag:
                    mm(ps[b][:, c0:c0 + 128], ident_b[:], negmask_b[:], False, True,
                       ["ident_b", "negmask_b"], ["ps%d" % b], True)
                pi = pt_ring.next()
                bias_ap, bias_names = m["bias"](kt)
                act(PT[pi][:, c0:512], ps[b][:, c0:512], AF.Exp, ["ps%d" % b] + bias_names, ["pt%d" % pi],
                    bias=bias_ap, scale=0.125)
                pend.append((mi, kt, nk, pi, c0))
                if len(pend) > LAG:
                    emit_pv(pend.pop(0))
                nstep[0] += 1
                if nstep[0] == 4:
                    run_deferred(st_ring)
            while pend:
                emit_pv(pend.pop(0))

        ACC6 = [2, 3, 4, 5, 6, 7]
        diff_chunk_ctr = [0]
        st2 = Ring([0, 1])
        st4 = Ring([0, 1, 2, 3])

        diff_pending = [None]

        def attend_diff(h):
            for qc in range(4):
                c = diff_chunk_ctr[0]
                diff_chunk_ctr[0] += 1
                A = [ACC6[(4 * c) % 6], ACC6[(4 * c + 2) % 6]]
                Sm = [ACC6[(4 * c + 1) % 6], ACC6[(4 * c + 3) % 6]]
                for m_ in range(2):
                    mp = dict(q=qk[m_][:, :], k=qk[2][:, :], rn=["qk%d" % m_, "qk2"],
                              bias=lambda kt: (None, []),
                              pv=[(A[m_], (lambda kt: Vd1[:, kt, :]), ["V"]),
                                  (Sm[m_], (lambda kt: ones_b[:]), ["ones_b"])])
                    attend_chunk(qc, [mp], st2)
                cs = slice(qc * 512, (qc + 1) * 512)
                a0, s0, a1, s1 = A[0], Sm[0], A[1], Sm[1]
                recip(T[0], ps[s0][:], ["ps%d" % s0], ["T0"])
                tt(T[1], ps[a0][:], T[0], ALU.mult, ["ps%d" % a0, "T0"], ["T1"])
                recip(T[0], ps[s1][:], ["ps%d" % s1], ["T0"])
                tt(T[2], ps[a1][:], T[0], ALU.mult, ["ps%d" % a1, "T0"], ["T2"])
                stt(T[3], T[2], neglam[:, 0:1], T[1], ALU.mult, ALU.add, ["T2", "T1", "neglam"], ["T3"])
                sqv = T[2].bitcast(BF16)[:, 0:512]
                tt(sqv, T[3], T[3], ALU.mult, ["T3", "T2"], ["T2"])

                def part2(ring, sqv=sqv, h=h, qc=qc, cs=cs):
                    b = ring.next()
                    mm(ps[b][:], ones_b[:], sqv, True, True, ["ones_b", "T2"], ["ps%d" % b], True)
                    act(T[0], ps[b][:], AF.Ln, ["ps%d" % b], ["T0"], bias=EPS, scale=1.0 / 128)
                    act(T[1], T[0], AF.Exp, ["T0"], ["T1"], scale=-0.5)
                    stt(oT[:, h, cs], T[3], gcol[:, 0:1], T[1], ALU.mult, ALU.mult, ["T3", "T1", "gcol"],
                        ["o%d_%d" % (h, qc)])
                deferred.append(part2)

        def attend_single(pair, chunk, R, bias_fn):
            for hl in range(2):
                h = 2 * pair + hl
                for qc in range(4):
                    ab = 4 + (hl * 4 + qc) % 4
                    maps = [dict(q=qk[hl][0:R, :], k=qk[2 + hl][0:R, :], rn=["qk%d" % hl, "qk%d" % (2 + hl)],
                                 bias=(lambda kt, h_=h: bias_fn(h_, kt)),
                                 pv=[(ab, (lambda kt, hl_=hl: Vs[:, kt, hl_, :]), ["V"])])]
                    attend_chunk(qc, maps, st4)
                    cs = slice(qc * 512, (qc + 1) * 512)
                    P.op("dve", (lambda ab_: lambda e: e.reciprocal(out=T[0][64:128, :], in_=ps[ab_][64:128, :]))(ab),
                         (), ["ps%d" % ab, "T0"])
                    d0 = hl * 64
                    tt(oT[d0:d0 + 64, chunk, cs], ps[ab][0:64, :], T[0][64:128, :], ALU.mult,
                       ["ps%d" % ab, "T0"], ["o%d_%d" % (chunk, qc)])

        def branch_diff(l, h):
            claim(N_QK + N_V + N_PT)
            wqk, nqk = wload([w_inT_d[l, h], w_inT_d[l, 4 + h]])
            wvv, nv = wload([w_inT_d[l, 8 + h]])
            if h == 0:
                memset(qk[0][64:128, :], 0.0, ["qk0"])
                memset(qk[1][0:64, :], 0.0, ["qk1"])
            proj_F(wqk[0], nqk, True, [(0, 64, qk[0], 0, "qk0"), (64, 64, qk[1], 64, "qk1")])
            proj_F(wqk[1], nqk, True, [(0, 128, qk[2], 0, "qk2")])

            def evac(t, b):
                cp(Vd1[:, t, :], ps[b][:, 0:128], ["ps%d" % b], ["V"])
            proj_V(wvv[0], nv, 128, evac)
            if h == 0:
                dump("qd0", qk[0], ["qk0"])
                dump("kd0", qk[2], ["qk2"])
            attend_diff(h)

        def branch_fox(l, pair):
            claim(N_QK + N_V + N_PT)
            wqk, nqk = wload([w_inT_d[l, 12 + pair], w_inT_d[l, 14 + pair]])
            pieces = [w_inT_d[l, 16 + pair]]
            if pair == 0:
                pieces.append(w_fgt_d[l].rearrange("(k p) n -> p k n", p=128))
            wvv, nv = wload(pieces)
            proj_F(wqk[0], nqk, False, [(0, 64, qk[0], 0, "qk0"), (64, 64, qk[1], 0, "qk1")])
            proj_F(wqk[1], nqk, False, [(0, 64, qk[2], 0, "qk2"), (64, 64, qk[3], 0, "qk3")])
            memset(Vs[:, :, :, 64:128], 1.0, ["V"], eng="pool")
            for i in range(4):
                memset(qk[i][64:128, :], 0.0, ["qk%d" % i], eng="pool")
            memset(qk[2][64:65, :], 1.0, ["qk2"], eng="pool")
            memset(qk[3][64:65, :], 1.0, ["qk3"], eng="pool")

            def evac(t, b):
                cp(Vs[:, t, :, 0:64], ps[b][:, 0:128].rearrange("p (j c) -> p j c", c=64), ["ps%d" % b], ["V"])
                if pair == 0:
                    tt(zb[:, :, t], ps[b][:, 128:132], bfrep[:], ALU.add, ["ps%d" % b, "bfrep"], ["zb"])
            proj_V(wvv[0], nv, 128, evac, extra=((wvv[1], 4) if pair == 0 else None))
            if pair == 0:
                act(lf[:], zb[:], AF.Exp, ["zb"], ["lf"], scale=-1.0)
                act(lf[:], lf[:], AF.Ln, ["lf"], ["lf"], bias=1.0)
                b1 = bank_aux.next()
                lf2 = lf[:].rearrange("p h t -> p (h t)")
                mm(ps[b1][:, 0:64], U_f[:], lf2, True, True, ["U_f", "lf"], ["ps%d" % b1], True)
                b2 = bank_aux.next()
                mm(ps[b2][:, 0:64], ones_f[:], lf2, True, True, ["ones_f", "lf"], ["ps%d" % b2], True)
                cp(tots[:].rearrange("p h t -> p (h t)"), ps[b2][:, 0:64], ["ps%d" % b2], ["tots"])
                memset(offs[:, :, 0:1], 0.0, ["offs"])
                for i in range(1, NT):
                    tt(offs[:, :, i:i + 1], offs[:, :, i - 1:i], tots[:, :, i - 1:i], ALU.add, ["offs", "tots"], ["offs"])
                tt(Lcum[:].rearrange("p h t -> p (h t)"), ps[b1][:, 0:64], offs[:].rearrange("p h t -> p (h t)"),
                   ALU.add, ["ps%d" % b1, "offs"], ["Lcum"])
                ts(r8[:], Lcum[:], -8.0, None, ALU.mult, None, ["Lcum"], ["r8"])
                dump("Lcum", Lcum[:], ["Lcum"])
            for hl in range(2):
                h = 2 * pair + hl
                for g in range(4):
                    b = bank_aux.next()
                    for i in range(4):
                        t = 4 * g + i
                        mm(ps[b][0:1, i * 128:(i + 1) * 128], r8[:, h, t:t + 1], ident_b[:], True, True,
                           ["r8", "ident_b"], ["ps%d" % b], i == 3)
                    act(qk[hl][64:65, g * 512:(g + 1) * 512], ps[b][0:1, :], AF.Identity, ["ps%d" % b], ["qk%d" % hl])
            if pair == 0:
                dump("qf0", qk[0], ["qk0"])
                dump("kf0", qk[2], ["qk2"])

            def bias_fn(h, kt):
                return Lcum[:, h, kt:kt + 1], ["Lcum"]
            attend_single(pair, 4 + pair, 128, bias_fn)

        def branch_moba(l, pair):
            claim(N_QK + N_V + N_PT)
            wqk, nqk = wload([w_inT_d[l, 18 + pair], w_inT_d[l, 20 + pair]])
            wvv, nv = wload([w_inT_d[l, 22 + pair]])
            proj_F(wqk[0], nqk, True, [(0, 64, qk[0], 0, "qk0"), (64, 64, qk[1], 0, "qk1")])
            proj_F(wqk[1], nqk, True, [(0, 64, qk[2], 0, "qk2"), (64, 64, qk[3], 0, "qk3")])
            memset(Vs[:, :, :, 64:128], 1.0, ["V"], eng="pool")
            for i in range(4):
                memset(qk[i][64:128, :], 0.0, ["qk%d" % i], eng="pool")
            for hl in range(2):
                dma("pool", qk[2 + hl][64:72, :], onehot_d, (), ["qk%d" % (2 + hl)])

            def evac(t, b):
                cp(Vs[:, t, :, 0:64], ps[b][:, 0:128].rearrange("p (j c) -> p j c", c=64), ["ps%d" % b], ["V"])
            proj_V(wvv[0], nv, 128, evac)
            bg = bank_aux.next()
            for hl in range(2):
                P.op("dve", (lambda hl_: lambda e: e.tensor_reduce(
                    out=kms[0:64, hl_, :], in_=qk[2 + hl_][0:64, :].rearrange("p (n k) -> p n k", k=256),
                    axis=AX.X, op=ALU.add))(hl), ["qk%d" % (2 + hl)], ["kms%d" % hl])
                ts(kmT[0:64, hl, :], kms[0:64, hl, :], 1.0 / 256, None, ALU.mult, None, ["kms%d" % hl], ["kmT%d" % hl])
                for i in range(8):
                    qt = 8 + i
                    c = (hl * 8 + i) * 8
                    mm(ps[bg][:, c:c + 8], qk[hl][0:64, qt * 128:(qt + 1) * 128], kmT[0:64, hl, :], True, True,
                       ["qk%d" % hl, "kmT%d" % hl], ["ps%d" % bg], (hl == 1 and i == 7))
            memset(gm[:], -1e30, ["gm"])
            for hl in range(2):
                for i in range(8):
                    own = (8 + i) // 2
                    c = (hl * 8 + i) * 8
                    cp(gm[:, hl, i, 0:own], ps[bg][:, c:c + own], ["ps%d" % bg, "gm"], ["gm"])
            for hl in range(2):
                for i in range(8):
                    own = (8 + i) // 2
                    P.op("dve", (lambda hl_, i_: lambda e: e.max(out=mx8[:], in_=gm[:, hl_, i_, :]))(hl, i),
                         ["gm"], ["mx8"])
                    ts(selb[:, hl, i, 0:own], gm[:, hl, i, 0:own], mx8[:, 2:3], NEGBIG, ALU.is_lt, ALU.mult,
                       ["gm", "mx8"], ["selb"])
            for i in range(8):
                qt = 8 + i
                b = bank_aux.next()
                for hl in range(2):
                    mm(ps[b][0:8, hl * 128:(hl + 1) * 128], selb[:, hl, i, :], ident_b[:], True, True,
                       ["selb", "ident_b"], ["ps%d" % b], hl == 1)
                for hl in range(2):
                    act(qk[hl][64:72, qt * 128:(qt + 1) * 128], ps[b][0:8, hl * 128:(hl + 1) * 128], AF.Identity,
                        ["ps%d" % b], ["qk%d" % hl])
            if pair == 0:
                dump("qm0", qk[0], ["qk0"])
                dump("km0", qk[2], ["qk2"])

            def bias_fn(h, kt):
                return None, []
            attend_single(pair, 6 + pair, 128, bias_fn)

        def update_x(half, nslab):
            unames = ["ssu%d" % i for i in range(8)]
            P.op("dve", lambda e: e.reduce_sum(out=ssu[:, 0:8], in_=ssq[:, :, 0:nslab], axis=AX.X),
                 ["ssq%d" % t for t in range(8)], unames)
            rmsn(ssu[:], lnvu[:], rstdu[:], unames, "lnvu", "rstdu", 1.0 / D)
            for t in range(8):
                tg = half * 8 + t
                stt(X[:, tg, :], ybuf[:, t, :], rstdu[:, t:t + 1], X[:, tg, :], ALU.mult, ALU.add,
                    ["yb%d" % t, "rstdu", "X%d" % tg], ["X%d" % tg])

        def evac_y(b, t, sl, ncol):
            cs = slice(sl * ncol, (sl + 1) * ncol)
            ji = junk_ring.next()
            act(junk2[:, ji, 0:ncol], ps[b][:, 0:ncol], AF.Square, ["ps%d" % b], ["ssq%d" % t, "junk%d" % ji],
                accum=ssq[:, t, sl:sl + 1])
            tt(ybuf[:, t, cs], ps[b][:, 0:ncol], G[:, cs], ALU.mult, ["ps%d" % b] + N_G, ["yb%d" % t])

        def merge_half(l, half):
            claim(N_MG)
            for fc in range(8):
                wa, na = wload([w_inT_d[l, 24 + fc], w_inT_d[l, 32 + fc]])
                wb, nb = wload([w_inT_d[l, 40 + fc], w_brT_d[0][l, fc], w_brT_d[1][l, fc], w_brT_d[2][l, fc]])
                for tcl in range(2):
                    tc = half * 2 + tcl
                    cs = slice(tc * 512, (tc + 1) * 512)
                    csl = slice(tcl * 512, (tcl + 1) * 512)
                    for br in range(3):
                        bgk = bank_main.next()
                        wsrc, wnm = (wa[0], na) if br == 0 else ((wa[1], na) if br == 1 else (wb[0], nb))
                        for kc in range(8):
                            mm(ps[bgk][:], wsrc[:, kc, :], hT[:, kc, cs], kc == 0, kc == 7,
                               wnm + ["hT%d" % tc], ["ps%d" % bgk], kc == 7)
                        by = bank_aux.next()
                        rng_ = (range(0, 4), range(4, 6), range(6, 8))[br]
                        kcs = [(wb[1 + br][:, kc - rng_[0], :], nb, kc) for kc in rng_]
                        for i, (wap, wnm2, oc) in enumerate(kcs):
                            mm(ps[by][:], wap, oT[:, oc, cs], i == 0, i == len(kcs) - 1,
                               wnm2 + ["o%d_%d" % (oc, tc)], ["ps%d" % by], i == len(kcs) - 1)
                        sg = T[br % 2]
                        sgn = "T%d" % (br % 2)
                        act(sg, ps[bgk][:], AF.Sigmoid, ["ps%d" % bgk], [sgn])
                        if br == 0:
                            tt(T[2], ps[by][:], sg, ALU.mult, ["ps%d" % by, sgn], ["T2"])
                        elif br == 1:
                            tt(T[3], ps[by][:], sg, ALU.mult, ["ps%d" % by, sgn], ["T3"])
                            tt(T[2], T[2], T[3], ALU.add, ["T2", "T3"], ["T2"])
                        else:
                            tt(T[3], ps[by][:], sg, ALU.mult, ["ps%d" % by, sgn], ["T3"])
                            tt(mg[:, fc, csl], T[2], T[3], ALU.add, ["T2", "T3"], ["mg%d" % tcl])
            if half == 0:
                dump("mg", R12[:, 16384:24576], N_MG)

        def wout_half(l, half):
            claim(N_YB)
            memset(ssq[:], 0.0, ["ssq%d" % t for t in range(8)])
            for sl in range(4):
                wvl, wn = wload([w_outT_d[l, sl]])
                wv = wvl[0]
                for t in range(8):
                    b = bank_main.next()
                    for kc in range(8):
                        mm(ps[b][:, 0:256], mg[:, kc, t * 128:(t + 1) * 128], wv[:, kc, :], kc == 0, kc == 7,
                           wn + ["mg%d" % (t // 4)], ["ps%d" % b], kc == 7)
                    evac_y(b, t, sl, 256)

        def gate_up(l, half):
            claim(N_AC)
            for j in range(NJ):
                wv, wn = wload([w_guT_d[l, j], w_guT_d[l, NJ + j]])
                for tcl in range(2):
                    tc = half * 2 + tcl
                    cs = slice(tc * 512, (tc + 1) * 512)
                    ba = bank_main.next()
                    for kc in range(8):
                        mm(ps[ba][:], wv[0][:, kc, :], hT[:, kc, cs], kc == 0, kc == 7, wn + ["hT%d" % tc],
                           ["ps%d" % ba], kc == 7)
                    bb = bank_aux.next()
                    for kc in range(8):
                        mm(ps[bb][:], wv[1][:, kc, :], hT[:, kc, cs], kc == 0, kc == 7, wn + ["hT%d" % tc],
                           ["ps%d" % bb], kc == 7)
                    sg = T[(j * 2 + tcl) % 2]
                    sgn = "T%d" % ((j * 2 + tcl) % 2)
                    act(sg, ps[ba][:], AF.Silu, ["ps%d" % ba], [sgn])
                    tt(actT[:, j, tcl * 512:(tcl + 1) * 512], ps[bb][:], sg, ALU.mult, ["ps%d" % bb, sgn], ["ac%d" % tcl])

        def down(l, half):
            claim(N_YB)
            memset(ssq[:], 0.0, ["ssq%d" % t for t in range(8)])
            for fcol in range(8):
                wvl, wn = wload([w_dnT_d[l, fcol]])
                wv = wvl[0]
                for t in range(8):
                    b = bank_main.next()
                    for kc in range(NJ):
                        mm(ps[b][:, 0:128], actT[:, kc, t * 128:(t + 1) * 128], wv[:, kc, :], kc == 0, kc == NJ - 1,
                           wn + ["ac%d" % (t // 4)], ["ps%d" % b], kc == NJ - 1)
                    evac_y(b, t, fcol, 128)
            update_x(half, 8)

        def ffn(l, s):
            stage_a(list(range(0, 8)), 16)
            gate_up(l, 0)
            stage_a(list(range(8, 16)), 16)
            down(l, 0)
            if l == layers[-1]:
                finish_tiles(s, range(0, 8))
            gate_up(l, 1)
            down(l, 1)
            if l == layers[-1]:
                finish_tiles(s, range(8, 16))

        def finish_tiles(s, tiles):
            for t in tiles:
                dma("sp", out_d[s, t * 128:(t + 1) * 128, :], X[:, t, :], ["X%d" % t], ())
            if s + 1 < n_seq:
                for t in tiles:
                    dma("sp", X[:, t, :], x_d[s + 1, t * 128:(t + 1) * 128, :], (), ["X%d" % t])

        def program():
            for s in range(n_seq):
                if s == 0:
                    for t in range(NT):
                        dma("sp", X[:, t, :], x_d[s, t * 128:(t + 1) * 128, :], (), ["X%d" % t])
                for kc in range(8):
                    cp(cbc[:, kc, :], ca[:, kc * 2 + s: kc * 2 + s + 1].to_broadcast([128, 128]), ["ca"], ["cbc"])
                rope_tables(s)
                for l in layers:
                    adaln_cols(s, l)
                    adaln_G(s, l, 0)
                    stage_a(list(range(NT)), 0)
                    dump("hT", hT[:], ["hT%d" % i for i in range(4)])
                    if stop == "a":
                        return
                    claim(N_O)
                    for h in range(4):
                        branch_diff(l, h)
                        if stop == "d0":
                            dump("oT2", R12[:, 0:4096], N_O)
                            return
                    for pair in range(2):
                        branch_fox(l, pair)
                    for pair in range(2):
                        branch_moba(l, pair)
                    dump("oT", R12[:, 0:16384], N_O)
                    if stop == "attn":
                        return
                    for half in range(2):
                        merge_half(l, half)
                        wout_half(l, half)
                        update_x(half, 4)
                    dump("xmix", X[:], ["X%d" % t for t in range(NT)])
                    if stop == "mix":
                        return
                    adaln_G(s, l, 1)
                    ffn(l, s)

        def lam_init(l):
            return 0.8 - 0.6 * math.exp(-0.3 * l)

        program()
        P.wait_tokens("sp", P.all_tokens())
        P.replay()
    nc._dbg_names = list(dbg_d.keys())
    nc._n_instr = {e: len(P.streams[e]) for e in ENGS}
    return nc


def host_consts():
    inv_freq = 1.0 / (10000.0 ** (np.arange(0, 64, 2, dtype=np.float32) / 64.0))
    invf = np.tile(inv_freq.astype(np.float32), 4).reshape(128, 1)
    rmat = np.zeros((128, 128), np.float32)
    for m in range(128):
        if m % 64 < 32:
            rmat[m + 32, m] = -1.0
        else:
            rmat[m - 32, m] = 1.0
    ident = np.eye(128, dtype=np.float32)
    mask01 = (np.arange(128)[:, None] <= np.arange(128)[None, :]).astype(np.float32)
    onehot = (np.arange(S)[None, :] // 256 == np.arange(8)[:, None]).astype(np.float32)
    negmask = np.where(np.arange(128)[:, None] > np.arange(128)[None, :], -30000.0, 0.0).astype(np.float32)
    return dict(invf=invf, rmat=rmat, ident=ident, mask01=mask01, onehot=onehot, negmask=negmask)


def make_in_maps(inputs, n_cores=8, n_seq=2):
    f = lambda a: np.ascontiguousarray(np.asarray(a))
    consts = host_consts()
    b_ada = f(inputs["b_ada"])
    badaT = np.ascontiguousarray(b_ada.reshape(2, 48, 128).transpose(0, 2, 1))
    gpm = f(inputs["g_pre_mix"]).reshape(2, 8, 128).transpose(0, 2, 1)
    gpf = f(inputs["g_pre_ffn"]).reshape(2, 8, 128).transpose(0, 2, 1)
    gsl = f(inputs["g_subln"]).reshape(2, 128, 1)
    gcols = np.ascontiguousarray(np.concatenate([gpm, gpf, gsl], axis=2))
    def tile_w(w, ncol):
        w = f(w)
        L, K, N = w.shape
        return np.ascontiguousarray(w.reshape(L, K // 128, 128, N // ncol, ncol).transpose(0, 3, 2, 1, 4))
    w_in = f(inputs["w_in"])
    shared = dict(
        w_adaT=tile_w(inputs["w_ada"], 128), b_ada=b_ada, badaT=badaT, gcols=gcols,
        g_post_mix=f(inputs["g_post_mix"]), g_post_ffn=f(inputs["g_post_ffn"]),
        w_inT=tile_w(w_in[:, :, 0:6144], 128), w_fgt=np.ascontiguousarray(w_in[:, :, 6144:6148]),
        b_fgt=f(inputs["b_fgt"]),
        lam_q1=f(inputs["lam_q1"]), lam_k1=f(inputs["lam_k1"]), lam_q2=f(inputs["lam_q2"]), lam_k2=f(inputs["lam_k2"]),
        w_braT=tile_w(inputs["w_br_a"], 128), w_brbT=tile_w(inputs["w_br_b"], 128), w_brcT=tile_w(inputs["w_br_c"], 128),
        w_outT=tile_w(inputs["w_out"], 256), w_guT=tile_w(inputs["w_gate_up"], 128),
        w_dnT=tile_w(inputs["w_down"], 128), **consts)
    x = f(inputs["x"])
    c = f(inputs["c"])
    pos = f(inputs["positions"]).astype(np.int32)
    maps = []
    for i in range(n_cores):
        bs = slice(i * n_seq, (i + 1) * n_seq)
        cc = c[bs]
        cT = np.zeros((128, 16), np.float32)
        cT[:, : 8 * 2] = 0
        for b in range(n_seq):
            cT[:, b::2][:, :8] = cc[b].reshape(8, 128).T
        m = dict(shared)
        m.update(x=np.ascontiguousarray(x[bs]), cT=cT, pos=np.ascontiguousarray(pos[bs]))
        maps.append(m)
    return maps


def kernel(**inputs):
    nc = build(n_seq=2, layers=(0, 1))
    maps = make_in_maps(inputs, 8, 2)
    res = run_bass_kernel_spmd(nc, maps, core_ids=list(range(8)))
    return np.concatenate([r["out"] for r in res.results], axis=0).astype(np.float32)
```

```python
import math
from contextlib import ExitStack
import numpy as np
import concourse.bass as bass
import concourse.mybir as mybir
from concourse.bass_utils import run_bass_kernel_spmd

F32 = mybir.dt.float32
BF16 = mybir.dt.bfloat16
I32 = mybir.dt.int32
AF = mybir.ActivationFunctionType
ALU = mybir.AluOpType
AX = mybir.AxisListType

ENGS = ("pe", "act", "dve", "pool", "sp")
N_DMA_SEMS = 24

D = 1024
S = 2048
NT = 16
DFF = 2816
NJ = 22
EPS = 1e-6
NEGBIG = -1920.0
PI = math.pi


class Plan:
    def __init__(self, nc):
        self.nc = nc
        self.streams = {e: [] for e in ENGS}
        self.count = {e: 0 for e in ENGS}
        self.wm = {e: {} for e in ENGS}
        self.res = {}
        self.dma_cnt = [0] * N_DMA_SEMS
        self.dma_rr = {"pool": 0, "sp": 0, "act": 0}
        self.sems = {}

    def _need(self, eng, reads, writes):
        need = {}

        def add(tok):
            if tok is None:
                return
            k, v = tok
            if need.get(k, 0) < v:
                need[k] = v

        for r in reads:
            ent = self.res.get(r)
            if ent is not None:
                add(ent[0])
        for w in writes:
            ent = self.res.get(w)
            if ent is not None:
                add(ent[0])
                for k, v in ent[1].items():
                    add((k, v))
        out = []
        for k, v in need.items():
            if k == "pe" and eng == "pe":
                continue
            if self.wm[eng].get(k, 0) >= v:
                continue
            self.wm[eng][k] = v
            out.append((k, v))
        return out

    def _record(self, tok, reads, writes):
        k, v = tok
        for r in reads:
            ent = self.res.setdefault(r, [None, {}])
            if ent[1].get(k, 0) < v:
                ent[1][k] = v
        for w in writes:
            self.res[w] = [tok, {}]

    def op(self, eng, fn, reads=(), writes=(), signal=True):
        writes = list(writes) + [r for r in reads if r.startswith("ps")]
        reads = [r for r in reads if not r.startswith("ps")]
        waits = self._need(eng, reads, writes)
        if signal:
            self.count[eng] += 1
            tok = (eng, self.count[eng])
            inc = (eng, 1)
        else:
            tok = (eng, self.count[eng] + 1)
            inc = None
        self._record(tok, reads, writes)
        self.streams[eng].append((waits, fn, inc))
        return tok

    def dma(self, eng, fn, reads=(), writes=()):
        half = N_DMA_SEMS // 2
        base = 0 if eng == "pool" else half
        s = base + self.dma_rr[eng]
        self.dma_rr[eng] = (self.dma_rr[eng] + 1) % half
        key = "dma%d" % s
        waits = self._need(eng, reads, writes)
        prev = 16 * self.dma_cnt[s]
        if prev and self.wm[eng].get(key, 0) < prev:
            self.wm[eng][key] = prev
            waits.append((key, prev))
        self.dma_cnt[s] += 1
        tok = (key, 16 * self.dma_cnt[s])
        self._record(tok, reads, writes)
        self.streams[eng].append((waits, fn, (key, 16)))
        return tok

    def fence(self, new, old):
        merged = {}
        for o in old:
            ent = self.res.get(o)
            if ent is None:
                continue
            if ent[0] is not None:
                k, v = ent[0]
                merged[k] = max(merged.get(k, 0), v)
            for k, v in ent[1].items():
                merged[k] = max(merged.get(k, 0), v)
        for n in new:
            ent = self.res.get(n)
            m2 = dict(merged)
            if ent is not None:
                if ent[0] is not None:
                    k, v = ent[0]
                    m2[k] = max(m2.get(k, 0), v)
                for k, v in ent[1].items():
                    m2[k] = max(m2.get(k, 0), v)
            self.res[n] = [None, m2]

    def wait_tokens(self, eng, toks):
        waits = []
        for k, v in toks:
            if self.wm[eng].get(k, 0) < v:
                self.wm[eng][k] = v
                waits.append((k, v))
        self.streams[eng].append((waits, None, None))

    def all_tokens(self):
        toks = {}
        for ent in self.res.values():
            if ent[0] is not None:
                k, v = ent[0]
                toks[k] = max(toks.get(k, 0), v)
            for k, v in ent[1].items():
                toks[k] = max(toks.get(k, 0), v)
        return list(toks.items())

    def replay(self):
        nc = self.nc
        with ExitStack() as es:
            for e in ENGS:
                self.sems[e] = es.enter_context(nc.semaphore("s_" + e))
            for i in range(N_DMA_SEMS):
                self.sems["dma%d" % i] = es.enter_context(nc.semaphore("s_dma%d" % i))
            block = es.enter_context(nc.Block())
            sems = self.sems

            def run(engname):
                def body(eng):
                    for waits, fn, inc in self.streams[engname]:
                        for k, v in waits:
                            eng.wait_ge(sems[k], v)
                        if fn is None:
                            continue
                        ins = fn(eng)
                        if inc is not None:
                            ins.then_inc(sems[inc[0]], inc[1])
                return body

            block.tensor(run("pe"))
            block.scalar(run("act"))
            block.vector(run("dve"))
            block.gpsimd(run("pool"))
            block.sync(run("sp"))


class Ring:
    def __init__(self, items):
        self.items = list(items)
        self.i = 0

    def next(self):
        v = self.items[self.i]
        self.i = (self.i + 1) % len(self.items)
        return v


def build(n_seq=2, layers=(0, 1), dbg=(), stop=None):
    nc = bass.Bass("TRN2", target_bir_lowering=False)

    def din(name, shape, dt=F32):
        return nc.dram_tensor(name, list(shape), dt, kind="ExternalInput").ap()

    x_d = din("x", [n_seq, S, D])
    cT_d = din("cT", [128, 16])
    pos_d = din("pos", [n_seq, S], I32)
    w_adaT_d = din("w_adaT", [2, 48, 128, 8, 128])
    b_ada_d = din("b_ada", [2, 6 * D])
    badaT_d = din("badaT", [2, 128, 48])
    gcols_d = din("gcols", [2, 128, 17])
    g_post_mix_d = din("g_post_mix", [2, D])
    g_post_ffn_d = din("g_post_ffn", [2, D])
    w_inT_d = din("w_inT", [2, 48, 128, 8, 128])
    w_fgt_d = din("w_fgt", [2, D, 4])
    b_fgt_d = din("b_fgt", [2, 4])
    lam_d = [din(n, [2, 64]) for n in ("lam_q1", "lam_k1", "lam_q2", "lam_k2")]
    w_brT_d = [din("w_braT", [2, 8, 128, 4, 128]), din("w_brbT", [2, 8, 128, 2, 128]),
               din("w_brcT", [2, 8, 128, 2, 128])]
    w_outT_d = din("w_outT", [2, 4, 128, 8, 256])
    w_guT_d = din("w_guT", [2, 44, 128, 8, 128])
    w_dnT_d = din("w_dnT", [2, 8, 128, NJ, 128])
    invf_d = din("invf", [128, 1])
    rmat_d = din("rmat", [128, 128])
    ident_d = din("ident", [128, 128])
    mask_d = din("mask01", [128, 128])
    negmask_d = din("negmask", [128, 128])
    onehot_d = din("onehot", [8, S])
    out_d = nc.dram_tensor("out", [n_seq, S, D], F32, kind="ExternalOutput").ap()
    dbg_d = {}

    es = ExitStack()
    with es:
        def sb(name, shape, dt):
            return es.enter_context(nc.sbuf_tensor(name, list(shape), dt))

        X = sb("X", [128, NT, D], F32)
        hT = sb("hT", [128, 8, S], BF16)
        R12 = sb("R12", [128, 24576], BF16)
        R3 = sb("R3", [128, 8192], BF16)
        tabC = sb("tabC", [128, S], BF16)
        tabS = sb("tabS", [128, S], BF16)
        NSLOT = 3
        wslot = [sb("wslot%d" % i, [128, 2816], BF16) for i in range(NSLOT)]
        G = sb("G", [128, D], F32)
        SCR = sb("SCR", [128, 2048], F32)
        ident_b = sb("ident_b", [128, 128], BF16)
        negmask_b = sb("negmask_b", [128, 128], BF16)
        rmat_b = sb("rmat_b", [128, 128], BF16)
        ones_b = sb("ones_b", [128, 128], BF16)
        U_f = sb("U_f", [128, 128], F32)
        ones_f = sb("ones_f", [128, 128], F32)
        invf = sb("invf_s", [128, 1], F32)
        halfpi = sb("halfpi", [128, 1], F32)
        modc1 = sb("modc1", [128, 2, 48], F32)
        cTs = sb("cTs", [128, 16], F32)
        ca = sb("ca", [128, 16], BF16)
        cbc = sb("cbc", [128, 8, 128], BF16)
        badaT = sb("badaT_s", [128, 48], F32)
        gcols = sb("gcols_s", [128, 17], F32)
        modc = sb("modc", [128, 48], F32)
        scsh = sb("scsh", [128, 32], F32)
        ss = sb("ss", [128, 16], F32)
        lnv = sb("lnv", [128, 16], F32)
        rstd = sb("rstd", [128, 16], F32)
        lamt = sb("lamt", [128, 4, 64], F32)
        lamp = sb("lamp", [128, 64], F32)
        lams = sb("lams", [128, 4], F32)
        neglam = sb("neglam", [128, 1], F32)
        gcol = sb("gcol", [128, 1], F32)
        bfrep = sb("bfrep", [128, 4], F32)
        zb = sb("zb", [128, 4, NT], F32)
        lf = sb("lf", [128, 4, NT], F32)
        tots = sb("tots", [128, 4, NT], F32)
        offs = sb("offs", [128, 4, NT], F32)
        Lcum = sb("Lcum", [128, 4, NT], F32)
        r8 = sb("r8", [128, 4, NT], BF16)
        kmT = sb("kmT", [128, 2, 8], BF16)
        kms = sb("kms", [128, 2, 8], F32)
        gm = sb("gm", [128, 2, 8, 8], F32)
        mx8 = sb("mx8", [128, 8], F32)
        selb = sb("selb", [128, 2, 8, 8], BF16)
        ssq = sb("ssq", [128, 8, 8], F32)
        ssu = sb("ssu", [128, 8], F32)
        lnvu = sb("lnvu", [128, 8], F32)
        rstdu = sb("rstdu", [128, 8], F32)
        junk2 = sb("junk2", [128, 4, 256], BF16)
        junk_ring = Ring([0, 1, 2, 3])

        ps = [es.enter_context(nc.psum_tensor("ps%d" % i, [128, 512], F32)) for i in range(8)]

        oT = R12[:, 0:16384].rearrange("p (c t) -> p c t", t=S)
        qk = [R12[:, 16384 + i * S: 16384 + (i + 1) * S] for i in range(4)]
        mg = R12[:, 16384:24576].rearrange("p (c t) -> p c t", t=1024)
        actT = R12[:, 0:22528].rearrange("p (c t) -> p c t", t=1024)

        Vd1 = R3[:, 0:2048].rearrange("p (t c) -> p t c", c=128)
        Vs = R3[:, 0:4096].rearrange("p (t j c) -> p t j c", j=2, c=128)
        PT = [R3[:, 4096 + i * 512: 4096 + (i + 1) * 512] for i in range(8)]
        ybuf = R3[:, 0:8192].rearrange("p (t c) -> p t c", c=1024)
        posi = R3[:, 0:4096].bitcast(I32)
        T = [SCR[:, i * 512:(i + 1) * 512] for i in range(4)]
        Ti = [T[i].bitcast(I32) for i in range(4)]
        xn = [T[2].bitcast(BF16), T[3].bitcast(BF16)]
        XN_NAMES = ["T2", "T3"]
        junkA = junk2[:].rearrange("p a b -> p (a b)")
        deferred = []

        def run_deferred(ring):
            while deferred:
                deferred.pop(0)(ring)

        N_O = ["o%d_%d" % (c, t) for c in range(8) for t in range(4)]
        N_AC = ["ac0", "ac1"]
        N_QK = ["qk0", "qk1", "qk2", "qk3"]
        N_MG = ["mg0", "mg1"]
        N_XN = ["xn0", "xn1"]
        N_V = ["V"]
        N_PT = ["pt%d" % i for i in range(8)]
        N_YB = ["yb%d" % i for i in range(8)]
        N_POSI = ["posi"]
        REG_A = N_O + N_AC
        REG_B = N_QK + N_MG + N_AC
        REG_C = N_V + N_PT + N_YB + N_POSI
        N_T = ["T0", "T1", "T2", "T3"]
        WNAMES = [["w%d_%d" % (s_, j) for j in range(4)] for s_ in range(NSLOT)]

        P = Plan(nc)

        def claim(names):
            for reg in (REG_A, REG_B, REG_C):
                mine = [n for n in names if n in reg]
                if mine:
                    P.fence(mine, [n for n in reg if n not in mine])

        def mm(out, lhsT, rhs, start, stop, reads, writes, signal):
            P.op("pe", lambda e: e.matmul(out, lhsT=lhsT, rhs=rhs, start=start, stop=stop),
                 reads, writes, signal)

        def tr(out, in_, reads, writes, signal=True):
            P.op("pe", lambda e: e.transpose(out=out, in_=in_, identity=ident_b[:]), reads, writes, signal)

        def act(out, in_, func, reads, writes, bias=None, scale=None, accum=None):
            kw = {}
            if bias is not None:
                kw["bias"] = bias
            if scale is not None:
                kw["scale"] = scale
            if accum is not None:
                kw["accum_out"] = accum
            P.op("act", lambda e: e.activation(out=out, in_=in_, func=func, **kw), reads, writes)

        def tt(out, in0, in1, op, reads, writes, eng="dve"):
            P.op(eng, lambda e: e.tensor_tensor(out=out, in0=in0, in1=in1, op=op), reads, writes)

        def ts(out, in0, s1, s2, op0, op1, reads, writes, eng="dve"):
            if op1 is None:
                P.op(eng, lambda e: e.tensor_scalar(out=out, in0=in0, scalar1=s1, scalar2=None, op0=op0), reads, writes)
            else:
                P.op(eng, lambda e: e.tensor_scalar(out=out, in0=in0, scalar1=s1, scalar2=s2, op0=op0, op1=op1), reads, writes)

        def stt(out, in0, scalar, in1, op0, op1, reads, writes, eng="dve"):
            P.op(eng, lambda e: e.scalar_tensor_tensor(out=out, in0=in0, scalar=scalar, in1=in1, op0=op0, op1=op1),
                 reads, writes)

        def cp(out, in_, reads, writes, eng="dve"):
            P.op(eng, lambda e: e.tensor_copy(out=out, in_=in_), reads, writes)

        def recip(out, in_, reads, writes):
            act(out, in_, AF.Ln, reads, writes)
            act(out, out, AF.Exp, list(writes), writes, scale=-1.0)

        def memset(ap, val, writes, eng="dve"):
            P.op(eng, lambda e: e.memset(ap, val), (), writes)

        def dma(eng, out, in_, reads, writes):
            P.dma(eng, lambda e: e.dma_start(out=out, in_=in_), reads, writes)

        def dump(name, ap, reads):
            if name not in dbg:
                return
            d = nc.dram_tensor("dbg_" + name, list(ap.shape), ap.dtype, kind="ExternalOutput").ap()
            dbg_d[name] = d
            dma("sp", d, ap, reads, ())

        bank_main = Ring([0, 1, 2, 3])
        bank_aux = Ring([4, 5, 6, 7])
        pt_ring = Ring(list(range(8)))
        slot_ring = Ring(list(range(NSLOT)))

        def wload(pieces):
            s_ = slot_ring.next()
            views = []
            off = 0
            for j, src in enumerate(pieces):
                kc_, n_ = src.shape[1], src.shape[2]
                v = wslot[s_][:, off:off + kc_ * n_].rearrange("p (k n) -> p k n", n=n_)
                off += kc_ * n_
                dma("pool", v, src, (), [WNAMES[s_][j]])
                views.append(v)
            assert off <= 2816
            return views, WNAMES[s_]

        dma("pool", ident_b[:], ident_d, (), ["ident_b"])
        dma("pool", negmask_b[:], negmask_d, (), ["negmask_b"])
        dma("pool", rmat_b[:], rmat_d, (), ["rmat_b"])
        dma("sp", U_f[:], mask_d, (), ["U_f"])
        dma("sp", invf[:], invf_d, (), ["invf"])
        dma("sp", cTs[:], cT_d, (), ["cTs"])
        memset(ones_b[:], 1.0, ["ones_b"])
        memset(halfpi[:], PI / 2, ["halfpi"])
        memset(ones_f[:], 1.0, ["ones_f"])
        memset(selb[:], 0.0, ["selb"])
        act(ca[:], cTs[:], AF.Silu, ["cTs"], ["ca"])

        def rmsn(ss_ap, lnv_ap, rstd_ap, in_names, ln_name, out_name, scale):
            act(lnv_ap, ss_ap, AF.Ln, list(in_names), [ln_name], bias=EPS, scale=scale)
            act(rstd_ap, lnv_ap, AF.Exp, [ln_name], [out_name], scale=-0.5)

        def rope_tables(s):
            claim(N_POSI)
            dma("sp", posi, pos_d[s:s + 1, :].to_broadcast([128, S]), (), ["posi"])
            for j in range(4):
                cs = slice(j * 512, (j + 1) * 512)
                ts(T[0], posi[:, cs], invf[:, 0:1], None, ALU.mult, None, ["posi", "invf"], ["T0"])
                ts(Ti[1], T[0], 1.0 / (2 * PI), None, ALU.mult, None, ["T0"], ["T1"])
                stt(T[2], Ti[1], -2 * PI, T[0], ALU.mult, ALU.add, ["T1", "T0"], ["T2"])
                ts(T[3], T[2], PI, -2 * PI, ALU.is_gt, ALU.mult, ["T2"], ["T3"])
                tt(T[2], T[2], T[3], ALU.add, ["T2", "T3"], ["T2"])
                ts(T[2], T[2], -PI, PI, ALU.max, ALU.min, ["T2"], ["T2"])
                act(tabS[:, cs], T[2], AF.Sin, ["T2"], ["tab0_%d" % j])
                stt(T[3], T[2], -1.0, T[2], ALU.mult, ALU.max, ["T2"], ["T3"])
                act(tabC[:, cs], T[3], AF.Sin, ["T3", "halfpi"], ["tab1_%d" % j], bias=halfpi[:, 0:1], scale=-1.0)
            dump("tabC", tabC[:], ["tab1_%d" % j for j in range(4)])
            dump("tabS", tabS[:], ["tab0_%d" % j for j in range(4)])

        def adaln_cols(s, l):
            dma("sp", badaT[:], badaT_d[l], (), ["badaT"])
            dma("sp", gcols[:], gcols_d[l], (), ["gcols"])
            if s == 0:
                b = bank_aux.next()
                chunks = list(range(0, 16)) + list(range(24, 40))
                for i in range(0, len(chunks), 2):
                    j0 = chunks[i]
                    wv, wn = wload([w_adaT_d[l, j0], w_adaT_d[l, j0 + 1]])
                    for jj in range(2):
                        j = j0 + jj
                        for kc in range(8):
                            mm(ps[b][:, 2 * j:2 * j + n_seq], wv[jj][:, kc, :],
                               ca[:, kc * 2: kc * 2 + n_seq], kc == 0, kc == 7, wn + ["ca"], ["ps%d" % b], kc == 7)
                psv = ps[b][:, 0:96].rearrange("p (j b) -> p j b", b=2)
                for (a0, a1) in ((0, 16), (24, 40)):
                    tt(modc[:, a0:a1], psv[:, a0:a1, 0], badaT[:, a0:a1], ALU.add, ["ps%d" % b, "badaT", "modc"], ["modc"])
                    if n_seq > 1:
                        tt(modc1[:, l, a0:a1], psv[:, a0:a1, 1], badaT[:, a0:a1], ALU.add,
                           ["ps%d" % b, "badaT", "modc1_%d" % l], ["modc1_%d" % l])
            else:
                for (a0, a1) in ((0, 16), (24, 40)):
                    cp(modc[:, a0:a1], modc1[:, l, a0:a1], ["modc1_%d" % l, "modc"], ["modc"])
            stt(scsh[:, 0:8], modc[:, 8:16], 1.0, gcols[:, 0:8], ALU.add, ALU.mult, ["modc", "gcols"], ["scsh"])
            cp(scsh[:, 8:16], modc[:, 0:8], ["modc"], ["scsh"])
            stt(scsh[:, 16:24], modc[:, 32:40], 1.0, gcols[:, 8:16], ALU.add, ALU.mult, ["modc", "gcols"], ["scsh"])
            cp(scsh[:, 24:32], modc[:, 24:32], ["modc"], ["scsh"])
            ts(gcol[:], gcols[:, 16:17], 1.0 - lam_init(l), None, ALU.mult, None, ["gcols"], ["gcol"])
            for i in range(4):
                dma("sp", lamt[:, i, :], lam_d[i][l:l + 1, :].to_broadcast([128, 64]), (), ["lamt%d" % i])
            for i in range(2):
                tt(lamp[:], lamt[:, 2 * i, :], lamt[:, 2 * i + 1, :], ALU.mult,
                   ["lamt%d" % (2 * i), "lamt%d" % (2 * i + 1)], ["lamp"])
                P.op("dve", (lambda i_: lambda e: e.reduce_sum(out=lams[:, i_:i_ + 1], in_=lamp[:], axis=AX.X))(i),
                     ["lamp"], ["lams%d" % i])
            act(lams[:, 2:4], lams[:, 0:2], AF.Exp, ["lams0", "lams1"], ["lamse"])
            tt(neglam[:], lams[:, 3:4], lams[:, 2:3], ALU.subtract, ["lamse"], ["neglam"])
            ts(neglam[:], neglam[:], -lam_init(l), None, ALU.add, None, ["neglam"], ["neglam"])
            dma("sp", bfrep[:], b_fgt_d[l:l + 1, :].to_broadcast([128, 4]), (), ["bfrep"])

        def adaln_G(s, l, which):
            c0 = 2048 if which == 0 else 5120
            gp = g_post_mix_d if which == 0 else g_post_ffn_d
            brep = SCR[:, 0:1024]
            grep = SCR[:, 1024:2048]
            dma("sp", brep, b_ada_d[l:l + 1, c0:c0 + 1024].to_broadcast([128, 1024]), (), ["T0", "T1"])
            dma("sp", grep, gp[l:l + 1, :].to_broadcast([128, 1024]), (), ["T2", "T3"])
            for q in range(4):
                jq = c0 // 128 + 2 * q
                wv, wn = wload([w_adaT_d[l, jq], w_adaT_d[l, jq + 1]])
                b = bank_main.next()
                for jj in range(2):
                    for kc in range(8):
                        mm(ps[b][:, jj * 128:(jj + 1) * 128], cbc[:, kc, :], wv[jj][:, kc, :], kc == 0, kc == 7,
                           wn + ["cbc"], ["ps%d" % b], kc == 7)
                cs = slice(q * 256, (q + 1) * 256)
                tt(G[:, cs], ps[b][:, 0:256], brep[:, cs], ALU.add, ["ps%d" % b, "T0", "T1"], ["G%d" % q])
                tt(G[:, cs], G[:, cs], grep[:, cs], ALU.mult, ["G%d" % q, "T2", "T3"], ["G%d" % q])

        N_G = ["G0", "G1", "G2", "G3"]

        def stage_a(tiles, off):
            ngrp = len(tiles) // 4

            def sq(g):
                grp = tiles[4 * g:4 * g + 4]
                c0 = grp[0]
                names = ["ss%d" % t for t in grp]
                memset(ss[:, c0:c0 + 4], 0.0, names)
                for t in grp:
                    act(junkA, X[:, t, :], AF.Square, ["X%d" % t], ["junk0", "junk1", "junk2", "junk3", "ss%d" % t],
                        accum=ss[:, t:t + 1])
                rmsn(ss[:, c0:c0 + 4], lnv[:, c0:c0 + 4], rstd[:, c0:c0 + 4], names, "lnv%d" % (c0 // 4),
                     "rstd%d" % (c0 // 4), 1.0 / D)

            sq(0)
            for g in range(ngrp):
                if g + 1 < ngrp:
                    sq(g + 1)
                grp = tiles[4 * g:4 * g + 4]
                tc = grp[0] // 4
                banks = [bank_main.next(), bank_main.next(), bank_aux.next(), bank_aux.next()]
                for i, t in enumerate(grp):
                    ts(xn[i % 2], X[:, t, :], rstd[:, t:t + 1], None, ALU.mult, None,
                       ["X%d" % t, "rstd%d" % tc], [XN_NAMES[i % 2]])
                    for kc in range(8):
                        b = banks[kc // 2]
                        pv = ps[b][:].bitcast(BF16)
                        o0 = (kc % 2) * 512 + i * 128
                        tr(pv[:, o0:o0 + 128], xn[i % 2][:, kc * 128:(kc + 1) * 128],
                           [XN_NAMES[i % 2], "ident_b"], ["ps%d" % b], signal=(kc == 7))
                for kc in range(8):
                    b = banks[kc // 2]
                    pv = ps[b][:].bitcast(BF16)
                    o0 = (kc % 2) * 512
                    if kc % 2 == 0:
                        act(hT[:, kc, tc * 512:(tc + 1) * 512], pv[:, o0:o0 + 512], AF.Identity,
                            ["ps%d" % b, "scsh"], ["hT%d" % tc],
                            bias=scsh[:, off + 8 + kc: off + 9 + kc], scale=scsh[:, off + kc: off + kc + 1])
                    else:
                        ts(hT[:, kc, tc * 512:(tc + 1) * 512], pv[:, o0:o0 + 512],
                           scsh[:, off + kc: off + kc + 1], scsh[:, off + 8 + kc: off + 9 + kc], ALU.mult, ALU.add,
                           ["ps%d" % b, "scsh"], ["hT%d" % tc])

        def proj_F(wp, wn, rope, dests):
            tail = [None]
            for tc in range(4):
                cs = slice(tc * 512, (tc + 1) * 512)
                b = bank_main.next()
                for kc in range(8):
                    mm(ps[b][:], wp[:, kc, :], hT[:, kc, cs], kc == 0, kc == 7,
                       wn + ["hT%d" % tc], ["ps%d" % b], kc == 7)
                if tc == 0:
                    run_deferred(bank_aux)
                if not rope:
                    for (r0, nr, tile, d0, nm) in dests:
                        cp(tile[d0:d0 + nr, cs], ps[b][r0:r0 + nr, :], ["ps%d" % b], [nm])
                else:
                    pi = pt_ring.next()
                    act(PT[pi], ps[b][:], AF.Identity, ["ps%d" % b], ["pt%d" % pi])
                    if tail[0] is not None:
                        tail[0]()

                    def mk(b=b, pi=pi, cs=cs, tc=tc):
                        b2 = bank_aux.next()
                        mm(ps[b2][:], rmat_b[:], PT[pi], True, True, ["rmat_b", "pt%d" % pi], ["ps%d" % b2], True)
                        tt(T[0], ps[b][:], tabC[:, cs], ALU.mult, ["ps%d" % b, "tab1_%d" % tc], ["T0"])
                        tt(T[1], ps[b2][:], tabS[:, cs], ALU.mult, ["ps%d" % b2, "tab0_%d" % tc], ["T1"])
                        for (r0, nr, tile, d0, nm) in dests:
                            tt(tile[d0:d0 + nr, cs], T[0][r0:r0 + nr, :], T[1][r0:r0 + nr, :], ALU.add,
                               ["T0", "T1"], [nm])
                    tail[0] = mk
            if tail[0] is not None:
                tail[0]()

        def proj_V(wp, wn, N, evac, extra=None):
            for t in range(NT):
                b = bank_main.next()
                for kc in range(8):
                    mm(ps[b][:, 0:N], hT[:, kc, t * 128:(t + 1) * 128], wp[:, kc, 0:N], kc == 0, kc == 7,
                       wn + ["hT%d" % (t // 4)], ["ps%d" % b], kc == 7)
                if extra is not None:
                    wp2, n2 = extra
                    for kc in range(8):
                        mm(ps[b][:, N:N + n2], hT[:, kc, t * 128:(t + 1) * 128], wp2[:, kc, :], kc == 0, kc == 7,
                           wn + ["hT%d" % (t // 4)], ["ps%d" % b], kc == 7)
                evac(t, b)

        att_pend = []
        att_cb = {}
        cid_ctr = [0]

        def emit_pv(st):
            m, kt, nk, pi, c0, cid = st
            for (ab, lfn, rds) in m["pv"]:
                mm(ps[ab][:, c0:512], lfn(kt), PT[pi][:, c0:512], kt == 0, kt == nk - 1,
                   ["pt%d" % pi] + rds, ["ps%d" % ab], True)

        def drain_check():
            live = set(e[5] for e in att_pend)
            for cid in list(att_cb.keys()):
                if cid not in live:
                    att_cb.pop(cid)()

        def att_flush():
            while att_pend:
                emit_pv(att_pend.pop(0))
            drain_check()

        def attend_chunk(qc, maps, st_ring, cid):
            nk = 4 * qc + 4
            LAG = 3 if len(st_ring.items) >= 4 else 2
            nstep = [0]
            for kt in range(nk):
                for mi, m in enumerate(maps):
                    j = kt - 4 * qc
                    c0 = max(j, 0) * 128
                    b = st_ring.next()
                    diag = j >= 0
                    mm(ps[b][:, c0:512], m["k"][:, kt * 128:(kt + 1) * 128], m["q"][:, qc * 512 + c0:(qc + 1) * 512],
                       True, not diag, m["rn"], ["ps%d" % b], not diag)
                    if diag:
                        mm(ps[b][:, c0:c0 + 128], ident_b[:], negmask_b[:], False, True,
                           ["ident_b", "negmask_b"], ["ps%d" % b], True)
                    pi = pt_ring.next()
                    bias_ap, bias_names = m["bias"](kt)
                    act(PT[pi][:, c0:512], ps[b][:, c0:512], AF.Exp, ["ps%d" % b] + bias_names, ["pt%d" % pi],
                        bias=bias_ap, scale=0.125)
                    att_pend.append((m, kt, nk, pi, c0, cid))
                    while len(att_pend) > LAG:
                        emit_pv(att_pend.pop(0))
                        drain_check()
                    nstep[0] += 1
                    if nstep[0] == 4:
                        run_deferred(st_ring)

        ACC6 = [2, 3, 4, 5, 6, 7]
        diff_chunk_ctr = [0]
        st2 = Ring([0, 1])
        st4 = Ring([0, 1, 2, 3])

        def attend_diff(h):
            for qc in range(4):
                c = diff_chunk_ctr[0]
                diff_chunk_ctr[0] += 1
                A = [ACC6[(4 * c) % 6], ACC6[(4 * c + 2) % 6]]
                Sm = [ACC6[(4 * c + 1) % 6], ACC6[(4 * c + 3) % 6]]
                cid = cid_ctr[0]
                cid_ctr[0] += 1
                for m_ in range(2):
                    mp = dict(q=qk[m_][:, :], k=qk[2][:, :], rn=["qk%d" % m_, "qk2"],
                              bias=lambda kt: (None, []),
                              pv=[(A[m_], (lambda kt: Vd1[:, kt, :]), ["V"]),
                                  (Sm[m_], (lambda kt: ones_b[:]), ["ones_b"])])
                    attend_chunk(qc, [mp], st2, cid)

                def part1(A=A, Sm=Sm, h=h, qc=qc):
                    cs = slice(qc * 512, (qc + 1) * 512)
                    a0, s0, a1, s1 = A[0], Sm[0], A[1], Sm[1]
                    recip(T[0], ps[s0][:], ["ps%d" % s0], ["T0"])
                    tt(T[1], ps[a0][:], T[0], ALU.mult, ["ps%d" % a0, "T0"], ["T1"])
                    recip(T[0], ps[s1][:], ["ps%d" % s1], ["T0"])
                    tt(T[2], ps[a1][:], T[0], ALU.mult, ["ps%d" % a1, "T0"], ["T2"])
                    stt(T[3], T[2], neglam[:, 0:1], T[1], ALU.mult, ALU.add, ["T2", "T1", "neglam"], ["T3"])
                    sqv = T[2].bitcast(BF16)[:, 0:512]
                    tt(sqv, T[3], T[3], ALU.mult, ["T3", "T2"], ["T2"])

                    def part2(ring, sqv=sqv, h=h, qc=qc, cs=cs):
                        b = ring.next()
                        mm(ps[b][:], ones_b[:], sqv, True, True, ["ones_b", "T2"], ["ps%d" % b], True)
                        act(T[0], ps[b][:], AF.Ln, ["ps%d" % b], ["T0"], bias=EPS, scale=1.0 / 128)
                        act(T[1], T[0], AF.Exp, ["T0"], ["T1"], scale=-0.5)
                        stt(oT[:, h, cs], T[3], gcol[:, 0:1], T[1], ALU.mult, ALU.mult, ["T3", "T1", "gcol"],
                            ["o%d_%d" % (h, qc)])
                    deferred.append(part2)
                att_cb[cid] = part1
                drain_check()
            att_flush()

        def attend_single(pair, chunk, R, bias_fn):
            for hl in range(2):
                h = 2 * pair + hl
                for qc in range(4):
                    ab = 4 + (hl * 4 + qc) % 4
                    cid = cid_ctr[0]
                    cid_ctr[0] += 1
                    maps = [dict(q=qk[hl][0:R, :], k=qk[2 + hl][0:R, :], rn=["qk%d" % hl, "qk%d" % (2 + hl)],
                                 bias=(lambda kt, h_=h: bias_fn(h_, kt)),
                                 pv=[(ab, (lambda kt, hl_=hl: Vs[:, kt, hl_, :]), ["V"])])]
                    attend_chunk(qc, maps, st4, cid)

                    def fin(ab=ab, hl=hl, qc=qc):
                        cs = slice(qc * 512, (qc + 1) * 512)
                        P.op("dve", lambda e: e.reciprocal(out=T[0][64:128, :], in_=ps[ab][64:128, :]),
                             (), ["ps%d" % ab, "T0"])
                        d0 = hl * 64
                        tt(oT[d0:d0 + 64, chunk, cs], ps[ab][0:64, :], T[0][64:128, :], ALU.mult,
                           ["ps%d" % ab, "T0"], ["o%d_%d" % (chunk, qc)])
                    att_cb[cid] = fin
                    drain_check()
            att_flush()

        def branch_diff(l, h):
            claim(N_QK + N_V + N_PT)
            wqk, nqk = wload([w_inT_d[l, h], w_inT_d[l, 4 + h]])
            wvv, nv = wload([w_inT_d[l, 8 + h]])
            if h == 0:
                memset(qk[0][64:128, :], 0.0, ["qk0"])
                memset(qk[1][0:64, :], 0.0, ["qk1"])
            proj_F(wqk[0], nqk, True, [(0, 64, qk[0], 0, "qk0"), (64, 64, qk[1], 64, "qk1")])
            proj_F(wqk[1], nqk, True, [(0, 128, qk[2], 0, "qk2")])

            def evac(t, b):
                cp(Vd1[:, t, :], ps[b][:, 0:128], ["ps%d" % b], ["V"])
            proj_V(wvv[0], nv, 128, evac)
            if h == 0:
                dump("qd0", qk[0], ["qk0"])
                dump("kd0", qk[2], ["qk2"])
            attend_diff(h)

        def branch_fox(l, pair):
            claim(N_QK + N_V + N_PT)
            wqk, nqk = wload([w_inT_d[l, 12 + pair], w_inT_d[l, 14 + pair]])
            pieces = [w_inT_d[l, 16 + pair]]
            if pair == 0:
                pieces.append(w_fgt_d[l].rearrange("(k p) n -> p k n", p=128))
            wvv, nv = wload(pieces)
            proj_F(wqk[0], nqk, False, [(0, 64, qk[0], 0, "qk0"), (64, 64, qk[1], 0, "qk1")])
            proj_F(wqk[1], nqk, False, [(0, 64, qk[2], 0, "qk2"), (64, 64, qk[3], 0, "qk3")])
            memset(Vs[:, :, :, 64:128], 1.0, ["V"], eng="pool")
            for i in range(4):
                memset(qk[i][64:128, :], 0.0, ["qk%d" % i], eng="pool")
            memset(qk[2][64:65, :], 1.0, ["qk2"], eng="pool")
            memset(qk[3][64:65, :], 1.0, ["qk3"], eng="pool")

            def evac(t, b):
                cp(Vs[:, t, :, 0:64], ps[b][:, 0:128].rearrange("p (j c) -> p j c", c=64), ["ps%d" % b], ["V"])
                if pair == 0:
                    tt(zb[:, :, t], ps[b][:, 128:132], bfrep[:], ALU.add, ["ps%d" % b, "bfrep"], ["zb"])
            proj_V(wvv[0], nv, 128, evac, extra=((wvv[1], 4) if pair == 0 else None))
            if pair == 0:
                act(lf[:], zb[:], AF.Exp, ["zb"], ["lf"], scale=-1.0)
                act(lf[:], lf[:], AF.Ln, ["lf"], ["lf"], bias=1.0)
                b1 = bank_aux.next()
                lf2 = lf[:].rearrange("p h t -> p (h t)")
                mm(ps[b1][:, 0:64], U_f[:], lf2, True, True, ["U_f", "lf"], ["ps%d" % b1], True)
                b2 = bank_aux.next()
                mm(ps[b2][:, 0:64], ones_f[:], lf2, True, True, ["ones_f", "lf"], ["ps%d" % b2], True)
                cp(tots[:].rearrange("p h t -> p (h t)"), ps[b2][:, 0:64], ["ps%d" % b2], ["tots"])
                memset(offs[:, :, 0:1], 0.0, ["offs"])
                for i in range(1, NT):
                    tt(offs[:, :, i:i + 1], offs[:, :, i - 1:i], tots[:, :, i - 1:i], ALU.add, ["offs", "tots"], ["offs"])
                tt(Lcum[:].rearrange("p h t -> p (h t)"), ps[b1][:, 0:64], offs[:].rearrange("p h t -> p (h t)"),
                   ALU.add, ["ps%d" % b1, "offs"], ["Lcum"])
                ts(r8[:], Lcum[:], -8.0, None, ALU.mult, None, ["Lcum"], ["r8"])
                dump("Lcum", Lcum[:], ["Lcum"])
            for hl in range(2):
                h = 2 * pair + hl
                for g in range(4):
                    b = bank_aux.next()
                    for i in range(4):
                        t = 4 * g + i
                        mm(ps[b][0:1, i * 128:(i + 1) * 128], r8[:, h, t:t + 1], ident_b[:], True, True,
                           ["r8", "ident_b"], ["ps%d" % b], i == 3)
                    act(qk[hl][64:65, g * 512:(g + 1) * 512], ps[b][0:1, :], AF.Identity, ["ps%d" % b], ["qk%d" % hl])
            if pair == 0:
                dump("qf0", qk[0], ["qk0"])
                dump("kf0", qk[2], ["qk2"])

            def bias_fn(h, kt):
                return Lcum[:, h, kt:kt + 1], ["Lcum"]
            attend_single(pair, 4 + pair, 128, bias_fn)

        def branch_moba(l, pair):
            claim(N_QK + N_V + N_PT)
            wqk, nqk = wload([w_inT_d[l, 18 + pair], w_inT_d[l, 20 + pair]])
            wvv, nv = wload([w_inT_d[l, 22 + pair]])
            proj_F(wqk[0], nqk, True, [(0, 64, qk[0], 0, "qk0"), (64, 64, qk[1], 0, "qk1")])
            proj_F(wqk[1], nqk, True, [(0, 64, qk[2], 0, "qk2"), (64, 64, qk[3], 0, "qk3")])
            memset(Vs[:, :, :, 64:128], 1.0, ["V"], eng="pool")
            for i in range(4):
                memset(qk[i][64:128, :], 0.0, ["qk%d" % i], eng="pool")
            for hl in range(2):
                dma("pool", qk[2 + hl][64:72, :], onehot_d, (), ["qk%d" % (2 + hl)])

            def evac(t, b):
                cp(Vs[:, t, :, 0:64], ps[b][:, 0:128].rearrange("p (j c) -> p j c", c=64), ["ps%d" % b], ["V"])
            proj_V(wvv[0], nv, 128, evac)
            bg = bank_aux.next()
            for hl in range(2):
                P.op("dve", (lambda hl_: lambda e: e.tensor_reduce(
                    out=kms[0:64, hl_, :], in_=qk[2 + hl_][0:64, :].rearrange("p (n k) -> p n k", k=256),
                    axis=AX.X, op=ALU.add))(hl), ["qk%d" % (2 + hl)], ["kms%d" % hl])
                ts(kmT[0:64, hl, :], kms[0:64, hl, :], 1.0 / 256, None, ALU.mult, None, ["kms%d" % hl], ["kmT%d" % hl])
                for i in range(8):
                    qt = 8 + i
                    c = (hl * 8 + i) * 8
                    mm(ps[bg][:, c:c + 8], qk[hl][0:64, qt * 128:(qt + 1) * 128], kmT[0:64, hl, :], True, True,
                       ["qk%d" % hl, "kmT%d" % hl], ["ps%d" % bg], (hl == 1 and i == 7))
            memset(gm[:], -1e30, ["gm"])
            for hl in range(2):
                for i in range(8):
                    own = (8 + i) // 2
                    c = (hl * 8 + i) * 8
                    cp(gm[:, hl, i, 0:own], ps[bg][:, c:c + own], ["ps%d" % bg, "gm"], ["gm"])
            for hl in range(2):
                for i in range(8):
                    own = (8 + i) // 2
                    P.op("dve", (lambda hl_, i_: lambda e: e.max(out=mx8[:], in_=gm[:, hl_, i_, :]))(hl, i),
                         ["gm"], ["mx8"])
                    ts(selb[:, hl, i, 0:own], gm[:, hl, i, 0:own], mx8[:, 2:3], NEGBIG, ALU.is_lt, ALU.mult,
                       ["gm", "mx8"], ["selb"])
            for i in range(8):
                qt = 8 + i
                b = bank_aux.next()
                for hl in range(2):
                    mm(ps[b][0:8, hl * 128:(hl + 1) * 128], selb[:, hl, i, :], ident_b[:], True, True,
                       ["selb", "ident_b"], ["ps%d" % b], hl == 1)
                for hl in range(2):
                    act(qk[hl][64:72, qt * 128:(qt + 1) * 128], ps[b][0:8, hl * 128:(hl + 1) * 128], AF.Identity,
                        ["ps%d" % b], ["qk%d" % hl])
            if pair == 0:
                dump("qm0", qk[0], ["qk0"])
                dump("km0", qk[2], ["qk2"])

            def bias_fn(h, kt):
                return None, []
            attend_single(pair, 6 + pair, 128, bias_fn)

        def update_x(half, nslab):
            unames = ["ssu%d" % i for i in range(8)]
            P.op("dve", lambda e: e.reduce_sum(out=ssu[:, 0:8], in_=ssq[:, :, 0:nslab], axis=AX.X),
                 ["ssq%d" % t for t in range(8)], unames)
            rmsn(ssu[:], lnvu[:], rstdu[:], unames, "lnvu", "rstdu", 1.0 / D)
            for t in range(8):
                tg = half * 8 + t
                stt(X[:, tg, :], ybuf[:, t, :], rstdu[:, t:t + 1], X[:, tg, :], ALU.mult, ALU.add,
                    ["yb%d" % t, "rstdu", "X%d" % tg], ["X%d" % tg])

        def evac_y(b, t, sl, ncol):
            cs = slice(sl * ncol, (sl + 1) * ncol)
            ji = junk_ring.next()
            act(junk2[:, ji, 0:ncol], ps[b][:, 0:ncol], AF.Square, ["ps%d" % b], ["ssq%d" % t, "junk%d" % ji],
                accum=ssq[:, t, sl:sl + 1])
            tt(ybuf[:, t, cs], ps[b][:, 0:ncol], G[:, cs], ALU.mult, ["ps%d" % b] + N_G, ["yb%d" % t])

        def merge_half(l, half):
            claim(N_MG)
            for fc in range(8):
                wa, na = wload([w_inT_d[l, 24 + fc], w_inT_d[l, 32 + fc]])
                wb, nb = wload([w_inT_d[l, 40 + fc], w_brT_d[0][l, fc], w_brT_d[1][l, fc], w_brT_d[2][l, fc]])
                for tcl in range(2):
                    tc = half * 2 + tcl
                    cs = slice(tc * 512, (tc + 1) * 512)
                    csl = slice(tcl * 512, (tcl + 1) * 512)
                    for br in range(3):
                        bgk = bank_main.next()
                        wsrc, wnm = (wa[0], na) if br == 0 else ((wa[1], na) if br == 1 else (wb[0], nb))
                        for kc in range(8):
                            mm(ps[bgk][:], wsrc[:, kc, :], hT[:, kc, cs], kc == 0, kc == 7,
                               wnm + ["hT%d" % tc], ["ps%d" % bgk], kc == 7)
                        by = bank_aux.next()
                        rng_ = (range(0, 4), range(4, 6), range(6, 8))[br]
                        kcs = [(wb[1 + br][:, kc - rng_[0], :], nb, kc) for kc in rng_]
                        for i, (wap, wnm2, oc) in enumerate(kcs):
                            mm(ps[by][:], wap, oT[:, oc, cs], i == 0, i == len(kcs) - 1,
                               wnm2 + ["o%d_%d" % (oc, tc)], ["ps%d" % by], i == len(kcs) - 1)
                        sg = T[br % 2]
                        sgn = "T%d" % (br % 2)
                        act(sg, ps[bgk][:], AF.Sigmoid, ["ps%d" % bgk], [sgn])
                        if br == 0:
                            tt(T[2], ps[by][:], sg, ALU.mult, ["ps%d" % by, sgn], ["T2"])
                        elif br == 1:
                            tt(T[3], ps[by][:], sg, ALU.mult, ["ps%d" % by, sgn], ["T3"])
                            tt(T[2], T[2], T[3], ALU.add, ["T2", "T3"], ["T2"])
                        else:
                            tt(T[3], ps[by][:], sg, ALU.mult, ["ps%d" % by, sgn], ["T3"])
                            tt(mg[:, fc, csl], T[2], T[3], ALU.add, ["T2", "T3"], ["mg%d" % tcl])
            if half == 0:
                dump("mg", R12[:, 16384:24576], N_MG)

        def wout_half(l, half):
            claim(N_YB)
            memset(ssq[:], 0.0, ["ssq%d" % t for t in range(8)])
            for sl in range(4):
                wvl, wn = wload([w_outT_d[l, sl]])
                wv = wvl[0]
                for t in range(8):
                    b = bank_main.next()
                    for kc in range(8):
                        mm(ps[b][:, 0:256], mg[:, kc, t * 128:(t + 1) * 128], wv[:, kc, :], kc == 0, kc == 7,
                           wn + ["mg%d" % (t // 4)], ["ps%d" % b], kc == 7)
                    evac_y(b, t, sl, 256)

        def gate_up(l, half):
            claim(N_AC)
            for j in range(NJ):
                wv, wn = wload([w_guT_d[l, j], w_guT_d[l, NJ + j]])
                for tcl in range(2):
                    tc = half * 2 + tcl
                    cs = slice(tc * 512, (tc + 1) * 512)
                    ba = bank_main.next()
                    for kc in range(8):
                        mm(ps[ba][:], wv[0][:, kc, :], hT[:, kc, cs], kc == 0, kc == 7, wn + ["hT%d" % tc],
                           ["ps%d" % ba], kc == 7)
                    bb = bank_aux.next()
                    for kc in range(8):
                        mm(ps[bb][:], wv[1][:, kc, :], hT[:, kc, cs], kc == 0, kc == 7, wn + ["hT%d" % tc],
                           ["ps%d" % bb], kc == 7)
                    sg = T[(j * 2 + tcl) % 2]
                    sgn = "T%d" % ((j * 2 + tcl) % 2)
                    act(sg, ps[ba][:], AF.Silu, ["ps%d" % ba], [sgn])
                    tt(actT[:, j, tcl * 512:(tcl + 1) * 512], ps[bb][:], sg, ALU.mult, ["ps%d" % bb, sgn], ["ac%d" % tcl])

        def down(l, half):
            claim(N_YB)
            memset(ssq[:], 0.0, ["ssq%d" % t for t in range(8)])
            for fcol in range(8):
                wvl, wn = wload([w_dnT_d[l, fcol]])
                wv = wvl[0]
                for t in range(8):
                    b = bank_main.next()
                    for kc in range(NJ):
                        mm(ps[b][:, 0:128], actT[:, kc, t * 128:(t + 1) * 128], wv[:, kc, :], kc == 0, kc == NJ - 1,
                           wn + ["ac%d" % (t // 4)], ["ps%d" % b], kc == NJ - 1)
                    evac_y(b, t, fcol, 128)
            update_x(half, 8)

        def ffn(l, s):
            stage_a(list(range(0, 8)), 16)
            gate_up(l, 0)
            stage_a(list(range(8, 16)), 16)
            down(l, 0)
            if l == layers[-1]:
                finish_tiles(s, range(0, 8))
            gate_up(l, 1)
            down(l, 1)
            if l == layers[-1]:
                finish_tiles(s, range(8, 16))

        def finish_tiles(s, tiles):
            for t in tiles:
                dma("sp", out_d[s, t * 128:(t + 1) * 128, :], X[:, t, :], ["X%d" % t], ())
            if s + 1 < n_seq:
                for t in tiles:
                    dma("sp", X[:, t, :], x_d[s + 1, t * 128:(t + 1) * 128, :], (), ["X%d" % t])

        def program():
            for s in range(n_seq):
                if s == 0:
                    for t in range(NT):
                        dma("sp", X[:, t, :], x_d[s, t * 128:(t + 1) * 128, :], (), ["X%d" % t])
                for kc in range(8):
                    cp(cbc[:, kc, :], ca[:, kc * 2 + s: kc * 2 + s + 1].to_broadcast([128, 128]), ["ca"], ["cbc"])
                rope_tables(s)
                for l in layers:
                    adaln_cols(s, l)
                    adaln_G(s, l, 0)
                    stage_a(list(range(NT)), 0)
                    dump("hT", hT[:], ["hT%d" % i for i in range(4)])
                    if stop == "a":
                        return
                    claim(N_O)
                    for h in range(4):
                        branch_diff(l, h)
                        if stop == "d0":
                            dump("oT2", R12[:, 0:4096], N_O)
                            return
                    for pair in range(2):
                        branch_fox(l, pair)
                    for pair in range(2):
                        branch_moba(l, pair)
                    dump("oT", R12[:, 0:16384], N_O)
                    if stop == "attn":
                        return
                    for half in range(2):
                        merge_half(l, half)
                        wout_half(l, half)
                        update_x(half, 4)
                    dump("xmix", X[:], ["X%d" % t for t in range(NT)])
                    if stop == "mix":
                        return
                    adaln_G(s, l, 1)
                    ffn(l, s)

        def lam_init(l):
            return 0.8 - 0.6 * math.exp(-0.3 * l)

        program()
        P.wait_tokens("sp", P.all_tokens())
        P.replay()
    nc._dbg_names = list(dbg_d.keys())
    nc._n_instr = {e: len(P.streams[e]) for e in ENGS}
    return nc


def host_consts():
    inv_freq = 1.0 / (10000.0 ** (np.arange(0, 64, 2, dtype=np.float32) / 64.0))
    invf = np.tile(inv_freq.astype(np.float32), 4).reshape(128, 1)
    rmat = np.zeros((128, 128), np.float32)
    for m in range(128):
        if m % 64 < 32:
            rmat[m + 32, m] = -1.0
        else:
            rmat[m - 32, m] = 1.0
    ident = np.eye(128, dtype=np.float32)
    mask01 = (np.arange(128)[:, None] <= np.arange(128)[None, :]).astype(np.float32)
    onehot = (np.arange(S)[None, :] // 256 == np.arange(8)[:, None]).astype(np.float32)
    negmask = np.where(np.arange(128)[:, None] > np.arange(128)[None, :], -30000.0, 0.0).astype(np.float32)
    return dict(invf=invf, rmat=rmat, ident=ident, mask01=mask01, onehot=onehot, negmask=negmask)


def make_in_maps(inputs, n_cores=8, n_seq=2):
    f = lambda a: np.ascontiguousarray(np.asarray(a))
    consts = host_consts()
    b_ada = f(inputs["b_ada"])
    badaT = np.ascontiguousarray(b_ada.reshape(2, 48, 128).transpose(0, 2, 1))
    gpm = f(inputs["g_pre_mix"]).reshape(2, 8, 128).transpose(0, 2, 1)
    gpf = f(inputs["g_pre_ffn"]).reshape(2, 8, 128).transpose(0, 2, 1)
    gsl = f(inputs["g_subln"]).reshape(2, 128, 1)
    gcols = np.ascontiguousarray(np.concatenate([gpm, gpf, gsl], axis=2))
    def tile_w(w, ncol):
        w = f(w)
        L, K, N = w.shape
        return np.ascontiguousarray(w.reshape(L, K // 128, 128, N // ncol, ncol).transpose(0, 3, 2, 1, 4))
    w_in = f(inputs["w_in"])
    shared = dict(
        w_adaT=tile_w(inputs["w_ada"], 128), b_ada=b_ada, badaT=badaT, gcols=gcols,
        g_post_mix=f(inputs["g_post_mix"]), g_post_ffn=f(inputs["g_post_ffn"]),
        w_inT=tile_w(w_in[:, :, 0:6144], 128), w_fgt=np.ascontiguousarray(w_in[:, :, 6144:6148]),
        b_fgt=f(inputs["b_fgt"]),
        lam_q1=f(inputs["lam_q1"]), lam_k1=f(inputs["lam_k1"]), lam_q2=f(inputs["lam_q2"]), lam_k2=f(inputs["lam_k2"]),
        w_braT=tile_w(inputs["w_br_a"], 128), w_brbT=tile_w(inputs["w_br_b"], 128), w_brcT=tile_w(inputs["w_br_c"], 128),
        w_outT=tile_w(inputs["w_out"], 256), w_guT=tile_w(inputs["w_gate_up"], 128),
        w_dnT=tile_w(inputs["w_down"], 128), **consts)
    x = f(inputs["x"])
    c = f(inputs["c"])
    pos = f(inputs["positions"]).astype(np.int32)
    maps = []
    for i in range(n_cores):
        bs = slice(i * n_seq, (i + 1) * n_seq)
        cc = c[bs]
        cT = np.zeros((128, 16), np.float32)
        cT[:, : 8 * 2] = 0
        for b in range(n_seq):
            cT[:, b::2][:, :8] = cc[b].reshape(8, 128).T
        m = dict(shared)
        m.update(x=np.ascontiguousarray(x[bs]), cT=cT, pos=np.ascontiguousarray(pos[bs]))
        maps.append(m)
    return maps


def kernel(**inputs):
    nc = build(n_seq=2, layers=(0, 1))
    maps = make_in_maps(inputs, 8, 2)
    res = run_bass_kernel_spmd(nc, maps, core_ids=list(range(8)))
    return np.concatenate([r["out"] for r in res.results], axis=0).astype(np.float32)
```

```python
import math
from contextlib import ExitStack
import numpy as np
import concourse.bass as bass
import concourse.mybir as mybir
from concourse.bass_utils import run_bass_kernel_spmd

F32 = mybir.dt.float32
BF16 = mybir.dt.bfloat16
I32 = mybir.dt.int32
AF = mybir.ActivationFunctionType
ALU = mybir.AluOpType
AX = mybir.AxisListType

ENGS = ("pe", "act", "dve", "pool", "sp")
N_DMA_SEMS = 24

D = 1024
S = 2048
NT = 16
DFF = 2816
NJ = 22
EPS = 1e-6
NEGBIG = -1920.0
PI = math.pi


class Plan:
    def __init__(self, nc):
        self.nc = nc
        self.streams = {e: [] for e in ENGS}
        self.count = {e: 0 for e in ENGS}
        self.wm = {e: {} for e in ENGS}
        self.res = {}
        self.dma_cnt = [0] * N_DMA_SEMS
        self.dma_rr = {"pool": 0, "sp": 0, "act": 0}
        self.sems = {}

    def _need(self, eng, reads, writes):
        need = {}

        def add(tok):
            if tok is None:
                return
            k, v = tok
            if need.get(k, 0) < v:
                need[k] = v

        for r in reads:
            ent = self.res.get(r)
            if ent is not None:
                add(ent[0])
        for w in writes:
            ent = self.res.get(w)
            if ent is not None:
                add(ent[0])
                for k, v in ent[1].items():
                    add((k, v))
        out = []
        for k, v in need.items():
            if k == "pe" and eng == "pe":
                continue
            if self.wm[eng].get(k, 0) >= v:
                continue
            self.wm[eng][k] = v
            out.append((k, v))
        return out

    def _record(self, tok, reads, writes):
        k, v = tok
        for r in reads:
            ent = self.res.setdefault(r, [None, {}])
            if ent[1].get(k, 0) < v:
                ent[1][k] = v
        for w in writes:
            self.res[w] = [tok, {}]

    def op(self, eng, fn, reads=(), writes=(), signal=True):
        writes = list(writes) + [r for r in reads if r.startswith("ps")]
        reads = [r for r in reads if not r.startswith("ps")]
        waits = self._need(eng, reads, writes)
        if signal:
            self.count[eng] += 1
            tok = (eng, self.count[eng])
            inc = (eng, 1)
        else:
            tok = (eng, self.count[eng] + 1)
            inc = None
        self._record(tok, reads, writes)
        self.streams[eng].append((waits, fn, inc))
        return tok

    def dma(self, eng, fn, reads=(), writes=()):
        half = N_DMA_SEMS // 2
        base = 0 if eng == "pool" else half
        s = base + self.dma_rr[eng]
        self.dma_rr[eng] = (self.dma_rr[eng] + 1) % half
        key = "dma%d" % s
        waits = self._need(eng, reads, writes)
        prev = 16 * self.dma_cnt[s]
        if prev and self.wm[eng].get(key, 0) < prev:
            self.wm[eng][key] = prev
            waits.append((key, prev))
        self.dma_cnt[s] += 1
        tok = (key, 16 * self.dma_cnt[s])
        self._record(tok, reads, writes)
        self.streams[eng].append((waits, fn, (key, 16)))
        return tok

    def fence(self, new, old):
        merged = {}
        for o in old:
            ent = self.res.get(o)
            if ent is None:
                continue
            if ent[0] is not None:
                k, v = ent[0]
                merged[k] = max(merged.get(k, 0), v)
            for k, v in ent[1].items():
                merged[k] = max(merged.get(k, 0), v)
        for n in new:
            ent = self.res.get(n)
            m2 = dict(merged)
            if ent is not None:
                if ent[0] is not None:
                    k, v = ent[0]
                    m2[k] = max(m2.get(k, 0), v)
                for k, v in ent[1].items():
                    m2[k] = max(m2.get(k, 0), v)
            self.res[n] = [None, m2]

    def wait_tokens(self, eng, toks):
        waits = []
        for k, v in toks:
            if self.wm[eng].get(k, 0) < v:
                self.wm[eng][k] = v
                waits.append((k, v))
        self.streams[eng].append((waits, None, None))

    def all_tokens(self):
        toks = {}
        for ent in self.res.values():
            if ent[0] is not None:
                k, v = ent[0]
                toks[k] = max(toks.get(k, 0), v)
            for k, v in ent[1].items():
                toks[k] = max(toks.get(k, 0), v)
        return list(toks.items())

    def replay(self):
        nc = self.nc
        with ExitStack() as es:
            for e in ENGS:
                self.sems[e] = es.enter_context(nc.semaphore("s_" + e))
            for i in range(N_DMA_SEMS):
                self.sems["dma%d" % i] = es.enter_context(nc.semaphore("s_dma%d" % i))
            block = es.enter_context(nc.Block())
            sems = self.sems

            def run(engname):
                def body(eng):
                    for waits, fn, inc in self.streams[engname]:
                        for k, v in waits:
                            eng.wait_ge(sems[k], v)
                        if fn is None:
                            continue
                        ins = fn(eng)
                        if inc is not None:
                            ins.then_inc(sems[inc[0]], inc[1])
                return body

            block.tensor(run("pe"))
            block.scalar(run("act"))
            block.vector(run("dve"))
            block.gpsimd(run("pool"))
            block.sync(run("sp"))


class Ring:
    def __init__(self, items):
        self.items = list(items)
        self.i = 0

    def next(self):
        v = self.items[self.i]
        self.i = (self.i + 1) % len(self.items)
        return v


def build(n_seq=2, layers=(0, 1), dbg=(), stop=None):
    nc = bass.Bass("TRN2", target_bir_lowering=False)

    def din(name, shape, dt=F32):
        return nc.dram_tensor(name, list(shape), dt, kind="ExternalInput").ap()

    x_d = din("x", [n_seq, S, D])
    cT_d = din("cT", [128, 16])
    pos_d = din("pos", [n_seq, S], I32)
    w_adaT_d = din("w_adaT", [2, 48, 128, 8, 128])
    b_ada_d = din("b_ada", [2, 6 * D])
    badaT_d = din("badaT", [2, 128, 48])
    gcols_d = din("gcols", [2, 128, 17])
    g_post_mix_d = din("g_post_mix", [2, D])
    g_post_ffn_d = din("g_post_ffn", [2, D])
    w_inT_d = din("w_inT", [2, 48, 128, 8, 128])
    w_fgt_d = din("w_fgt", [2, D, 4])
    b_fgt_d = din("b_fgt", [2, 4])
    lam_d = [din(n, [2, 64]) for n in ("lam_q1", "lam_k1", "lam_q2", "lam_k2")]
    w_brT_d = [din("w_braT", [2, 8, 128, 4, 128]), din("w_brbT", [2, 8, 128, 2, 128]),
               din("w_brcT", [2, 8, 128, 2, 128])]
    w_outT_d = din("w_outT", [2, 4, 128, 8, 256])
    w_guT_d = din("w_guT", [2, 44, 128, 8, 128])
    w_dnT_d = din("w_dnT", [2, 8, 128, NJ, 128])
    invf_d = din("invf", [128, 1])
    rmat_d = din("rmat", [128, 128])
    ident_d = din("ident", [128, 128])
    mask_d = din("mask01", [128, 128])
    negmask_d = din("negmask", [128, 128])
    onehot_d = din("onehot", [8, S])
    out_d = nc.dram_tensor("out", [n_seq, S, D], F32, kind="ExternalOutput").ap()
    dbg_d = {}

    es = ExitStack()
    with es:
        def sb(name, shape, dt):
            return es.enter_context(nc.sbuf_tensor(name, list(shape), dt))

        X = sb("X", [128, NT, D], F32)
        hT = sb("hT", [128, 8, S], BF16)
        R12 = sb("R12", [128, 24576], BF16)
        R3 = sb("R3", [128, 8192], BF16)
        tabC = sb("tabC", [128, S], BF16)
        tabS = sb("tabS", [128, S], BF16)
        NSLOT = 3
        wslot = [sb("wslot%d" % i, [128, 2816], BF16) for i in range(NSLOT)]
        G = sb("G", [128, D], F32)
        SCR = sb("SCR", [128, 2048], F32)
        ident_b = sb("ident_b", [128, 128], BF16)
        negmask_b = sb("negmask_b", [128, 128], BF16)
        rmat_b = sb("rmat_b", [128, 128], BF16)
        ones_b = sb("ones_b", [128, 128], BF16)
        U_f = sb("U_f", [128, 128], F32)
        ones_f = sb("ones_f", [128, 128], F32)
        invf = sb("invf_s", [128, 1], F32)
        halfpi = sb("halfpi", [128, 1], F32)
        modc1 = sb("modc1", [128, 2, 48], F32)
        cTs = sb("cTs", [128, 16], F32)
        ca = sb("ca", [128, 16], BF16)
        cbc = sb("cbc", [128, 8, 128], BF16)
        badaT = sb("badaT_s", [128, 48], F32)
        gcols = sb("gcols_s", [128, 17], F32)
        modc = sb("modc", [128, 48], F32)
        scsh = sb("scsh", [128, 32], F32)
        ss = sb("ss", [128, 16], F32)
        lnv = sb("lnv", [128, 16], F32)
        rstd = sb("rstd", [128, 16], F32)
        lamt = sb("lamt", [128, 4, 64], F32)
        lamp = sb("lamp", [128, 64], F32)
        lams = sb("lams", [128, 4], F32)
        neglam = sb("neglam", [128, 1], F32)
        gcol = sb("gcol", [128, 1], F32)
        bfrep = sb("bfrep", [128, 4], F32)
        zb = sb("zb", [128, 4, NT], F32)
        lf = sb("lf", [128, 4, NT], F32)
        tots = sb("tots", [128, 4, NT], F32)
        offs = sb("offs", [128, 4, NT], F32)
        Lcum = sb("Lcum", [128, 4, NT], F32)
        r8 = sb("r8", [128, 4, NT], BF16)
        kmT = sb("kmT", [128, 2, 8], BF16)
        kms = sb("kms", [128, 2, 8], F32)
        gm = sb("gm", [128, 2, 8, 8], F32)
        mx8 = sb("mx8", [128, 8], F32)
        selb = sb("selb", [128, 2, 8, 8], BF16)
        ssq = sb("ssq", [128, 8, 8], F32)
        ssu = sb("ssu", [128, 8], F32)
        lnvu = sb("lnvu", [128, 8], F32)
        rstdu = sb("rstdu", [128, 8], F32)
        junk2 = sb("junk2", [128, 4, 256], BF16)
        junk_ring = Ring([0, 1, 2, 3])

        ps = [es.enter_context(nc.psum_tensor("ps%d" % i, [128, 512], F32)) for i in range(8)]

        oT = R12[:, 0:16384].rearrange("p (c t) -> p c t", t=S)
        qk = [R12[:, 16384 + i * S: 16384 + (i + 1) * S] for i in range(4)]
        mg = R12[:, 16384:24576].rearrange("p (c t) -> p c t", t=1024)
        actT = R12[:, 0:22528].rearrange("p (c t) -> p c t", t=1024)

        Vd1 = R3[:, 0:2048].rearrange("p (t c) -> p t c", c=128)
        Vs = R3[:, 0:4096].rearrange("p (t j c) -> p t j c", j=2, c=128)
        PT = [R3[:, 4096 + i * 512: 4096 + (i + 1) * 512] for i in range(8)]
        ybuf = R3[:, 0:8192].rearrange("p (t c) -> p t c", c=1024)
        posi = R3[:, 0:4096].bitcast(I32)
        T = [SCR[:, i * 512:(i + 1) * 512] for i in range(4)]
        Ti = [T[i].bitcast(I32) for i in range(4)]
        xn = [T[2].bitcast(BF16), T[3].bitcast(BF16)]
        XN_NAMES = ["T2", "T3"]
        junkA = junk2[:].rearrange("p a b -> p (a b)")
        deferred = []

        gstep = [0]

        def run_deferred(ring, force=True):
            while deferred and (force or deferred[0][0] <= gstep[0]):
                deferred.pop(0)[1](ring)

        N_O = ["o%d_%d" % (c, t) for c in range(8) for t in range(4)]
        N_AC = ["ac0", "ac1"]
        N_QK = ["qk0", "qk1", "qk2", "qk3"]
        N_MG = ["mg0", "mg1"]
        N_XN = ["xn0", "xn1"]
        N_V = ["V"]
        N_PT = ["pt%d" % i for i in range(8)]
        N_YB = ["yb%d" % i for i in range(8)]
        N_POSI = ["posi"]
        REG_A = N_O + N_AC
        REG_B = N_QK + N_MG + N_AC
        REG_C = N_V + N_PT + N_YB + N_POSI
        N_T = ["T0", "T1", "T2", "T3"]
        WNAMES = [["w%d_%d" % (s_, j) for j in range(4)] for s_ in range(NSLOT)]

        P = Plan(nc)

        def claim(names):
            for reg in (REG_A, REG_B, REG_C):
                mine = [n for n in names if n in reg]
                if mine:
                    P.fence(mine, [n for n in reg if n not in mine])

        def mm(out, lhsT, rhs, start, stop, reads, writes, signal):
            P.op("pe", lambda e: e.matmul(out, lhsT=lhsT, rhs=rhs, start=start, stop=stop),
                 reads, writes, signal)

        def tr(out, in_, reads, writes, signal=True):
            P.op("pe", lambda e: e.transpose(out=out, in_=in_, identity=ident_b[:]), reads, writes, signal)

        def act(out, in_, func, reads, writes, bias=None, scale=None, accum=None):
            kw = {}
            if bias is not None:
                kw["bias"] = bias
            if scale is not None:
                kw["scale"] = scale
            if accum is not None:
                kw["accum_out"] = accum
            P.op("act", lambda e: e.activation(out=out, in_=in_, func=func, **kw), reads, writes)

        def tt(out, in0, in1, op, reads, writes, eng="dve"):
            P.op(eng, lambda e: e.tensor_tensor(out=out, in0=in0, in1=in1, op=op), reads, writes)

        def ts(out, in0, s1, s2, op0, op1, reads, writes, eng="dve"):
            if op1 is None:
                P.op(eng, lambda e: e.tensor_scalar(out=out, in0=in0, scalar1=s1, scalar2=None, op0=op0), reads, writes)
            else:
                P.op(eng, lambda e: e.tensor_scalar(out=out, in0=in0, scalar1=s1, scalar2=s2, op0=op0, op1=op1), reads, writes)

        def stt(out, in0, scalar, in1, op0, op1, reads, writes, eng="dve"):
            P.op(eng, lambda e: e.scalar_tensor_tensor(out=out, in0=in0, scalar=scalar, in1=in1, op0=op0, op1=op1),
                 reads, writes)

        def cp(out, in_, reads, writes, eng="dve"):
            P.op(eng, lambda e: e.tensor_copy(out=out, in_=in_), reads, writes)

        def recip(out, in_, reads, writes):
            act(out, in_, AF.Ln, reads, writes)
            act(out, out, AF.Exp, list(writes), writes, scale=-1.0)

        def memset(ap, val, writes, eng="dve"):
            P.op(eng, lambda e: e.memset(ap, val), (), writes)

        def dma(eng, out, in_, reads, writes):
            P.dma(eng, lambda e: e.dma_start(out=out, in_=in_), reads, writes)

        def dump(name, ap, reads):
            if name not in dbg:
                return
            d = nc.dram_tensor("dbg_" + name, list(ap.shape), ap.dtype, kind="ExternalOutput").ap()
            dbg_d[name] = d
            dma("sp", d, ap, reads, ())

        bank_main = Ring([0, 1, 2, 3])
        bank_aux = Ring([4, 5, 6, 7])
        pt_ring = Ring(list(range(8)))
        slot_ring = Ring(list(range(NSLOT)))

        def wload(pieces):
            s_ = slot_ring.next()
            views = []
            off = 0
            for j, src in enumerate(pieces):
                kc_, n_ = src.shape[1], src.shape[2]
                v = wslot[s_][:, off:off + kc_ * n_].rearrange("p (k n) -> p k n", n=n_)
                off += kc_ * n_
                dma("pool", v, src, (), [WNAMES[s_][j]])
                views.append(v)
            assert off <= 2816
            return views, WNAMES[s_]

        dma("pool", ident_b[:], ident_d, (), ["ident_b"])
        dma("pool", negmask_b[:], negmask_d, (), ["negmask_b"])
        dma("pool", rmat_b[:], rmat_d, (), ["rmat_b"])
        dma("sp", U_f[:], mask_d, (), ["U_f"])
        dma("sp", invf[:], invf_d, (), ["invf"])
        dma("sp", cTs[:], cT_d, (), ["cTs"])
        memset(ones_b[:], 1.0, ["ones_b"])
        memset(halfpi[:], PI / 2, ["halfpi"])
        memset(ones_f[:], 1.0, ["ones_f"])
        memset(selb[:], 0.0, ["selb"])
        act(ca[:], cTs[:], AF.Silu, ["cTs"], ["ca"])

        def rmsn(ss_ap, lnv_ap, rstd_ap, in_names, ln_name, out_name, scale):
            act(lnv_ap, ss_ap, AF.Ln, list(in_names), [ln_name], bias=EPS, scale=scale)
            act(rstd_ap, lnv_ap, AF.Exp, [ln_name], [out_name], scale=-0.5)

        def rope_tables(s):
            claim(N_POSI)
            dma("sp", posi, pos_d[s:s + 1, :].to_broadcast([128, S]), (), ["posi"])
            for j in range(4):
                cs = slice(j * 512, (j + 1) * 512)
                ts(T[0], posi[:, cs], invf[:, 0:1], None, ALU.mult, None, ["posi", "invf"], ["T0"])
                ts(Ti[1], T[0], 1.0 / (2 * PI), None, ALU.mult, None, ["T0"], ["T1"])
                stt(T[2], Ti[1], -2 * PI, T[0], ALU.mult, ALU.add, ["T1", "T0"], ["T2"])
                ts(T[3], T[2], PI, -2 * PI, ALU.is_gt, ALU.mult, ["T2"], ["T3"])
                tt(T[2], T[2], T[3], ALU.add, ["T2", "T3"], ["T2"])
                ts(T[2], T[2], -PI, PI, ALU.max, ALU.min, ["T2"], ["T2"])
                act(tabS[:, cs], T[2], AF.Sin, ["T2"], ["tab0_%d" % j])
                stt(T[3], T[2], -1.0, T[2], ALU.mult, ALU.max, ["T2"], ["T3"])
                act(tabC[:, cs], T[3], AF.Sin, ["T3", "halfpi"], ["tab1_%d" % j], bias=halfpi[:, 0:1], scale=-1.0)
            dump("tabC", tabC[:], ["tab1_%d" % j for j in range(4)])
            dump("tabS", tabS[:], ["tab0_%d" % j for j in range(4)])

        def adaln_cols(s, l):
            dma("sp", badaT[:], badaT_d[l], (), ["badaT"])
            dma("sp", gcols[:], gcols_d[l], (), ["gcols"])
            if s == 0:
                b = bank_aux.next()
                chunks = list(range(0, 16)) + list(range(24, 40))
                for i in range(0, len(chunks), 2):
                    j0 = chunks[i]
                    wv, wn = wload([w_adaT_d[l, j0], w_adaT_d[l, j0 + 1]])
                    for jj in range(2):
                        j = j0 + jj
                        for kc in range(8):
                            mm(ps[b][:, 2 * j:2 * j + n_seq], wv[jj][:, kc, :],
                               ca[:, kc * 2: kc * 2 + n_seq], kc == 0, kc == 7, wn + ["ca"], ["ps%d" % b], kc == 7)
                psv = ps[b][:, 0:96].rearrange("p (j b) -> p j b", b=2)
                for (a0, a1) in ((0, 16), (24, 40)):
                    tt(modc[:, a0:a1], psv[:, a0:a1, 0], badaT[:, a0:a1], ALU.add, ["ps%d" % b, "badaT", "modc"], ["modc"])
                    if n_seq > 1:
                        tt(modc1[:, l, a0:a1], psv[:, a0:a1, 1], badaT[:, a0:a1], ALU.add,
                           ["ps%d" % b, "badaT", "modc1_%d" % l], ["modc1_%d" % l])
            else:
                for (a0, a1) in ((0, 16), (24, 40)):
                    cp(modc[:, a0:a1], modc1[:, l, a0:a1], ["modc1_%d" % l, "modc"], ["modc"])
            stt(scsh[:, 0:8], modc[:, 8:16], 1.0, gcols[:, 0:8], ALU.add, ALU.mult, ["modc", "gcols"], ["scsh"])
            cp(scsh[:, 8:16], modc[:, 0:8], ["modc"], ["scsh"])
            stt(scsh[:, 16:24], modc[:, 32:40], 1.0, gcols[:, 8:16], ALU.add, ALU.mult, ["modc", "gcols"], ["scsh"])
            cp(scsh[:, 24:32], modc[:, 24:32], ["modc"], ["scsh"])
            ts(gcol[:], gcols[:, 16:17], 1.0 - lam_init(l), None, ALU.mult, None, ["gcols"], ["gcol"])
            for i in range(4):
                dma("sp", lamt[:, i, :], lam_d[i][l:l + 1, :].to_broadcast([128, 64]), (), ["lamt%d" % i])
            for i in range(2):
                tt(lamp[:], lamt[:, 2 * i, :], lamt[:, 2 * i + 1, :], ALU.mult,
                   ["lamt%d" % (2 * i), "lamt%d" % (2 * i + 1)], ["lamp"])
                P.op("dve", (lambda i_: lambda e: e.reduce_sum(out=lams[:, i_:i_ + 1], in_=lamp[:], axis=AX.X))(i),
                     ["lamp"], ["lams%d" % i])
            act(lams[:, 2:4], lams[:, 0:2], AF.Exp, ["lams0", "lams1"], ["lamse"])
            tt(neglam[:], lams[:, 3:4], lams[:, 2:3], ALU.subtract, ["lamse"], ["neglam"])
            ts(neglam[:], neglam[:], -lam_init(l), None, ALU.add, None, ["neglam"], ["neglam"])
            dma("sp", bfrep[:], b_fgt_d[l:l + 1, :].to_broadcast([128, 4]), (), ["bfrep"])

        def adaln_G(s, l, which):
            c0 = 2048 if which == 0 else 5120
            gp = g_post_mix_d if which == 0 else g_post_ffn_d
            brep = SCR[:, 0:1024]
            grep = SCR[:, 1024:2048]
            dma("sp", brep, b_ada_d[l:l + 1, c0:c0 + 1024].to_broadcast([128, 1024]), (), ["T0", "T1"])
            dma("sp", grep, gp[l:l + 1, :].to_broadcast([128, 1024]), (), ["T2", "T3"])
            for q in range(4):
                jq = c0 // 128 + 2 * q
                wv, wn = wload([w_adaT_d[l, jq], w_adaT_d[l, jq + 1]])
                b = bank_main.next()
                for jj in range(2):
                    for kc in range(8):
                        mm(ps[b][:, jj * 128:(jj + 1) * 128], cbc[:, kc, :], wv[jj][:, kc, :], kc == 0, kc == 7,
                           wn + ["cbc"], ["ps%d" % b], kc == 7)
                cs = slice(q * 256, (q + 1) * 256)
                tt(G[:, cs], ps[b][:, 0:256], brep[:, cs], ALU.add, ["ps%d" % b, "T0", "T1"], ["G%d" % q])
                tt(G[:, cs], G[:, cs], grep[:, cs], ALU.mult, ["G%d" % q, "T2", "T3"], ["G%d" % q])

        N_G = ["G0", "G1", "G2", "G3"]

        def stage_a(tiles, off):
            ngrp = len(tiles) // 4

            def sq(g):
                grp = tiles[4 * g:4 * g + 4]
                c0 = grp[0]
                names = ["ss%d" % t for t in grp]
                memset(ss[:, c0:c0 + 4], 0.0, names)
                for t in grp:
                    act(junkA, X[:, t, :], AF.Square, ["X%d" % t], ["junk0", "junk1", "junk2", "junk3", "ss%d" % t],
                        accum=ss[:, t:t + 1])
                rmsn(ss[:, c0:c0 + 4], lnv[:, c0:c0 + 4], rstd[:, c0:c0 + 4], names, "lnv%d" % (c0 // 4),
                     "rstd%d" % (c0 // 4), 1.0 / D)

            sq(0)
            for g in range(ngrp):
                if g + 1 < ngrp:
                    sq(g + 1)
                grp = tiles[4 * g:4 * g + 4]
                tc = grp[0] // 4
                banks = [bank_main.next(), bank_main.next(), bank_aux.next(), bank_aux.next()]
                for i, t in enumerate(grp):
                    ts(xn[i % 2], X[:, t, :], rstd[:, t:t + 1], None, ALU.mult, None,
                       ["X%d" % t, "rstd%d" % tc], [XN_NAMES[i % 2]])
                    for kc in range(8):
                        b = banks[kc // 2]
                        pv = ps[b][:].bitcast(BF16)
                        o0 = (kc % 2) * 512 + i * 128
                        tr(pv[:, o0:o0 + 128], xn[i % 2][:, kc * 128:(kc + 1) * 128],
                           [XN_NAMES[i % 2], "ident_b"], ["ps%d" % b], signal=(kc == 7))
                for kc in range(8):
                    b = banks[kc // 2]
                    pv = ps[b][:].bitcast(BF16)
                    o0 = (kc % 2) * 512
                    if kc % 2 == 0:
                        act(hT[:, kc, tc * 512:(tc + 1) * 512], pv[:, o0:o0 + 512], AF.Identity,
                            ["ps%d" % b, "scsh"], ["hT%d" % tc],
                            bias=scsh[:, off + 8 + kc: off + 9 + kc], scale=scsh[:, off + kc: off + kc + 1])
                    else:
                        ts(hT[:, kc, tc * 512:(tc + 1) * 512], pv[:, o0:o0 + 512],
                           scsh[:, off + kc: off + kc + 1], scsh[:, off + 8 + kc: off + 9 + kc], ALU.mult, ALU.add,
                           ["ps%d" % b, "scsh"], ["hT%d" % tc])

        def proj_F(wp, wn, rope, dests):
            tail = [None]
            for tc in range(4):
                cs = slice(tc * 512, (tc + 1) * 512)
                b = bank_main.next()
                for kc in range(8):
                    mm(ps[b][:], wp[:, kc, :], hT[:, kc, cs], kc == 0, kc == 7,
                       wn + ["hT%d" % tc], ["ps%d" % b], kc == 7)
                if tc == 0:
                    run_deferred(bank_aux)
                if not rope:
                    for (r0, nr, tile, d0, nm) in dests:
                        cp(tile[d0:d0 + nr, cs], ps[b][r0:r0 + nr, :], ["ps%d" % b], [nm])
                else:
                    pi = pt_ring.next()
                    act(PT[pi], ps[b][:], AF.Identity, ["ps%d" % b], ["pt%d" % pi])
                    if tail[0] is not None:
                        tail[0]()

                    def mk(b=b, pi=pi, cs=cs, tc=tc):
                        b2 = bank_aux.next()
                        mm(ps[b2][:], rmat_b[:], PT[pi], True, True, ["rmat_b", "pt%d" % pi], ["ps%d" % b2], True)
                        tt(T[0], ps[b][:], tabC[:, cs], ALU.mult, ["ps%d" % b, "tab1_%d" % tc], ["T0"])
                        tt(T[1], ps[b2][:], tabS[:, cs], ALU.mult, ["ps%d" % b2, "tab0_%d" % tc], ["T1"])
                        for (r0, nr, tile, d0, nm) in dests:
                            tt(tile[d0:d0 + nr, cs], T[0][r0:r0 + nr, :], T[1][r0:r0 + nr, :], ALU.add,
                               ["T0", "T1"], [nm])
                    tail[0] = mk
            if tail[0] is not None:
                tail[0]()

        def proj_V(wp, wn, N, evac, extra=None):
            for t in range(NT):
                b = bank_main.next()
                for kc in range(8):
                    mm(ps[b][:, 0:N], hT[:, kc, t * 128:(t + 1) * 128], wp[:, kc, 0:N], kc == 0, kc == 7,
                       wn + ["hT%d" % (t // 4)], ["ps%d" % b], kc == 7)
                if extra is not None:
                    wp2, n2 = extra
                    for kc in range(8):
                        mm(ps[b][:, N:N + n2], hT[:, kc, t * 128:(t + 1) * 128], wp2[:, kc, :], kc == 0, kc == 7,
                           wn + ["hT%d" % (t // 4)], ["ps%d" % b], kc == 7)
                evac(t, b)

        att_pend = []
        att_cb = {}
        cid_ctr = [0]

        def emit_pv(st):
            m, kt, nk, pi, c0, cid = st
            for (ab, lfn, rds) in m["pv"]:
                mm(ps[ab][:, c0:512], lfn(kt), PT[pi][:, c0:512], kt == 0, kt == nk - 1,
                   ["pt%d" % pi] + rds, ["ps%d" % ab], True)

        def drain_check():
            live = set(e[5] for e in att_pend)
            for cid in list(att_cb.keys()):
                if cid not in live:
                    att_cb.pop(cid)()

        def att_flush():
            while att_pend:
                emit_pv(att_pend.pop(0))
            drain_check()

        def attend_chunk(qc, maps, st_ring, cid):
            nk = 4 * qc + 4
            LAG = 3 if len(st_ring.items) >= 4 else 2
            nstep = [0]
            for kt in range(nk):
                for mi, m in enumerate(maps):
                    j = kt - 4 * qc
                    c0 = max(j, 0) * 128
                    b = st_ring.next()
                    diag = j >= 0
                    mm(ps[b][:, c0:512], m["k"][:, kt * 128:(kt + 1) * 128], m["q"][:, qc * 512 + c0:(qc + 1) * 512],
                       True, not diag, m["rn"], ["ps%d" % b], not diag)
                    if diag:
                        mm(ps[b][:, c0:c0 + 128], ident_b[:], negmask_b[:], False, True,
                           ["ident_b", "negmask_b"], ["ps%d" % b], True)
                    pi = pt_ring.next()
                    bias_ap, bias_names = m["bias"](kt)
                    act(PT[pi][:, c0:512], ps[b][:, c0:512], AF.Exp, ["ps%d" % b] + bias_names, ["pt%d" % pi],
                        bias=bias_ap, scale=0.125)
                    att_pend.append((m, kt, nk, pi, c0, cid))
                    while len(att_pend) > LAG:
                        emit_pv(att_pend.pop(0))
                        drain_check()
                    gstep[0] += 1
                    run_deferred(st_ring, force=False)

        ACC6 = [2, 3, 4, 5, 6, 7]
        diff_chunk_ctr = [0]
        st2 = Ring([0, 1])
        st4 = Ring([0, 1, 2, 3])

        def attend_diff(h):
            for qc in range(4):
                c = diff_chunk_ctr[0]
                diff_chunk_ctr[0] += 1
                A = [ACC6[(4 * c) % 6], ACC6[(4 * c + 2) % 6]]
                Sm = [ACC6[(4 * c + 1) % 6], ACC6[(4 * c + 3) % 6]]
                cid = cid_ctr[0]
                cid_ctr[0] += 1
                for m_ in range(2):
                    mp = dict(q=qk[m_][:, :], k=qk[2][:, :], rn=["qk%d" % m_, "qk2"],
                              bias=lambda kt: (None, []),
                              pv=[(A[m_], (lambda kt: Vd1[:, kt, :]), ["V"]),
                                  (Sm[m_], (lambda kt: ones_b[:]), ["ones_b"])])
                    attend_chunk(qc, [mp], st2, cid)

                def part1(A=A, Sm=Sm, h=h, qc=qc):
                    run_deferred(st2)
                    cs = slice(qc * 512, (qc + 1) * 512)
                    a0, s0, a1, s1 = A[0], Sm[0], A[1], Sm[1]
                    recip(T[0], ps[s0][:], ["ps%d" % s0], ["T0"])
                    tt(T[1], ps[a0][:], T[0], ALU.mult, ["ps%d" % a0, "T0"], ["T1"])
                    recip(T[0], ps[s1][:], ["ps%d" % s1], ["T0"])
                    tt(T[2], ps[a1][:], T[0], ALU.mult, ["ps%d" % a1, "T0"], ["T2"])
                    stt(T[3], T[2], neglam[:, 0:1], T[1], ALU.mult, ALU.add, ["T2", "T1", "neglam"], ["T3"])
                    sqv = T[2].bitcast(BF16)[:, 0:512]
                    tt(sqv, T[3], T[3], ALU.mult, ["T3", "T2"], ["T2"])

                    def part2(ring, sqv=sqv, h=h, qc=qc, cs=cs):
                        b = ring.next()
                        mm(ps[b][:], ones_b[:], sqv, True, True, ["ones_b", "T2"], ["ps%d" % b], True)
                        act(T[0], ps[b][:], AF.Ln, ["ps%d" % b], ["T0"], bias=EPS, scale=1.0 / 128)
                        act(T[1], T[0], AF.Exp, ["T0"], ["T1"], scale=-0.5)
                        stt(oT[:, h, cs], T[3], gcol[:, 0:1], T[1], ALU.mult, ALU.mult, ["T3", "T1", "gcol"],
                            ["o%d_%d" % (h, qc)])
                    deferred.append((gstep[0] + 8, part2))
                att_cb[cid] = part1
                drain_check()
            att_flush()

        def attend_single(pair, chunk, R, bias_fn):
            for hl in range(2):
                h = 2 * pair + hl
                for qc in range(4):
                    ab = 4 + (hl * 4 + qc) % 4
                    cid = cid_ctr[0]
                    cid_ctr[0] += 1
                    maps = [dict(q=qk[hl][0:R, :], k=qk[2 + hl][0:R, :], rn=["qk%d" % hl, "qk%d" % (2 + hl)],
                                 bias=(lambda kt, h_=h: bias_fn(h_, kt)),
                                 pv=[(ab, (lambda kt, hl_=hl: Vs[:, kt, hl_, :]), ["V"])])]
                    attend_chunk(qc, maps, st4, cid)

                    def fin(ab=ab, hl=hl, qc=qc):
                        cs = slice(qc * 512, (qc + 1) * 512)
                        P.op("dve", lambda e: e.reciprocal(out=T[0][64:128, :], in_=ps[ab][64:128, :]),
                             (), ["ps%d" % ab, "T0"])
                        d0 = hl * 64
                        tt(oT[d0:d0 + 64, chunk, cs], ps[ab][0:64, :], T[0][64:128, :], ALU.mult,
                           ["ps%d" % ab, "T0"], ["o%d_%d" % (chunk, qc)])
                    att_cb[cid] = fin
                    drain_check()
            att_flush()

        def branch_diff(l, h):
            claim(N_QK + N_V + N_PT)
            wqk, nqk = wload([w_inT_d[l, h], w_inT_d[l, 4 + h]])
            wvv, nv = wload([w_inT_d[l, 8 + h]])
            if h == 0:
                memset(qk[0][64:128, :], 0.0, ["qk0"])
                memset(qk[1][0:64, :], 0.0, ["qk1"])
            proj_F(wqk[0], nqk, True, [(0, 64, qk[0], 0, "qk0"), (64, 64, qk[1], 64, "qk1")])
            proj_F(wqk[1], nqk, True, [(0, 128, qk[2], 0, "qk2")])

            def evac(t, b):
                cp(Vd1[:, t, :], ps[b][:, 0:128], ["ps%d" % b], ["V"])
            proj_V(wvv[0], nv, 128, evac)
            if h == 0:
                dump("qd0", qk[0], ["qk0"])
                dump("kd0", qk[2], ["qk2"])
            attend_diff(h)

        def branch_fox(l, pair):
            claim(N_QK + N_V + N_PT)
            wqk, nqk = wload([w_inT_d[l, 12 + pair], w_inT_d[l, 14 + pair]])
            pieces = [w_inT_d[l, 16 + pair]]
            if pair == 0:
                pieces.append(w_fgt_d[l].rearrange("(k p) n -> p k n", p=128))
            wvv, nv = wload(pieces)
            proj_F(wqk[0], nqk, False, [(0, 64, qk[0], 0, "qk0"), (64, 64, qk[1], 0, "qk1")])
            proj_F(wqk[1], nqk, False, [(0, 64, qk[2], 0, "qk2"), (64, 64, qk[3], 0, "qk3")])
            memset(Vs[:, :, :, 64:128], 1.0, ["V"], eng="pool")
            for i in range(4):
                memset(qk[i][64:128, :], 0.0, ["qk%d" % i], eng="pool")
            memset(qk[2][64:65, :], 1.0, ["qk2"], eng="pool")
            memset(qk[3][64:65, :], 1.0, ["qk3"], eng="pool")

            def evac(t, b):
                cp(Vs[:, t, :, 0:64], ps[b][:, 0:128].rearrange("p (j c) -> p j c", c=64), ["ps%d" % b], ["V"])
                if pair == 0:
                    tt(zb[:, :, t], ps[b][:, 128:132], bfrep[:], ALU.add, ["ps%d" % b, "bfrep"], ["zb"])
            proj_V(wvv[0], nv, 128, evac, extra=((wvv[1], 4) if pair == 0 else None))
            if pair == 0:
                act(lf[:], zb[:], AF.Exp, ["zb"], ["lf"], scale=-1.0)
                act(lf[:], lf[:], AF.Ln, ["lf"], ["lf"], bias=1.0)
                b1 = bank_aux.next()
                lf2 = lf[:].rearrange("p h t -> p (h t)")
                mm(ps[b1][:, 0:64], U_f[:], lf2, True, True, ["U_f", "lf"], ["ps%d" % b1], True)
                b2 = bank_aux.next()
                mm(ps[b2][:, 0:64], ones_f[:], lf2, True, True, ["ones_f", "lf"], ["ps%d" % b2], True)
                cp(tots[:].rearrange("p h t -> p (h t)"), ps[b2][:, 0:64], ["ps%d" % b2], ["tots"])
                memset(offs[:, :, 0:1], 0.0, ["offs"])
                for i in range(1, NT):
                    tt(offs[:, :, i:i + 1], offs[:, :, i - 1:i], tots[:, :, i - 1:i], ALU.add, ["offs", "tots"], ["offs"])
                tt(Lcum[:].rearrange("p h t -> p (h t)"), ps[b1][:, 0:64], offs[:].rearrange("p h t -> p (h t)"),
                   ALU.add, ["ps%d" % b1, "offs"], ["Lcum"])
                ts(r8[:], Lcum[:], -8.0, None, ALU.mult, None, ["Lcum"], ["r8"])
                dump("Lcum", Lcum[:], ["Lcum"])
            for hl in range(2):
                h = 2 * pair + hl
                for g in range(4):
                    b = bank_aux.next()
                    for i in range(4):
                        t = 4 * g + i
                        mm(ps[b][0:1, i * 128:(i + 1) * 128], r8[:, h, t:t + 1], ident_b[:], True, True,
                           ["r8", "ident_b"], ["ps%d" % b], i == 3)
                    act(qk[hl][64:65, g * 512:(g + 1) * 512], ps[b][0:1, :], AF.Identity, ["ps%d" % b], ["qk%d" % hl])
            if pair == 0:
                dump("qf0", qk[0], ["qk0"])
                dump("kf0", qk[2], ["qk2"])

            def bias_fn(h, kt):
                return Lcum[:, h, kt:kt + 1], ["Lcum"]
            attend_single(pair, 4 + pair, 128, bias_fn)

        def branch_moba(l, pair):
            claim(N_QK + N_V + N_PT)
            wqk, nqk = wload([w_inT_d[l, 18 + pair], w_inT_d[l, 20 + pair]])
            wvv, nv = wload([w_inT_d[l, 22 + pair]])
            proj_F(wqk[0], nqk, True, [(0, 64, qk[0], 0, "qk0"), (64, 64, qk[1], 0, "qk1")])
            proj_F(wqk[1], nqk, True, [(0, 64, qk[2], 0, "qk2"), (64, 64, qk[3], 0, "qk3")])
            memset(Vs[:, :, :, 64:128], 1.0, ["V"], eng="pool")
            for i in range(4):
                memset(qk[i][64:128, :], 0.0, ["qk%d" % i], eng="pool")
            for hl in range(2):
                dma("pool", qk[2 + hl][64:72, :], onehot_d, (), ["qk%d" % (2 + hl)])

            def evac(t, b):
                cp(Vs[:, t, :, 0:64], ps[b][:, 0:128].rearrange("p (j c) -> p j c", c=64), ["ps%d" % b], ["V"])
            proj_V(wvv[0], nv, 128, evac)
            bg = bank_aux.next()
            for hl in range(2):
                P.op("dve", (lambda hl_: lambda e: e.tensor_reduce(
                    out=kms[0:64, hl_, :], in_=qk[2 + hl_][0:64, :].rearrange("p (n k) -> p n k", k=256),
                    axis=AX.X, op=ALU.add))(hl), ["qk%d" % (2 + hl)], ["kms%d" % hl])
                ts(kmT[0:64, hl, :], kms[0:64, hl, :], 1.0 / 256, None, ALU.mult, None, ["kms%d" % hl], ["kmT%d" % hl])
                for i in range(8):
                    qt = 8 + i
                    c = (hl * 8 + i) * 8
                    mm(ps[bg][:, c:c + 8], qk[hl][0:64, qt * 128:(qt + 1) * 128], kmT[0:64, hl, :], True, True,
                       ["qk%d" % hl, "kmT%d" % hl], ["ps%d" % bg], (hl == 1 and i == 7))
            memset(gm[:], -1e30, ["gm"])
            for hl in range(2):
                for i in range(8):
                    own = (8 + i) // 2
                    c = (hl * 8 + i) * 8
                    cp(gm[:, hl, i, 0:own], ps[bg][:, c:c + own], ["ps%d" % bg, "gm"], ["gm"])
            for hl in range(2):
                for i in range(8):
                    own = (8 + i) // 2
                    P.op("dve", (lambda hl_, i_: lambda e: e.max(out=mx8[:], in_=gm[:, hl_, i_, :]))(hl, i),
                         ["gm"], ["mx8"])
                    ts(selb[:, hl, i, 0:own], gm[:, hl, i, 0:own], mx8[:, 2:3], NEGBIG, ALU.is_lt, ALU.mult,
                       ["gm", "mx8"], ["selb"])
            for i in range(8):
                qt = 8 + i
                b = bank_aux.next()
                for hl in range(2):
                    mm(ps[b][0:8, hl * 128:(hl + 1) * 128], selb[:, hl, i, :], ident_b[:], True, True,
                       ["selb", "ident_b"], ["ps%d" % b], hl == 1)
                for hl in range(2):
                    act(qk[hl][64:72, qt * 128:(qt + 1) * 128], ps[b][0:8, hl * 128:(hl + 1) * 128], AF.Identity,
                        ["ps%d" % b], ["qk%d" % hl])
            if pair == 0:
                dump("qm0", qk[0], ["qk0"])
                dump("km0", qk[2], ["qk2"])

            def bias_fn(h, kt):
                return None, []
            attend_single(pair, 6 + pair, 128, bias_fn)

        def update_x(half, nslab):
            unames = ["ssu%d" % i for i in range(8)]
            P.op("dve", lambda e: e.reduce_sum(out=ssu[:, 0:8], in_=ssq[:, :, 0:nslab], axis=AX.X),
                 ["ssq%d" % t for t in range(8)], unames)
            rmsn(ssu[:], lnvu[:], rstdu[:], unames, "lnvu", "rstdu", 1.0 / D)
            for t in range(8):
                tg = half * 8 + t
                stt(X[:, tg, :], ybuf[:, t, :], rstdu[:, t:t + 1], X[:, tg, :], ALU.mult, ALU.add,
                    ["yb%d" % t, "rstdu", "X%d" % tg], ["X%d" % tg])

        def evac_y(b, t, sl, ncol):
            cs = slice(sl * ncol, (sl + 1) * ncol)
            ji = junk_ring.next()
            act(junk2[:, ji, 0:ncol], ps[b][:, 0:ncol], AF.Square, ["ps%d" % b], ["ssq%d" % t, "junk%d" % ji],
                accum=ssq[:, t, sl:sl + 1])
            tt(ybuf[:, t, cs], ps[b][:, 0:ncol], G[:, cs], ALU.mult, ["ps%d" % b] + N_G, ["yb%d" % t])

        def merge_half(l, half):
            claim(N_MG)
            for fc in range(8):
                wa, na = wload([w_inT_d[l, 24 + fc], w_inT_d[l, 32 + fc]])
                wb, nb = wload([w_inT_d[l, 40 + fc], w_brT_d[0][l, fc], w_brT_d[1][l, fc], w_brT_d[2][l, fc]])
                for tcl in range(2):
                    tc = half * 2 + tcl
                    cs = slice(tc * 512, (tc + 1) * 512)
                    csl = slice(tcl * 512, (tcl + 1) * 512)
                    for br in range(3):
                        bgk = bank_main.next()
                        wsrc, wnm = (wa[0], na) if br == 0 else ((wa[1], na) if br == 1 else (wb[0], nb))
                        for kc in range(8):
                            mm(ps[bgk][:], wsrc[:, kc, :], hT[:, kc, cs], kc == 0, kc == 7,
                               wnm + ["hT%d" % tc], ["ps%d" % bgk], kc == 7)
                        by = bank_aux.next()
                        rng_ = (range(0, 4), range(4, 6), range(6, 8))[br]
                        kcs = [(wb[1 + br][:, kc - rng_[0], :], nb, kc) for kc in rng_]
                        for i, (wap, wnm2, oc) in enumerate(kcs):
                            mm(ps[by][:], wap, oT[:, oc, cs], i == 0, i == len(kcs) - 1,
                               wnm2 + ["o%d_%d" % (oc, tc)], ["ps%d" % by], i == len(kcs) - 1)
                        sg = T[br % 2]
                        sgn = "T%d" % (br % 2)
                        act(sg, ps[bgk][:], AF.Sigmoid, ["ps%d" % bgk], [sgn])
                        if br == 0:
                            tt(T[2], ps[by][:], sg, ALU.mult, ["ps%d" % by, sgn], ["T2"])
                        elif br == 1:
                            tt(T[3], ps[by][:], sg, ALU.mult, ["ps%d" % by, sgn], ["T3"])
                            tt(T[2], T[2], T[3], ALU.add, ["T2", "T3"], ["T2"])
                        else:
                            tt(T[3], ps[by][:], sg, ALU.mult, ["ps%d" % by, sgn], ["T3"])
                            tt(mg[:, fc, csl], T[2], T[3], ALU.add, ["T2", "T3"], ["mg%d" % tcl])
            if half == 0:
                dump("mg", R12[:, 16384:24576], N_MG)

        def wout_half(l, half):
            claim(N_YB)
            memset(ssq[:], 0.0, ["ssq%d" % t for t in range(8)])
            for sl in range(4):
                wvl, wn = wload([w_outT_d[l, sl]])
                wv = wvl[0]
                for t in range(8):
                    b = bank_main.next()
                    for kc in range(8):
                        mm(ps[b][:, 0:256], mg[:, kc, t * 128:(t + 1) * 128], wv[:, kc, :], kc == 0, kc == 7,
                           wn + ["mg%d" % (t // 4)], ["ps%d" % b], kc == 7)
                    evac_y(b, t, sl, 256)

        def gate_up(l, half):
            claim(N_AC)
            for j in range(NJ):
                wv, wn = wload([w_guT_d[l, j], w_guT_d[l, NJ + j]])
                for tcl in range(2):
                    tc = half * 2 + tcl
                    cs = slice(tc * 512, (tc + 1) * 512)
                    ba = bank_main.next()
                    for kc in range(8):
                        mm(ps[ba][:], wv[0][:, kc, :], hT[:, kc, cs], kc == 0, kc == 7, wn + ["hT%d" % tc],
                           ["ps%d" % ba], kc == 7)
                    bb = bank_aux.next()
                    for kc in range(8):
                        mm(ps[bb][:], wv[1][:, kc, :], hT[:, kc, cs], kc == 0, kc == 7, wn + ["hT%d" % tc],
                           ["ps%d" % bb], kc == 7)
                    sg = T[(j * 2 + tcl) % 2]
                    sgn = "T%d" % ((j * 2 + tcl) % 2)
                    act(sg, ps[ba][:], AF.Silu, ["ps%d" % ba], [sgn])
                    tt(actT[:, j, tcl * 512:(tcl + 1) * 512], ps[bb][:], sg, ALU.mult, ["ps%d" % bb, sgn], ["ac%d" % tcl])

        def down(l, half):
            claim(N_YB)
            memset(ssq[:], 0.0, ["ssq%d" % t for t in range(8)])
            for fcol in range(8):
                wvl, wn = wload([w_dnT_d[l, fcol]])
                wv = wvl[0]
                for t in range(8):
                    b = bank_main.next()
                    for kc in range(NJ):
                        mm(ps[b][:, 0:128], actT[:, kc, t * 128:(t + 1) * 128], wv[:, kc, :], kc == 0, kc == NJ - 1,
                           wn + ["ac%d" % (t // 4)], ["ps%d" % b], kc == NJ - 1)
                    evac_y(b, t, fcol, 128)
            update_x(half, 8)

        def ffn(l, s):
            stage_a(list(range(0, 8)), 16)
            gate_up(l, 0)
            stage_a(list(range(8, 16)), 16)
            down(l, 0)
            if l == layers[-1]:
                finish_tiles(s, range(0, 8))
            gate_up(l, 1)
            down(l, 1)
            if l == layers[-1]:
                finish_tiles(s, range(8, 16))

        def finish_tiles(s, tiles):
            for t in tiles:
                dma("sp", out_d[s, t * 128:(t + 1) * 128, :], X[:, t, :], ["X%d" % t], ())
            if s + 1 < n_seq:
                for t in tiles:
                    dma("sp", X[:, t, :], x_d[s + 1, t * 128:(t + 1) * 128, :], (), ["X%d" % t])

        def program():
            for s in range(n_seq):
                if s == 0:
                    for t in range(NT):
                        dma("sp", X[:, t, :], x_d[s, t * 128:(t + 1) * 128, :], (), ["X%d" % t])
                for kc in range(8):
                    cp(cbc[:, kc, :], ca[:, kc * 2 + s: kc * 2 + s + 1].to_broadcast([128, 128]), ["ca"], ["cbc"])
                rope_tables(s)
                for l in layers:
                    adaln_cols(s, l)
                    adaln_G(s, l, 0)
                    stage_a(list(range(NT)), 0)
                    dump("hT", hT[:], ["hT%d" % i for i in range(4)])
                    if stop == "a":
                        return
                    claim(N_O)
                    for h in range(4):
                        branch_diff(l, h)
                        if stop == "d0":
                            dump("oT2", R12[:, 0:4096], N_O)
                            return
                    for pair in range(2):
                        branch_fox(l, pair)
                    for pair in range(2):
                        branch_moba(l, pair)
                    dump("oT", R12[:, 0:16384], N_O)
                    if stop == "attn":
                        return
                    for half in range(2):
                        merge_half(l, half)
                        wout_half(l, half)
                        update_x(half, 4)
                    dump("xmix", X[:], ["X%d" % t for t in range(NT)])
                    if stop == "mix":
                        return
                    adaln_G(s, l, 1)
                    ffn(l, s)

        def lam_init(l):
            return 0.8 - 0.6 * math.exp(-0.3 * l)

        program()
        P.wait_tokens("sp", P.all_tokens())
        P.replay()
    nc._dbg_names = list(dbg_d.keys())
    nc._n_instr = {e: len(P.streams[e]) for e in ENGS}
    return nc


def host_consts():
    inv_freq = 1.0 / (10000.0 ** (np.arange(0, 64, 2, dtype=np.float32) / 64.0))
    invf = np.tile(inv_freq.astype(np.float32), 4).reshape(128, 1)
    rmat = np.zeros((128, 128), np.float32)
    for m in range(128):
        if m % 64 < 32:
            rmat[m + 32, m] = -1.0
        else:
            rmat[m - 32, m] = 1.0
    ident = np.eye(128, dtype=np.float32)
    mask01 = (np.arange(128)[:, None] <= np.arange(128)[None, :]).astype(np.float32)
    onehot = (np.arange(S)[None, :] // 256 == np.arange(8)[:, None]).astype(np.float32)
    negmask = np.where(np.arange(128)[:, None] > np.arange(128)[None, :], -30000.0, 0.0).astype(np.float32)
    return dict(invf=invf, rmat=rmat, ident=ident, mask01=mask01, onehot=onehot, negmask=negmask)


def make_in_maps(inputs, n_cores=8, n_seq=2):
    f = lambda a: np.ascontiguousarray(np.asarray(a))
    consts = host_consts()
    b_ada = f(inputs["b_ada"])
    badaT = np.ascontiguousarray(b_ada.reshape(2, 48, 128).transpose(0, 2, 1))
    gpm = f(inputs["g_pre_mix"]).reshape(2, 8, 128).transpose(0, 2, 1)
    gpf = f(inputs["g_pre_ffn"]).reshape(2, 8, 128).transpose(0, 2, 1)
    gsl = f(inputs["g_subln"]).reshape(2, 128, 1)
    gcols = np.ascontiguousarray(np.concatenate([gpm, gpf, gsl], axis=2))
    def tile_w(w, ncol):
        w = f(w)
        L, K, N = w.shape
        return np.ascontiguousarray(w.reshape(L, K // 128, 128, N // ncol, ncol).transpose(0, 3, 2, 1, 4))
    w_in = f(inputs["w_in"])
    shared = dict(
        w_adaT=tile_w(inputs["w_ada"], 128), b_ada=b_ada, badaT=badaT, gcols=gcols,
        g_post_mix=f(inputs["g_post_mix"]), g_post_ffn=f(inputs["g_post_ffn"]),
        w_inT=tile_w(w_in[:, :, 0:6144], 128), w_fgt=np.ascontiguousarray(w_in[:, :, 6144:6148]),
        b_fgt=f(inputs["b_fgt"]),
        lam_q1=f(inputs["lam_q1"]), lam_k1=f(inputs["lam_k1"]), lam_q2=f(inputs["lam_q2"]), lam_k2=f(inputs["lam_k2"]),
        w_braT=tile_w(inputs["w_br_a"], 128), w_brbT=tile_w(inputs["w_br_b"], 128), w_brcT=tile_w(inputs["w_br_c"], 128),
        w_outT=tile_w(inputs["w_out"], 256), w_guT=tile_w(inputs["w_gate_up"], 128),
        w_dnT=tile_w(inputs["w_down"], 128), **consts)
    x = f(inputs["x"])
    c = f(inputs["c"])
    pos = f(inputs["positions"]).astype(np.int32)
    maps = []
    for i in range(n_cores):
        bs = slice(i * n_seq, (i + 1) * n_seq)
        cc = c[bs]
        cT = np.zeros((128, 16), np.float32)
        cT[:, : 8 * 2] = 0
        for b in range(n_seq):
            cT[:, b::2][:, :8] = cc[b].reshape(8, 128).T
        m = dict(shared)
        m.update(x=np.ascontiguousarray(x[bs]), cT=cT, pos=np.ascontiguousarray(pos[bs]))
        maps.append(m)
    return maps


def kernel(**inputs):
    nc = build(n_seq=2, layers=(0, 1))
    maps = make_in_maps(inputs, 8, 2)
    res = run_bass_kernel_spmd(nc, maps, core_ids=list(range(8)))
    return np.concatenate([r["out"] for r in res.results], axis=0).astype(np.float32)
```

```python
import math
from contextlib import ExitStack
import numpy as np
import concourse.bass as bass
import concourse.mybir as mybir
from concourse.bass_utils import run_bass_kernel_spmd

F32 = mybir.dt.float32
BF16 = mybir.dt.bfloat16
I32 = mybir.dt.int32
AF = mybir.ActivationFunctionType
ALU = mybir.AluOpType
AX = mybir.AxisListType

ENGS = ("pe", "act", "dve", "pool", "sp")
N_DMA_SEMS = 24

D = 1024
S = 2048
NT = 16
DFF = 2816
NJ = 22
EPS = 1e-6
NEGBIG = -1920.0
PI = math.pi


class Plan:
    def __init__(self, nc):
        self.nc = nc
        self.streams = {e: [] for e in ENGS}
        self.count = {e: 0 for e in ENGS}
        self.wm = {e: {} for e in ENGS}
        self.res = {}
        self.dma_cnt = [0] * N_DMA_SEMS
        self.dma_rr = {"pool": 0, "sp": 0, "act": 0}
        self.sems = {}

    def _need(self, eng, reads, writes):
        need = {}

        def add(tok):
            if tok is None:
                return
            k, v = tok
            if need.get(k, 0) < v:
                need[k] = v

        for r in reads:
            ent = self.res.get(r)
            if ent is not None:
                add(ent[0])
        for w in writes:
            ent = self.res.get(w)
            if ent is not None:
                add(ent[0])
                for k, v in ent[1].items():
                    add((k, v))
        out = []
        for k, v in need.items():
            if k == "pe" and eng == "pe":
                continue
            if self.wm[eng].get(k, 0) >= v:
                continue
            self.wm[eng][k] = v
            out.append((k, v))
        return out

    def _record(self, tok, reads, writes):
        k, v = tok
        for r in reads:
            ent = self.res.setdefault(r, [None, {}])
            if ent[1].get(k, 0) < v:
                ent[1][k] = v
        for w in writes:
            self.res[w] = [tok, {}]

    def op(self, eng, fn, reads=(), writes=(), signal=True):
        writes = list(writes) + [r for r in reads if r.startswith("ps")]
        reads = [r for r in reads if not r.startswith("ps")]
        waits = self._need(eng, reads, writes)
        if signal:
            self.count[eng] += 1
            tok = (eng, self.count[eng])
            inc = (eng, 1)
        else:
            tok = (eng, self.count[eng] + 1)
            inc = None
        self._record(tok, reads, writes)
        self.streams[eng].append((waits, fn, inc))
        return tok

    def dma(self, eng, fn, reads=(), writes=()):
        half = N_DMA_SEMS // 2
        base = 0 if eng == "pool" else half
        s = base + self.dma_rr[eng]
        self.dma_rr[eng] = (self.dma_rr[eng] + 1) % half
        key = "dma%d" % s
        waits = self._need(eng, reads, writes)
        prev = 16 * self.dma_cnt[s]
        if prev and self.wm[eng].get(key, 0) < prev:
            self.wm[eng][key] = prev
            waits.append((key, prev))
        self.dma_cnt[s] += 1
        tok = (key, 16 * self.dma_cnt[s])
        self._record(tok, reads, writes)
        self.streams[eng].append((waits, fn, (key, 16)))
        return tok

    def fence(self, new, old):
        merged = {}
        for o in old:
            ent = self.res.get(o)
            if ent is None:
                continue
            if ent[0] is not None:
                k, v = ent[0]
                merged[k] = max(merged.get(k, 0), v)
            for k, v in ent[1].items():
                merged[k] = max(merged.get(k, 0), v)
        for n in new:
            ent = self.res.get(n)
            m2 = dict(merged)
            if ent is not None:
                if ent[0] is not None:
                    k, v = ent[0]
                    m2[k] = max(m2.get(k, 0), v)
                for k, v in ent[1].items():
                    m2[k] = max(m2.get(k, 0), v)
            self.res[n] = [None, m2]

    def wait_tokens(self, eng, toks):
        waits = []
        for k, v in toks:
            if self.wm[eng].get(k, 0) < v:
                self.wm[eng][k] = v
                waits.append((k, v))
        self.streams[eng].append((waits, None, None))

    def all_tokens(self):
        toks = {}
        for ent in self.res.values():
            if ent[0] is not None:
                k, v = ent[0]
                toks[k] = max(toks.get(k, 0), v)
            for k, v in ent[1].items():
                toks[k] = max(toks.get(k, 0), v)
        return list(toks.items())

    def replay(self):
        nc = self.nc
        with ExitStack() as es:
            for e in ENGS:
                self.sems[e] = es.enter_context(nc.semaphore("s_" + e))
            for i in range(N_DMA_SEMS):
                self.sems["dma%d" % i] = es.enter_context(nc.semaphore("s_dma%d" % i))
            block = es.enter_context(nc.Block())
            sems = self.sems

            def run(engname):
                def body(eng):
                    for waits, fn, inc in self.streams[engname]:
                        for k, v in waits:
                            eng.wait_ge(sems[k], v)
                        if fn is None:
                            continue
                        ins = fn(eng)
                        if inc is not None:
                            ins.then_inc(sems[inc[0]], inc[1])
                return body

            block.tensor(run("pe"))
            block.scalar(run("act"))
            block.vector(run("dve"))
            block.gpsimd(run("pool"))
            block.sync(run("sp"))


class Ring:
    def __init__(self, items):
        self.items = list(items)
        self.i = 0

    def next(self):
        v = self.items[self.i]
        self.i = (self.i + 1) % len(self.items)
        return v


def build(n_seq=2, layers=(0, 1), dbg=(), stop=None):
    nc = bass.Bass("TRN2", target_bir_lowering=False)

    def din(name, shape, dt=F32):
        return nc.dram_tensor(name, list(shape), dt, kind="ExternalInput").ap()

    x_d = din("x", [n_seq, S, D])
    cT_d = din("cT", [128, 16])
    pos_d = din("pos", [n_seq, S], I32)
    w_adaT_d = din("w_adaT", [2, 48, 128, 8, 128])
    b_ada_d = din("b_ada", [2, 6 * D])
    badaT_d = din("badaT", [2, 128, 48])
    gcols_d = din("gcols", [2, 128, 17])
    g_post_mix_d = din("g_post_mix", [2, D])
    g_post_ffn_d = din("g_post_ffn", [2, D])
    w_inT_d = din("w_inT", [2, 48, 128, 8, 128])
    w_fgt_d = din("w_fgt", [2, D, 4])
    b_fgt_d = din("b_fgt", [2, 4])
    lam_d = [din(n, [2, 64]) for n in ("lam_q1", "lam_k1", "lam_q2", "lam_k2")]
    w_brT_d = [din("w_braT", [2, 8, 128, 4, 128]), din("w_brbT", [2, 8, 128, 2, 128]),
               din("w_brcT", [2, 8, 128, 2, 128])]
    w_outT_d = din("w_outT", [2, 4, 128, 8, 256])
    w_guT_d = din("w_guT", [2, 44, 128, 8, 128])
    w_dnT_d = din("w_dnT", [2, 8, 128, NJ, 128])
    invf_d = din("invf", [128, 1])
    rmat_d = din("rmat", [128, 128])
    ident_d = din("ident", [128, 128])
    mask_d = din("mask01", [128, 128])
    negmask_d = din("negmask", [128, 128])
    onehot_d = din("onehot", [8, S])
    out_d = nc.dram_tensor("out", [n_seq, S, D], F32, kind="ExternalOutput").ap()
    dbg_d = {}

    es = ExitStack()
    with es:
        def sb(name, shape, dt):
            return es.enter_context(nc.sbuf_tensor(name, list(shape), dt))

        X = sb("X", [128, NT, D], F32)
        hT = sb("hT", [128, 8, S], BF16)
        R12 = sb("R12", [128, 24576], BF16)
        R3 = sb("R3", [128, 8192], BF16)
        tabC = sb("tabC", [128, S], BF16)
        tabS = sb("tabS", [128, S], BF16)
        NSLOT = 3
        wslot = [sb("wslot%d" % i, [128, 2816], BF16) for i in range(NSLOT)]
        G = sb("G", [128, D], F32)
        SCR = sb("SCR", [128, 2048], F32)
        ident_b = sb("ident_b", [128, 128], BF16)
        negmask_b = sb("negmask_b", [128, 128], BF16)
        rmat_b = sb("rmat_b", [128, 128], BF16)
        ones_b = sb("ones_b", [128, 128], BF16)
        U_f = sb("U_f", [128, 128], F32)
        ones_f = sb("ones_f", [128, 128], F32)
        invf = sb("invf_s", [128, 1], F32)
        halfpi = sb("halfpi", [128, 1], F32)
        modc1 = sb("modc1", [128, 2, 48], F32)
        cTs = sb("cTs", [128, 16], F32)
        ca = sb("ca", [128, 16], BF16)
        cbc = sb("cbc", [128, 8, 128], BF16)
        badaT = sb("badaT_s", [128, 48], F32)
        gcols = sb("gcols_s", [128, 17], F32)
        modc = sb("modc", [128, 48], F32)
        scsh = sb("scsh", [128, 32], F32)
        ss = sb("ss", [128, 16], F32)
        lnv = sb("lnv", [128, 16], F32)
        rstd = sb("rstd", [128, 16], F32)
        lamt = sb("lamt", [128, 4, 64], F32)
        lamp = sb("lamp", [128, 64], F32)
        lams = sb("lams", [128, 4], F32)
        neglam = sb("neglam", [128, 1], F32)
        gcol = sb("gcol", [128, 1], F32)
        bfrep = sb("bfrep", [128, 4], F32)
        zb = sb("zb", [128, 4, NT], F32)
        lf = sb("lf", [128, 4, NT], F32)
        tots = sb("tots", [128, 4, NT], F32)
        offs = sb("offs", [128, 4, NT], F32)
        Lcum = sb("Lcum", [128, 4, NT], F32)
        r8 = sb("r8", [128, 4, NT], BF16)
        kmT = sb("kmT", [128, 2, 8], BF16)
        kms = sb("kms", [128, 2, 8], F32)
        gm = sb("gm", [128, 2, 8, 8], F32)
        mx8 = sb("mx8", [128, 8], F32)
        selb = sb("selb", [128, 2, 8, 8], BF16)
        ssq = sb("ssq", [128, 8, 8], F32)
        ssu = sb("ssu", [128, 8], F32)
        lnvu = sb("lnvu", [128, 8], F32)
        rstdu = sb("rstdu", [128, 8], F32)
        junk2 = sb("junk2", [128, 4, 256], BF16)
        junk_ring = Ring([0, 1, 2, 3])

        ps = [es.enter_context(nc.psum_tensor("ps%d" % i, [128, 512], F32)) for i in range(8)]

        oT = R12[:, 0:16384].rearrange("p (c t) -> p c t", t=S)
        qk = [R12[:, 16384 + i * S: 16384 + (i + 1) * S] for i in range(4)]
        mg = R12[:, 16384:24576].rearrange("p (c t) -> p c t", t=1024)
        actT = R12[:, 0:22528].rearrange("p (c t) -> p c t", t=1024)

        Vd1 = R3[:, 0:2048].rearrange("p (t c) -> p t c", c=128)
        Vs = R3[:, 0:4096].rearrange("p (t j c) -> p t j c", j=2, c=128)
        PT = [R3[:, 4096 + i * 512: 4096 + (i + 1) * 512] for i in range(8)]
        ybuf = R3[:, 0:8192].rearrange("p (t c) -> p t c", c=1024)
        posi = R3[:, 0:4096].bitcast(I32)
        T = [SCR[:, i * 512:(i + 1) * 512] for i in range(4)]
        Ti = [T[i].bitcast(I32) for i in range(4)]
        xn = [T[2].bitcast(BF16), T[3].bitcast(BF16)]
        XN_NAMES = ["T2", "T3"]
        junkA = junk2[:].rearrange("p a b -> p (a b)")
        deferred = []

        gstep = [0]

        def run_deferred(ring, force=True):
            while deferred and (force or deferred[0][0] <= gstep[0]):
                deferred.pop(0)[1](ring)

        N_O = ["o%d_%d" % (c, t) for c in range(8) for t in range(4)]
        N_AC = ["ac0", "ac1"]
        N_QK = ["qk0", "qk1", "qk2", "qk3"]
        N_MG = ["mg0", "mg1"]
        N_XN = ["xn0", "xn1"]
        N_V = ["V"]
        N_PT = ["pt%d" % i for i in range(8)]
        N_YB = ["yb%d" % i for i in range(8)]
        N_POSI = ["posi"]
        REG_A = N_O + N_AC
        REG_B = N_QK + N_MG + N_AC
        REG_C = N_V + N_PT + N_YB + N_POSI
        N_T = ["T0", "T1", "T2", "T3"]
        WNAMES = [["w%d_%d" % (s_, j) for j in range(4)] for s_ in range(NSLOT)]

        P = Plan(nc)

        def claim(names):
            for reg in (REG_A, REG_B, REG_C):
                mine = [n for n in names if n in reg]
                if mine:
                    P.fence(mine, [n for n in reg if n not in mine])

        def mm(out, lhsT, rhs, start, stop, reads, writes, signal):
            P.op("pe", lambda e: e.matmul(out, lhsT=lhsT, rhs=rhs, start=start, stop=stop),
                 reads, writes, signal)

        def tr(out, in_, reads, writes, signal=True):
            P.op("pe", lambda e: e.transpose(out=out, in_=in_, identity=ident_b[:]), reads, writes, signal)

        def act(out, in_, func, reads, writes, bias=None, scale=None, accum=None):
            kw = {}
            if bias is not None:
                kw["bias"] = bias
            if scale is not None:
                kw["scale"] = scale
            if accum is not None:
                kw["accum_out"] = accum
            P.op("act", lambda e: e.activation(out=out, in_=in_, func=func, **kw), reads, writes)

        def tt(out, in0, in1, op, reads, writes, eng="dve"):
            P.op(eng, lambda e: e.tensor_tensor(out=out, in0=in0, in1=in1, op=op), reads, writes)

        def ts(out, in0, s1, s2, op0, op1, reads, writes, eng="dve"):
            if op1 is None:
                P.op(eng, lambda e: e.tensor_scalar(out=out, in0=in0, scalar1=s1, scalar2=None, op0=op0), reads, writes)
            else:
                P.op(eng, lambda e: e.tensor_scalar(out=out, in0=in0, scalar1=s1, scalar2=s2, op0=op0, op1=op1), reads, writes)

        def stt(out, in0, scalar, in1, op0, op1, reads, writes, eng="dve"):
            P.op(eng, lambda e: e.scalar_tensor_tensor(out=out, in0=in0, scalar=scalar, in1=in1, op0=op0, op1=op1),
                 reads, writes)

        def cp(out, in_, reads, writes, eng="dve"):
            P.op(eng, lambda e: e.tensor_copy(out=out, in_=in_), reads, writes)

        def recip(out, in_, reads, writes):
            act(out, in_, AF.Ln, reads, writes)
            act(out, out, AF.Exp, list(writes), writes, scale=-1.0)

        def memset(ap, val, writes, eng="dve"):
            P.op(eng, lambda e: e.memset(ap, val), (), writes)

        def dma(eng, out, in_, reads, writes):
            P.dma(eng, lambda e: e.dma_start(out=out, in_=in_), reads, writes)

        def dump(name, ap, reads):
            if name not in dbg:
                return
            d = nc.dram_tensor("dbg_" + name, list(ap.shape), ap.dtype, kind="ExternalOutput").ap()
            dbg_d[name] = d
            dma("sp", d, ap, reads, ())

        bank_main = Ring([0, 1, 2, 3])
        bank_aux = Ring([4, 5, 6, 7])
        pt_ring = Ring(list(range(8)))
        slot_ring = Ring(list(range(NSLOT)))

        def wload(pieces):
            s_ = slot_ring.next()
            views = []
            off = 0
            for j, src in enumerate(pieces):
                kc_, n_ = src.shape[1], src.shape[2]
                v = wslot[s_][:, off:off + kc_ * n_].rearrange("p (k n) -> p k n", n=n_)
                off += kc_ * n_
                dma("pool", v, src, (), [WNAMES[s_][j]])
                views.append(v)
            assert off <= 2816
            return views, WNAMES[s_]

        dma("pool", ident_b[:], ident_d, (), ["ident_b"])
        dma("pool", negmask_b[:], negmask_d, (), ["negmask_b"])
        dma("pool", rmat_b[:], rmat_d, (), ["rmat_b"])
        dma("sp", U_f[:], mask_d, (), ["U_f"])
        dma("sp", invf[:], invf_d, (), ["invf"])
        dma("sp", cTs[:], cT_d, (), ["cTs"])
        memset(ones_b[:], 1.0, ["ones_b"])
        memset(halfpi[:], PI / 2, ["halfpi"])
        memset(ones_f[:], 1.0, ["ones_f"])
        memset(selb[:], 0.0, ["selb"])
        act(ca[:], cTs[:], AF.Silu, ["cTs"], ["ca"])

        def rmsn(ss_ap, lnv_ap, rstd_ap, in_names, ln_name, out_name, scale):
            act(lnv_ap, ss_ap, AF.Ln, list(in_names), [ln_name], bias=EPS, scale=scale)
            act(rstd_ap, lnv_ap, AF.Exp, [ln_name], [out_name], scale=-0.5)

        def rope_tables(s):
            claim(N_POSI)
            dma("sp", posi, pos_d[s:s + 1, :].to_broadcast([128, S]), (), ["posi"])
            for j in range(4):
                cs = slice(j * 512, (j + 1) * 512)
                ts(T[0], posi[:, cs], invf[:, 0:1], None, ALU.mult, None, ["posi", "invf"], ["T0"])
                ts(Ti[1], T[0], 1.0 / (2 * PI), None, ALU.mult, None, ["T0"], ["T1"])
                stt(T[2], Ti[1], -2 * PI, T[0], ALU.mult, ALU.add, ["T1", "T0"], ["T2"])
                ts(T[3], T[2], PI, -2 * PI, ALU.is_gt, ALU.mult, ["T2"], ["T3"])
                tt(T[2], T[2], T[3], ALU.add, ["T2", "T3"], ["T2"])
                ts(T[2], T[2], -PI, PI, ALU.max, ALU.min, ["T2"], ["T2"])
                act(tabS[:, cs], T[2], AF.Sin, ["T2"], ["tab0_%d" % j])
                stt(T[3], T[2], -1.0, T[2], ALU.mult, ALU.max, ["T2"], ["T3"])
                act(tabC[:, cs], T[3], AF.Sin, ["T3", "halfpi"], ["tab1_%d" % j], bias=halfpi[:, 0:1], scale=-1.0)
            dump("tabC", tabC[:], ["tab1_%d" % j for j in range(4)])
            dump("tabS", tabS[:], ["tab0_%d" % j for j in range(4)])

        def adaln_cols(s, l):
            dma("sp", badaT[:], badaT_d[l], (), ["badaT"])
            dma("sp", gcols[:], gcols_d[l], (), ["gcols"])
            if s == 0:
                b = bank_aux.next()
                chunks = list(range(0, 16)) + list(range(24, 40))
                for i in range(0, len(chunks), 2):
                    j0 = chunks[i]
                    wv, wn = wload([w_adaT_d[l, j0], w_adaT_d[l, j0 + 1]])
                    for jj in range(2):
                        j = j0 + jj
                        for kc in range(8):
                            mm(ps[b][:, 2 * j:2 * j + n_seq], wv[jj][:, kc, :],
                               ca[:, kc * 2: kc * 2 + n_seq], kc == 0, kc == 7, wn + ["ca"], ["ps%d" % b], kc == 7)
                psv = ps[b][:, 0:96].rearrange("p (j b) -> p j b", b=2)
                for (a0, a1) in ((0, 16), (24, 40)):
                    tt(modc[:, a0:a1], psv[:, a0:a1, 0], badaT[:, a0:a1], ALU.add, ["ps%d" % b, "badaT", "modc"], ["modc"])
                    if n_seq > 1:
                        tt(modc1[:, l, a0:a1], psv[:, a0:a1, 1], badaT[:, a0:a1], ALU.add,
                           ["ps%d" % b, "badaT", "modc1_%d" % l], ["modc1_%d" % l])
            else:
                for (a0, a1) in ((0, 16), (24, 40)):
                    cp(modc[:, a0:a1], modc1[:, l, a0:a1], ["modc1_%d" % l, "modc"], ["modc"])
            stt(scsh[:, 0:8], modc[:, 8:16], 1.0, gcols[:, 0:8], ALU.add, ALU.mult, ["modc", "gcols"], ["scsh"])
            cp(scsh[:, 8:16], modc[:, 0:8], ["modc"], ["scsh"])
            stt(scsh[:, 16:24], modc[:, 32:40], 1.0, gcols[:, 8:16], ALU.add, ALU.mult, ["modc", "gcols"], ["scsh"])
            cp(scsh[:, 24:32], modc[:, 24:32], ["modc"], ["scsh"])
            ts(gcol[:], gcols[:, 16:17], 1.0 - lam_init(l), None, ALU.mult, None, ["gcols"], ["gcol"])
            for i in range(4):
                dma("sp", lamt[:, i, :], lam_d[i][l:l + 1, :].to_broadcast([128, 64]), (), ["lamt%d" % i])
            for i in range(2):
                tt(lamp[:], lamt[:, 2 * i, :], lamt[:, 2 * i + 1, :], ALU.mult,
                   ["lamt%d" % (2 * i), "lamt%d" % (2 * i + 1)], ["lamp"])
                P.op("dve", (lambda i_: lambda e: e.reduce_sum(out=lams[:, i_:i_ + 1], in_=lamp[:], axis=AX.X))(i),
                     ["lamp"], ["lams%d" % i])
            act(lams[:, 2:4], lams[:, 0:2], AF.Exp, ["lams0", "lams1"], ["lamse"])
            tt(neglam[:], lams[:, 3:4], lams[:, 2:3], ALU.subtract, ["lamse"], ["neglam"])
            ts(neglam[:], neglam[:], -lam_init(l), None, ALU.add, None, ["neglam"], ["neglam"])
            dma("sp", bfrep[:], b_fgt_d[l:l + 1, :].to_broadcast([128, 4]), (), ["bfrep"])

        def adaln_G(s, l, which):
            c0 = 2048 if which == 0 else 5120
            gp = g_post_mix_d if which == 0 else g_post_ffn_d
            brep = SCR[:, 0:1024]
            grep = SCR[:, 1024:2048]
            dma("sp", brep, b_ada_d[l:l + 1, c0:c0 + 1024].to_broadcast([128, 1024]), (), ["T0", "T1"])
            dma("sp", grep, gp[l:l + 1, :].to_broadcast([128, 1024]), (), ["T2", "T3"])
            for q in range(4):
                jq = c0 // 128 + 2 * q
                wv, wn = wload([w_adaT_d[l, jq], w_adaT_d[l, jq + 1]])
                b = bank_main.next()
                for jj in range(2):
                    for kc in range(8):
                        mm(ps[b][:, jj * 128:(jj + 1) * 128], cbc[:, kc, :], wv[jj][:, kc, :], kc == 0, kc == 7,
                           wn + ["cbc"], ["ps%d" % b], kc == 7)
                cs = slice(q * 256, (q + 1) * 256)
                tt(G[:, cs], ps[b][:, 0:256], brep[:, cs], ALU.add, ["ps%d" % b, "T0", "T1"], ["G%d" % q])
                tt(G[:, cs], G[:, cs], grep[:, cs], ALU.mult, ["G%d" % q, "T2", "T3"], ["G%d" % q])

        N_G = ["G0", "G1", "G2", "G3"]

        def stage_a(tiles, off):
            ngrp = len(tiles) // 4

            def sq(g):
                grp = tiles[4 * g:4 * g + 4]
                c0 = grp[0]
                names = ["ss%d" % t for t in grp]
                memset(ss[:, c0:c0 + 4], 0.0, names)
                for t in grp:
                    act(junkA, X[:, t, :], AF.Square, ["X%d" % t], ["junk0", "junk1", "junk2", "junk3", "ss%d" % t],
                        accum=ss[:, t:t + 1])
                rmsn(ss[:, c0:c0 + 4], lnv[:, c0:c0 + 4], rstd[:, c0:c0 + 4], names, "lnv%d" % (c0 // 4),
                     "rstd%d" % (c0 // 4), 1.0 / D)

            sq(0)
            for g in range(ngrp):
                if g + 1 < ngrp:
                    sq(g + 1)
                grp = tiles[4 * g:4 * g + 4]
                tc = grp[0] // 4
                banks = [bank_main.next(), bank_main.next(), bank_aux.next(), bank_aux.next()]
                for i, t in enumerate(grp):
                    ts(xn[i % 2], X[:, t, :], rstd[:, t:t + 1], None, ALU.mult, None,
                       ["X%d" % t, "rstd%d" % tc], [XN_NAMES[i % 2]])
                    for kc in range(8):
                        b = banks[kc // 2]
                        pv = ps[b][:].bitcast(BF16)
                        o0 = (kc % 2) * 512 + i * 128
                        tr(pv[:, o0:o0 + 128], xn[i % 2][:, kc * 128:(kc + 1) * 128],
                           [XN_NAMES[i % 2], "ident_b"], ["ps%d" % b], signal=(kc == 7))
                for kc in range(8):
                    b = banks[kc // 2]
                    pv = ps[b][:].bitcast(BF16)
                    o0 = (kc % 2) * 512
                    if kc % 2 == 0:
                        act(hT[:, kc, tc * 512:(tc + 1) * 512], pv[:, o0:o0 + 512], AF.Identity,
                            ["ps%d" % b, "scsh"], ["hT%d" % tc],
                            bias=scsh[:, off + 8 + kc: off + 9 + kc], scale=scsh[:, off + kc: off + kc + 1])
                    else:
                        ts(hT[:, kc, tc * 512:(tc + 1) * 512], pv[:, o0:o0 + 512],
                           scsh[:, off + kc: off + kc + 1], scsh[:, off + 8 + kc: off + 9 + kc], ALU.mult, ALU.add,
                           ["ps%d" % b, "scsh"], ["hT%d" % tc])

        def proj_F(wp, wn, rope, dests):
            tail = [None]
            for tc in range(4):
                cs = slice(tc * 512, (tc + 1) * 512)
                b = bank_main.next()
                for kc in range(8):
                    mm(ps[b][:], wp[:, kc, :], hT[:, kc, cs], kc == 0, kc == 7,
                       wn + ["hT%d" % tc], ["ps%d" % b], kc == 7)
                if tc == 2:
                    run_deferred(bank_aux)
                if not rope:
                    for (r0, nr, tile, d0, nm) in dests:
                        cp(tile[d0:d0 + nr, cs], ps[b][r0:r0 + nr, :], ["ps%d" % b], [nm])
                else:
                    pi = pt_ring.next()
                    act(PT[pi], ps[b][:], AF.Identity, ["ps%d" % b], ["pt%d" % pi])
                    if tail[0] is not None:
                        tail[0]()

                    def mk(b=b, pi=pi, cs=cs, tc=tc):
                        b2 = bank_aux.next()
                        mm(ps[b2][:], rmat_b[:], PT[pi], True, True, ["rmat_b", "pt%d" % pi], ["ps%d" % b2], True)
                        tt(T[0], ps[b][:], tabC[:, cs], ALU.mult, ["ps%d" % b, "tab1_%d" % tc], ["T0"])
                        tt(T[1], ps[b2][:], tabS[:, cs], ALU.mult, ["ps%d" % b2, "tab0_%d" % tc], ["T1"])
                        for (r0, nr, tile, d0, nm) in dests:
                            tt(tile[d0:d0 + nr, cs], T[0][r0:r0 + nr, :], T[1][r0:r0 + nr, :], ALU.add,
                               ["T0", "T1"], [nm])
                    tail[0] = mk
            if tail[0] is not None:
                tail[0]()

        def proj_V(wp, wn, N, evac, extra=None):
            for t in range(NT):
                b = bank_main.next()
                for kc in range(8):
                    mm(ps[b][:, 0:N], hT[:, kc, t * 128:(t + 1) * 128], wp[:, kc, 0:N], kc == 0, kc == 7,
                       wn + ["hT%d" % (t // 4)], ["ps%d" % b], kc == 7)
                if extra is not None:
                    wp2, n2 = extra
                    for kc in range(8):
                        mm(ps[b][:, N:N + n2], hT[:, kc, t * 128:(t + 1) * 128], wp2[:, kc, :], kc == 0, kc == 7,
                           wn + ["hT%d" % (t // 4)], ["ps%d" % b], kc == 7)
                evac(t, b)

        att_pend = []
        att_cb = {}
        cid_ctr = [0]

        def emit_pv(st):
            m, kt, nk, pi, c0, cid = st
            for (ab, lfn, rds) in m["pv"]:
                mm(ps[ab][:, c0:512], lfn(kt), PT[pi][:, c0:512], kt == 0, kt == nk - 1,
                   ["pt%d" % pi] + rds, ["ps%d" % ab], True)

        def drain_check():
            live = set(e[5] for e in att_pend)
            for cid in list(att_cb.keys()):
                if cid not in live:
                    att_cb.pop(cid)()

        def att_flush():
            while att_pend:
                emit_pv(att_pend.pop(0))
            drain_check()

        def attend_chunk(qc, maps, st_ring, cid):
            nk = 4 * qc + 4
            LAG = 3 if len(st_ring.items) >= 4 else 2
            nstep = [0]
            for kt in range(nk):
                for mi, m in enumerate(maps):
                    j = kt - 4 * qc
                    c0 = max(j, 0) * 128
                    b = st_ring.next()
                    diag = j >= 0
                    mm(ps[b][:, c0:512], m["k"][:, kt * 128:(kt + 1) * 128], m["q"][:, qc * 512 + c0:(qc + 1) * 512],
                       True, not diag, m["rn"], ["ps%d" % b], not diag)
                    if diag:
                        mm(ps[b][:, c0:c0 + 128], ident_b[:], negmask_b[:], False, True,
                           ["ident_b", "negmask_b"], ["ps%d" % b], True)
                    pi = pt_ring.next()
                    bias_ap, bias_names = m["bias"](kt)
                    act(PT[pi][:, c0:512], ps[b][:, c0:512], AF.Exp, ["ps%d" % b] + bias_names, ["pt%d" % pi],
                        bias=bias_ap, scale=0.125)
                    att_pend.append((m, kt, nk, pi, c0, cid))
                    while len(att_pend) > LAG:
                        emit_pv(att_pend.pop(0))
                        drain_check()
                    gstep[0] += 1
                    run_deferred(st_ring, force=False)

        ACC6 = [2, 3, 4, 5, 6, 7]
        diff_chunk_ctr = [0]
        st2 = Ring([0, 1])
        st4 = Ring([0, 1, 2, 3])

        def attend_diff(h):
            for qc in range(4):
                c = diff_chunk_ctr[0]
                diff_chunk_ctr[0] += 1
                A = [ACC6[(4 * c) % 6], ACC6[(4 * c + 2) % 6]]
                Sm = [ACC6[(4 * c + 1) % 6], ACC6[(4 * c + 3) % 6]]
                cid = cid_ctr[0]
                cid_ctr[0] += 1
                for m_ in range(2):
                    mp = dict(q=qk[m_][:, :], k=qk[2][:, :], rn=["qk%d" % m_, "qk2"],
                              bias=lambda kt: (None, []),
                              pv=[(A[m_], (lambda kt: Vd1[:, kt, :]), ["V"]),
                                  (Sm[m_], (lambda kt: ones_b[:]), ["ones_b"])])
                    attend_chunk(qc, [mp], st2, cid)

                def part1(A=A, Sm=Sm, h=h, qc=qc):
                    run_deferred(st2)
                    cs = slice(qc * 512, (qc + 1) * 512)
                    a0, s0, a1, s1 = A[0], Sm[0], A[1], Sm[1]
                    recip(T[0], ps[s0][:], ["ps%d" % s0], ["T0"])
                    tt(T[1], ps[a0][:], T[0], ALU.mult, ["ps%d" % a0, "T0"], ["T1"])
                    recip(T[0], ps[s1][:], ["ps%d" % s1], ["T0"])
                    tt(T[2], ps[a1][:], T[0], ALU.mult, ["ps%d" % a1, "T0"], ["T2"])
                    stt(T[3], T[2], neglam[:, 0:1], T[1], ALU.mult, ALU.add, ["T2", "T1", "neglam"], ["T3"])
                    sqv = T[2].bitcast(BF16)[:, 0:512]
                    tt(sqv, T[3], T[3], ALU.mult, ["T3", "T2"], ["T2"])

                    def part2(ring, sqv=sqv, h=h, qc=qc, cs=cs):
                        b = ring.next()
                        mm(ps[b][:], ones_b[:], sqv, True, True, ["ones_b", "T2"], ["ps%d" % b], True)
                        act(T[0], ps[b][:], AF.Ln, ["ps%d" % b], ["T0"], bias=EPS, scale=1.0 / 128)
                        act(T[1], T[0], AF.Exp, ["T0"], ["T1"], scale=-0.5)
                        stt(oT[:, h, cs], T[3], gcol[:, 0:1], T[1], ALU.mult, ALU.mult, ["T3", "T1", "gcol"],
                            ["o%d_%d" % (h, qc)])
                    deferred.append((gstep[0] + 8, part2))
                att_cb[cid] = part1
                drain_check()
            att_flush()

        def attend_single(pair, chunk, R, bias_fn):
            for hl in range(2):
                h = 2 * pair + hl
                for qc in range(4):
                    ab = 4 + (hl * 4 + qc) % 4
                    cid = cid_ctr[0]
                    cid_ctr[0] += 1
                    maps = [dict(q=qk[hl][0:R, :], k=qk[2 + hl][0:R, :], rn=["qk%d" % hl, "qk%d" % (2 + hl)],
                                 bias=(lambda kt, h_=h: bias_fn(h_, kt)),
                                 pv=[(ab, (lambda kt, hl_=hl: Vs[:, kt, hl_, :]), ["V"])])]
                    attend_chunk(qc, maps, st4, cid)

                    def fin(ab=ab, hl=hl, qc=qc):
                        cs = slice(qc * 512, (qc + 1) * 512)
                        P.op("dve", lambda e: e.reciprocal(out=T[0][64:128, :], in_=ps[ab][64:128, :]),
                             (), ["ps%d" % ab, "T0"])
                        d0 = hl * 64
                        tt(oT[d0:d0 + 64, chunk, cs], ps[ab][0:64, :], T[0][64:128, :], ALU.mult,
                           ["ps%d" % ab, "T0"], ["o%d_%d" % (chunk, qc)])
                    att_cb[cid] = fin
                    drain_check()
            att_flush()

        def branch_diff(l, h):
            claim(N_QK + N_V + N_PT)
            wqk, nqk = wload([w_inT_d[l, h], w_inT_d[l, 4 + h]])
            wvv, nv = wload([w_inT_d[l, 8 + h]])
            if h == 0:
                memset(qk[0][64:128, :], 0.0, ["qk0"])
                memset(qk[1][0:64, :], 0.0, ["qk1"])
            proj_F(wqk[0], nqk, True, [(0, 64, qk[0], 0, "qk0"), (64, 64, qk[1], 64, "qk1")])
            proj_F(wqk[1], nqk, True, [(0, 128, qk[2], 0, "qk2")])

            def evac(t, b):
                cp(Vd1[:, t, :], ps[b][:, 0:128], ["ps%d" % b], ["V"])
            proj_V(wvv[0], nv, 128, evac)
            if h == 0:
                dump("qd0", qk[0], ["qk0"])
                dump("kd0", qk[2], ["qk2"])
            attend_diff(h)

        def branch_fox(l, pair):
            claim(N_QK + N_V + N_PT)
            wqk, nqk = wload([w_inT_d[l, 12 + pair], w_inT_d[l, 14 + pair]])
            pieces = [w_inT_d[l, 16 + pair]]
            if pair == 0:
                pieces.append(w_fgt_d[l].rearrange("(k p) n -> p k n", p=128))
            wvv, nv = wload(pieces)
            proj_F(wqk[0], nqk, False, [(0, 64, qk[0], 0, "qk0"), (64, 64, qk[1], 0, "qk1")])
            proj_F(wqk[1], nqk, False, [(0, 64, qk[2], 0, "qk2"), (64, 64, qk[3], 0, "qk3")])
            memset(Vs[:, :, :, 64:128], 1.0, ["V"], eng="pool")
            for i in range(4):
                memset(qk[i][64:128, :], 0.0, ["qk%d" % i], eng="pool")
            memset(qk[2][64:65, :], 1.0, ["qk2"], eng="pool")
            memset(qk[3][64:65, :], 1.0, ["qk3"], eng="pool")

            def evac(t, b):
                cp(Vs[:, t, :, 0:64], ps[b][:, 0:128].rearrange("p (j c) -> p j c", c=64), ["ps%d" % b], ["V"])
                if pair == 0:
                    tt(zb[:, :, t], ps[b][:, 128:132], bfrep[:], ALU.add, ["ps%d" % b, "bfrep"], ["zb"])
            proj_V(wvv[0], nv, 128, evac, extra=((wvv[1], 4) if pair == 0 else None))
            if pair == 0:
                act(lf[:], zb[:], AF.Exp, ["zb"], ["lf"], scale=-1.0)
                act(lf[:], lf[:], AF.Ln, ["lf"], ["lf"], bias=1.0)
                b1 = bank_aux.next()
                lf2 = lf[:].rearrange("p h t -> p (h t)")
                mm(ps[b1][:, 0:64], U_f[:], lf2, True, True, ["U_f", "lf"], ["ps%d" % b1], True)
                b2 = bank_aux.next()
                mm(ps[b2][:, 0:64], ones_f[:], lf2, True, True, ["ones_f", "lf"], ["ps%d" % b2], True)
                cp(tots[:].rearrange("p h t -> p (h t)"), ps[b2][:, 0:64], ["ps%d" % b2], ["tots"])
                memset(offs[:, :, 0:1], 0.0, ["offs"])
                for i in range(1, NT):
                    tt(offs[:, :, i:i + 1], offs[:, :, i - 1:i], tots[:, :, i - 1:i], ALU.add, ["offs", "tots"], ["offs"])
                tt(Lcum[:].rearrange("p h t -> p (h t)"), ps[b1][:, 0:64], offs[:].rearrange("p h t -> p (h t)"),
                   ALU.add, ["ps%d" % b1, "offs"], ["Lcum"])
                ts(r8[:], Lcum[:], -8.0, None, ALU.mult, None, ["Lcum"], ["r8"])
                dump("Lcum", Lcum[:], ["Lcum"])
            for hl in range(2):
                h = 2 * pair + hl
                for g in range(4):
                    b = bank_aux.next()
                    for i in range(4):
                        t = 4 * g + i
                        mm(ps[b][0:1, i * 128:(i + 1) * 128], r8[:, h, t:t + 1], ident_b[:], True, True,
                           ["r8", "ident_b"], ["ps%d" % b], i == 3)
                    act(qk[hl][64:65, g * 512:(g + 1) * 512], ps[b][0:1, :], AF.Identity, ["ps%d" % b], ["qk%d" % hl])
            if pair == 0:
                dump("qf0", qk[0], ["qk0"])
                dump("kf0", qk[2], ["qk2"])

            def bias_fn(h, kt):
                return Lcum[:, h, kt:kt + 1], ["Lcum"]
            attend_single(pair, 4 + pair, 128, bias_fn)

        def branch_moba(l, pair):
            claim(N_QK + N_V + N_PT)
            wqk, nqk = wload([w_inT_d[l, 18 + pair], w_inT_d[l, 20 + pair]])
            wvv, nv = wload([w_inT_d[l, 22 + pair]])
            proj_F(wqk[0], nqk, True, [(0, 64, qk[0], 0, "qk0"), (64, 64, qk[1], 0, "qk1")])
            proj_F(wqk[1], nqk, True, [(0, 64, qk[2], 0, "qk2"), (64, 64, qk[3], 0, "qk3")])
            memset(Vs[:, :, :, 64:128], 1.0, ["V"], eng="pool")
            for i in range(4):
                memset(qk[i][64:128, :], 0.0, ["qk%d" % i], eng="pool")
            for hl in range(2):
                dma("pool", qk[2 + hl][64:72, :], onehot_d, (), ["qk%d" % (2 + hl)])

            def evac(t, b):
                cp(Vs[:, t, :, 0:64], ps[b][:, 0:128].rearrange("p (j c) -> p j c", c=64), ["ps%d" % b], ["V"])
            proj_V(wvv[0], nv, 128, evac)
            bg = bank_aux.next()
            for hl in range(2):
                P.op("dve", (lambda hl_: lambda e: e.tensor_reduce(
                    out=kms[0:64, hl_, :], in_=qk[2 + hl_][0:64, :].rearrange("p (n k) -> p n k", k=256),
                    axis=AX.X, op=ALU.add))(hl), ["qk%d" % (2 + hl)], ["kms%d" % hl])
                ts(kmT[0:64, hl, :], kms[0:64, hl, :], 1.0 / 256, None, ALU.mult, None, ["kms%d" % hl], ["kmT%d" % hl])
                for i in range(8):
                    qt = 8 + i
                    c = (hl * 8 + i) * 8
                    mm(ps[bg][:, c:c + 8], qk[hl][0:64, qt * 128:(qt + 1) * 128], kmT[0:64, hl, :], True, True,
                       ["qk%d" % hl, "kmT%d" % hl], ["ps%d" % bg], (hl == 1 and i == 7))
            memset(gm[:], -1e30, ["gm"])
            for hl in range(2):
                for i in range(8):
                    own = (8 + i) // 2
                    c = (hl * 8 + i) * 8
                    cp(gm[:, hl, i, 0:own], ps[bg][:, c:c + own], ["ps%d" % bg, "gm"], ["gm"])
            for hl in range(2):
                for i in range(8):
                    own = (8 + i) // 2
                    P.op("dve", (lambda hl_, i_: lambda e: e.max(out=mx8[:], in_=gm[:, hl_, i_, :]))(hl, i),
                         ["gm"], ["mx8"])
                    ts(selb[:, hl, i, 0:own], gm[:, hl, i, 0:own], mx8[:, 2:3], NEGBIG, ALU.is_lt, ALU.mult,
                       ["gm", "mx8"], ["selb"])
            for i in range(8):
                qt = 8 + i
                b = bank_aux.next()
                for hl in range(2):
                    mm(ps[b][0:8, hl * 128:(hl + 1) * 128], selb[:, hl, i, :], ident_b[:], True, True,
                       ["selb", "ident_b"], ["ps%d" % b], hl == 1)
                for hl in range(2):
                    act(qk[hl][64:72, qt * 128:(qt + 1) * 128], ps[b][0:8, hl * 128:(hl + 1) * 128], AF.Identity,
                        ["ps%d" % b], ["qk%d" % hl])
            if pair == 0:
                dump("qm0", qk[0], ["qk0"])
                dump("km0", qk[2], ["qk2"])

            def bias_fn(h, kt):
                return None, []
            attend_single(pair, 6 + pair, 128, bias_fn)

        def update_x(half, nslab):
            unames = ["ssu%d" % i for i in range(8)]
            P.op("dve", lambda e: e.reduce_sum(out=ssu[:, 0:8], in_=ssq[:, :, 0:nslab], axis=AX.X),
                 ["ssq%d" % t for t in range(8)], unames)
            rmsn(ssu[:], lnvu[:], rstdu[:], unames, "lnvu", "rstdu", 1.0 / D)
            for t in range(8):
                tg = half * 8 + t
                stt(X[:, tg, :], ybuf[:, t, :], rstdu[:, t:t + 1], X[:, tg, :], ALU.mult, ALU.add,
                    ["yb%d" % t, "rstdu", "X%d" % tg], ["X%d" % tg])

        def evac_y(b, t, sl, ncol):
            cs = slice(sl * ncol, (sl + 1) * ncol)
            ji = junk_ring.next()
            act(junk2[:, ji, 0:ncol], ps[b][:, 0:ncol], AF.Square, ["ps%d" % b], ["ssq%d" % t, "junk%d" % ji],
                accum=ssq[:, t, sl:sl + 1])
            tt(ybuf[:, t, cs], ps[b][:, 0:ncol], G[:, cs], ALU.mult, ["ps%d" % b] + N_G, ["yb%d" % t])

        def merge_half(l, half):
            claim(N_MG)
            for fc in range(8):
                wa, na = wload([w_inT_d[l, 24 + fc], w_inT_d[l, 32 + fc]])
                wb, nb = wload([w_inT_d[l, 40 + fc], w_brT_d[0][l, fc], w_brT_d[1][l, fc], w_brT_d[2][l, fc]])
                for tcl in range(2):
                    tc = half * 2 + tcl
                    cs = slice(tc * 512, (tc + 1) * 512)
                    csl = slice(tcl * 512, (tcl + 1) * 512)
                    for br in range(3):
                        bgk = bank_main.next()
                        wsrc, wnm = (wa[0], na) if br == 0 else ((wa[1], na) if br == 1 else (wb[0], nb))
                        for kc in range(8):
                            mm(ps[bgk][:], wsrc[:, kc, :], hT[:, kc, cs], kc == 0, kc == 7,
                               wnm + ["hT%d" % tc], ["ps%d" % bgk], kc == 7)
                        by = bank_aux.next()
                        rng_ = (range(0, 4), range(4, 6), range(6, 8))[br]
                        kcs = [(wb[1 + br][:, kc - rng_[0], :], nb, kc) for kc in rng_]
                        for i, (wap, wnm2, oc) in enumerate(kcs):
                            mm(ps[by][:], wap, oT[:, oc, cs], i == 0, i == len(kcs) - 1,
                               wnm2 + ["o%d_%d" % (oc, tc)], ["ps%d" % by], i == len(kcs) - 1)
                        sg = T[br % 2]
                        sgn = "T%d" % (br % 2)
                        act(sg, ps[bgk][:], AF.Sigmoid, ["ps%d" % bgk], [sgn])
                        if br == 0:
                            tt(T[2], ps[by][:], sg, ALU.mult, ["ps%d" % by, sgn], ["T2"])
                        elif br == 1:
                            tt(T[3], ps[by][:], sg, ALU.mult, ["ps%d" % by, sgn], ["T3"])
                            tt(T[2], T[2], T[3], ALU.add, ["T2", "T3"], ["T2"])
                        else:
                            tt(T[3], ps[by][:], sg, ALU.mult, ["ps%d" % by, sgn], ["T3"])
                            tt(mg[:, fc, csl], T[2], T[3], ALU.add, ["T2", "T3"], ["mg%d" % tcl])
            if half == 0:
                dump("mg", R12[:, 16384:24576], N_MG)

        def wout_half(l, half):
            claim(N_YB)
            memset(ssq[:], 0.0, ["ssq%d" % t for t in range(8)])
            for sl in range(4):
                wvl, wn = wload([w_outT_d[l, sl]])
                wv = wvl[0]
                for t in range(8):
                    b = bank_main.next()
                    for kc in range(8):
                        mm(ps[b][:, 0:256], mg[:, kc, t * 128:(t + 1) * 128], wv[:, kc, :], kc == 0, kc == 7,
                           wn + ["mg%d" % (t // 4)], ["ps%d" % b], kc == 7)
                    evac_y(b, t, sl, 256)

        def gate_up(l, half):
            claim(N_AC)
            for j in range(NJ):
                wv, wn = wload([w_guT_d[l, j], w_guT_d[l, NJ + j]])
                for tcl in range(2):
                    tc = half * 2 + tcl
                    cs = slice(tc * 512, (tc + 1) * 512)
                    ba = bank_main.next()
                    for kc in range(8):
                        mm(ps[ba][:], wv[0][:, kc, :], hT[:, kc, cs], kc == 0, kc == 7, wn + ["hT%d" % tc],
                           ["ps%d" % ba], kc == 7)
                    bb = bank_aux.next()
                    for kc in range(8):
                        mm(ps[bb][:], wv[1][:, kc, :], hT[:, kc, cs], kc == 0, kc == 7, wn + ["hT%d" % tc],
                           ["ps%d" % bb], kc == 7)
                    sg = T[(j * 2 + tcl) % 2]
                    sgn = "T%d" % ((j * 2 + tcl) % 2)
                    act(sg, ps[ba][:], AF.Silu, ["ps%d" % ba], [sgn])
                    tt(actT[:, j, tcl * 512:(tcl + 1) * 512], ps[bb][:], sg, ALU.mult, ["ps%d" % bb, sgn], ["ac%d" % tcl])

        def down(l, half):
            claim(N_YB)
            memset(ssq[:], 0.0, ["ssq%d" % t for t in range(8)])
            for fcol in range(8):
                wvl, wn = wload([w_dnT_d[l, fcol]])
                wv = wvl[0]
                for t in range(8):
                    b = bank_main.next()
                    for kc in range(NJ):
                        mm(ps[b][:, 0:128], actT[:, kc, t * 128:(t + 1) * 128], wv[:, kc, :], kc == 0, kc == NJ - 1,
                           wn + ["ac%d" % (t // 4)], ["ps%d" % b], kc == NJ - 1)
                    evac_y(b, t, fcol, 128)
            update_x(half, 8)

        def ffn(l, s):
            stage_a(list(range(0, 8)), 16)
            gate_up(l, 0)
            stage_a(list(range(8, 16)), 16)
            down(l, 0)
            if l == layers[-1]:
                finish_tiles(s, range(0, 8))
            gate_up(l, 1)
            down(l, 1)
            if l == layers[-1]:
                finish_tiles(s, range(8, 16))

        def finish_tiles(s, tiles):
            for t in tiles:
                dma("sp", out_d[s, t * 128:(t + 1) * 128, :], X[:, t, :], ["X%d" % t], ())
            if s + 1 < n_seq:
                for t in tiles:
                    dma("sp", X[:, t, :], x_d[s + 1, t * 128:(t + 1) * 128, :], (), ["X%d" % t])

        def program():
            for s in range(n_seq):
                if s == 0:
                    for t in range(NT):
                        dma("sp", X[:, t, :], x_d[s, t * 128:(t + 1) * 128, :], (), ["X%d" % t])
                for kc in range(8):
                    cp(cbc[:, kc, :], ca[:, kc * 2 + s: kc * 2 + s + 1].to_broadcast([128, 128]), ["ca"], ["cbc"])
                rope_tables(s)
                for l in layers:
                    adaln_cols(s, l)
                    adaln_G(s, l, 0)
                    stage_a(list(range(NT)), 0)
                    dump("hT", hT[:], ["hT%d" % i for i in range(4)])
                    if stop == "a":
                        return
                    claim(N_O)
                    for h in range(4):
                        branch_diff(l, h)
                        if stop == "d0":
                            dump("oT2", R12[:, 0:4096], N_O)
                            return
                    for pair in range(2):
                        branch_fox(l, pair)
                    for pair in range(2):
                        branch_moba(l, pair)
                    dump("oT", R12[:, 0:16384], N_O)
                    if stop == "attn":
                        return
                    for half in range(2):
                        merge_half(l, half)
                        wout_half(l, half)
                        update_x(half, 4)
                    dump("xmix", X[:], ["X%d" % t for t in range(NT)])
                    if stop == "mix":
                        return
                    adaln_G(s, l, 1)
                    ffn(l, s)

        def lam_init(l):
            return 0.8 - 0.6 * math.exp(-0.3 * l)

        program()
        P.wait_tokens("sp", P.all_tokens())
        P.replay()
    nc._dbg_names = list(dbg_d.keys())
    nc._n_instr = {e: len(P.streams[e]) for e in ENGS}
    return nc


def host_consts():
    inv_freq = 1.0 / (10000.0 ** (np.arange(0, 64, 2, dtype=np.float32) / 64.0))
    invf = np.tile(inv_freq.astype(np.float32), 4).reshape(128, 1)
    rmat = np.zeros((128, 128), np.float32)
    for m in range(128):
        if m % 64 < 32:
            rmat[m + 32, m] = -1.0
        else:
            rmat[m - 32, m] = 1.0
    ident = np.eye(128, dtype=np.float32)
    mask01 = (np.arange(128)[:, None] <= np.arange(128)[None, :]).astype(np.float32)
    onehot = (np.arange(S)[None, :] // 256 == np.arange(8)[:, None]).astype(np.float32)
    negmask = np.where(np.arange(128)[:, None] > np.arange(128)[None, :], -30000.0, 0.0).astype(np.float32)
    return dict(invf=invf, rmat=rmat, ident=ident, mask01=mask01, onehot=onehot, negmask=negmask)


def make_in_maps(inputs, n_cores=8, n_seq=2):
    f = lambda a: np.ascontiguousarray(np.asarray(a))
    consts = host_consts()
    b_ada = f(inputs["b_ada"])
    badaT = np.ascontiguousarray(b_ada.reshape(2, 48, 128).transpose(0, 2, 1))
    gpm = f(inputs["g_pre_mix"]).reshape(2, 8, 128).transpose(0, 2, 1)
    gpf = f(inputs["g_pre_ffn"]).reshape(2, 8, 128).transpose(0, 2, 1)
    gsl = f(inputs["g_subln"]).reshape(2, 128, 1)
    gcols = np.ascontiguousarray(np.concatenate([gpm, gpf, gsl], axis=2))
    def tile_w(w, ncol):
        w = f(w)
        L, K, N = w.shape
        return np.ascontiguousarray(w.reshape(L, K // 128, 128, N // ncol, ncol).transpose(0, 3, 2, 1, 4))
    w_in = f(inputs["w_in"])
    shared = dict(
        w_adaT=tile_w(inputs["w_ada"], 128), b_ada=b_ada, badaT=badaT, gcols=gcols,
        g_post_mix=f(inputs["g_post_mix"]), g_post_ffn=f(inputs["g_post_ffn"]),
        w_inT=tile_w(w_in[:, :, 0:6144], 128), w_fgt=np.ascontiguousarray(w_in[:, :, 6144:6148]),
        b_fgt=f(inputs["b_fgt"]),
        lam_q1=f(inputs["lam_q1"]), lam_k1=f(inputs["lam_k1"]), lam_q2=f(inputs["lam_q2"]), lam_k2=f(inputs["lam_k2"]),
        w_braT=tile_w(inputs["w_br_a"], 128), w_brbT=tile_w(inputs["w_br_b"], 128), w_brcT=tile_w(inputs["w_br_c"], 128),
        w_outT=tile_w(inputs["w_out"], 256), w_guT=tile_w(inputs["w_gate_up"], 128),
        w_dnT=tile_w(inputs["w_down"], 128), **consts)
    x = f(inputs["x"])
    c = f(inputs["c"])
    pos = f(inputs["positions"]).astype(np.int32)
    maps = []
    for i in range(n_cores):
        bs = slice(i * n_seq, (i + 1) * n_seq)
        cc = c[bs]
        cT = np.zeros((128, 16), np.float32)
        cT[:, : 8 * 2] = 0
        for b in range(n_seq):
            cT[:, b::2][:, :8] = cc[b].reshape(8, 128).T
        m = dict(shared)
        m.update(x=np.ascontiguousarray(x[bs]), cT=cT, pos=np.ascontiguousarray(pos[bs]))
        maps.append(m)
    return maps


def kernel(**inputs):
    nc = build(n_seq=2, layers=(0, 1))
    maps = make_in_maps(inputs, 8, 2)
    res = run_bass_kernel_spmd(nc, maps, core_ids=list(range(8)))
    return np.concatenate([r["out"] for r in res.results], axis=0).astype(np.float32)
```
